# Optimizing a Trainium2 kernel written in Bass

```python
import math, functools
import jax, jax.numpy as jnp
from jax import lax
import numpy as np


D_MODEL = 1024
BATCH = 16
SEQ = 2048
DEPTH = 4

GRID_W = 64
CTX_LEN = 256
MIX_W = D_MODEL
N_MIXERS = 4
BR_W = MIX_W // N_MIXERS
HEAD_DIM = 64
RET_HEADS = BR_W // HEAD_DIM
SG_HEADS = BR_W // HEAD_DIM
GDN_HEADS = BR_W // HEAD_DIM
RET_CHUNK = 128
SG_CHUNK = 128
GDN_CHUNK = 64
CONV_W = 3
ROPE_BASE = 10000.0
EPS = 1e-6
SEGMENTS = (BR_W,) * 15 + (GDN_HEADS,) * 4
IN_W = 15 * BR_W + 4 * GDN_HEADS

kernel_name = 'hybrid_parallel_mixer_dit_block'


def rms_norm(x, g):
    xf = x.astype(jnp.float32)
    y = xf * lax.rsqrt(jnp.mean(xf * xf, axis=-1, keepdims=True) + EPS)
    return (y * g.astype(jnp.float32)).astype(x.dtype)


def layer_norm_plain(x):
    xf = x.astype(jnp.float32)
    mu = jnp.mean(xf, axis=-1, keepdims=True)
    var = jnp.mean(jnp.square(xf - mu), axis=-1, keepdims=True)
    return ((xf - mu) * lax.rsqrt(var + EPS)).astype(x.dtype)


def l2_normalize(x):
    return x * lax.rsqrt(jnp.sum(x * x, axis=-1, keepdims=True) + EPS)


def dwconv_centred(x, w):
    k_w = w.shape[0]
    t = x.shape[1]
    xp = jnp.pad(x, ((0, 0), (k_w // 2, k_w // 2), (0, 0)))
    out = xp[:, 0:t] * w[0]
    for i in range(1, k_w):
        out = out + xp[:, i:i + t] * w[i]
    return out


def grid_rotary(x, row, col):
    half = x.shape[-1] // 2
    nf = half // 2
    inv = ROPE_BASE ** (-jnp.arange(nf, dtype=jnp.float32) / nf)

    def rot(xs, pos):
        ang = pos.astype(jnp.float32)[:, None] * inv
        cos, sin = jnp.cos(ang)[None, :, None, :], jnp.sin(ang)[None, :, None, :]
        x1, x2 = xs[..., :nf], xs[..., nf:]
        return jnp.concatenate([x1 * cos - x2 * sin, x1 * sin + x2 * cos], axis=-1)

    return jnp.concatenate([rot(x[..., :half], row), rot(x[..., half:], col)], axis=-1)


def split_proj(p):
    cuts = [int(i) for i in np.cumsum(SEGMENTS)[:-1]]
    return jnp.split(p, cuts, axis=-1)


def identity(a):
    return a


flip_time = functools.partial(jnp.flip, axis=1)


def retention_scan(q, k, v, log_gamma, s0, with_output):
    bsz, t, h, dk = k.shape
    n = t // RET_CHUNK

    def chunks(a):
        return a.reshape(bsz, n, RET_CHUNK, h, a.shape[-1]).transpose(1, 0, 3, 2, 4)

    pos = jnp.arange(RET_CHUNK, dtype=jnp.float32)
    lg = log_gamma[:, None]
    k_dec = jnp.exp((RET_CHUNK - 1.0 - pos) * lg)[..., None]
    c_dec = jnp.exp(RET_CHUNK * log_gamma)[:, None, None]
    kc, vc = chunks(k * dk ** -0.5), chunks(v)

    def update(s, ki, vi):
        return s * c_dec + jnp.einsum('bhcd,bhce->bhde', ki * k_dec, vi)

    if not with_output:
        s_fin, _ = lax.scan(lambda s, kv: (update(s, kv[0], kv[1]), None), s0, (kc, vc))
        return None, s_fin
    diff = pos[:, None] - pos[None, :]
    intra = jnp.exp(jnp.where(diff >= 0, diff * lg[..., None], -jnp.inf))
    q_dec = jnp.exp((pos + 1.0) * lg)[..., None]

    def step(s, inp):
        qi, ki, vi = inp
        scores = jnp.einsum('bhid,bhjd->bhij', qi, ki) * intra
        o = jnp.einsum('bhij,bhje->bhie', scores, vi) + jnp.einsum('bhid,bhde->bhie', qi, s) * q_dec
        return update(s, ki, vi), o

    s_fin, o = lax.scan(step, s0, (chunks(q), kc, vc))
    return o.transpose(1, 0, 3, 2, 4).reshape(bsz, t, h, -1), s_fin


def gated_delta_scan(q, k, v, g, beta, s0, with_output):
    bsz, t, h, dk = k.shape
    dv = v.shape[-1]
    n = t // GDN_CHUNK

    def chunks(a):
        return a.reshape(bsz, n, GDN_CHUNK, h, -1).transpose(1, 0, 3, 2, 4)

    kc, vc = chunks(k), chunks(v)
    bc = chunks(beta[..., None])
    gc = jnp.cumsum(chunks(g[..., None])[..., 0], axis=-1)
    idx = jnp.arange(GDN_CHUNK)
    diff = gc[..., :, None] - gc[..., None, :]
    decay = jnp.exp(jnp.where(idx[:, None] >= idx[None, :], diff, -jnp.inf))
    kb = kc * bc
    lower = jnp.where(idx[:, None] > idx[None, :],
                      jnp.einsum('nbhid,nbhjd->nbhij', kb, kc) * decay, 0.0)
    rhs = jnp.concatenate([vc * bc, kb * jnp.exp(gc)[..., None]], axis=-1)
    sol = lax.linalg.triangular_solve(lower, rhs, left_side=True, lower=True, unit_diagonal=True)
    uc, wc = sol[..., :dv], sol[..., dv:]
    g_last = gc[..., -1:]
    k_tail = kc * jnp.exp(g_last - gc)[..., None]
    c_dec = jnp.exp(g_last)[..., None]

    def update(s, ui, wi, kti, cdi):
        v_new = ui - jnp.einsum('bhck,bhkv->bhcv', wi, s)
        return s * cdi + jnp.einsum('bhck,bhcv->bhkv', kti, v_new), v_new

    if not with_output:
        s_fin, _ = lax.scan(lambda s, xs: (update(s, *xs)[0], None), s0, (uc, wc, k_tail, c_dec))
        return None, s_fin
    qc = chunks(q * dk ** -0.5)
    qd = qc * jnp.exp(gc)[..., None]

    def step(s, inp):
        qi, qdi, ki, di, ui, wi, kti, cdi = inp
        s_new, v_new = update(s, ui, wi, kti, cdi)
        intra = jnp.einsum('bhik,bhjk->bhij', qi, ki) * di
        o = jnp.einsum('bhck,bhkv->bhcv', qdi, s) + jnp.einsum('bhij,bhjv->bhiv', intra, v_new)
        return s_new, o

    s_fin, o = lax.scan(step, s0, (qc, qd, kc, decay, uc, wc, k_tail, c_dec))
    return o.transpose(1, 0, 3, 2, 4).reshape(bsz, t, h, dv), s_fin


def retention_mixer(lat, ctx, row, col, norm_g, ctx_out):
    def heads(a):
        return a.astype(jnp.float32).reshape(a.shape[0], a.shape[1], RET_HEADS, HEAD_DIM)

    ql, kl, vl = grid_rotary(heads(lat[0]), row, col), grid_rotary(heads(lat[1]), row, col), heads(lat[2])
    qc, kc, vc = heads(ctx[0]), heads(ctx[1]), heads(ctx[2])
    log_gamma = jnp.log(1.0 - 2.0 ** (-5.0 - jnp.arange(RET_HEADS, dtype=jnp.float32)))
    s0 = jnp.zeros((ql.shape[0], RET_HEADS, HEAD_DIM, HEAD_DIM), jnp.float32)
    o_lat, o_ctx = 0.0, 0.0
    for d in (identity, flip_time):
        oc, sc = retention_scan(d(qc), d(kc), d(vc), log_gamma, s0, ctx_out)
        ol, _ = retention_scan(d(ql), d(kl), d(vl), log_gamma, sc, True)
        o_lat = o_lat + d(ol)
        if ctx_out:
            o_ctx = o_ctx + d(oc)

    def finish(o, z):
        mu = jnp.mean(o, axis=-1, keepdims=True)
        var = jnp.mean(jnp.square(o - mu), axis=-1, keepdims=True)
        y = ((o - mu) * lax.rsqrt(var + EPS)).reshape(o.shape[0], o.shape[1], BR_W) * norm_g
        return (y * jax.nn.silu(z.astype(jnp.float32))).astype(z.dtype)

    return finish(o_lat, lat[3]), (finish(o_ctx, ctx[3]) if ctx_out else None)


def spatial_gating(u, v, z, w_s, b_s):
    bsz, t, _ = u.shape
    n = t // SG_CHUNK
    u = jax.nn.gelu(u)
    v = layer_norm_plain(jax.nn.gelu(v)).reshape(bsz, n, SG_CHUNK, SG_HEADS, HEAD_DIM)
    s = jnp.einsum('hij,bnjhd->bnihd', w_s, v) + b_s.T[:, :, None]
    return u * s.reshape(bsz, t, BR_W) * jax.nn.silu(z)


def short_conv(b, c, h, z, w):
    return b * dwconv_centred(c * h, w) * jax.nn.silu(z)


def gdn_mixer(lat, ctx, conv_w, a_log, dt_bias, norm_g, ctx_out):
    def prep(parts):
        q, k, v, _, a_f, a_b, b_f, b_b = parts
        qkv = jax.nn.silu(dwconv_centred(jnp.concatenate([q, k, v], axis=-1), conv_w)).astype(jnp.float32)
        bsz, t, _ = qkv.shape
        q, k, v = [a.reshape(bsz, t, GDN_HEADS, HEAD_DIM) for a in jnp.split(qkv, 3, axis=-1)]
        g = [-jnp.exp(a_log[i]) * jax.nn.softplus(a.astype(jnp.float32) + dt_bias[i])
             for i, a in enumerate((a_f, a_b))]
        beta = [jax.nn.sigmoid(b.astype(jnp.float32)) for b in (b_f, b_b)]
        return l2_normalize(q), l2_normalize(k), v, g, beta

    ql, kl, vl, gl, bl = prep(lat)
    qc, kc, vc, gcx, bcx = prep(ctx)
    s0 = jnp.zeros((ql.shape[0], GDN_HEADS, HEAD_DIM, HEAD_DIM), jnp.float32)
    o_lat, o_ctx = 0.0, 0.0
    for i, d in enumerate((identity, flip_time)):
        oc, sc = gated_delta_scan(d(qc), d(kc), d(vc), d(gcx[i]), d(bcx[i]), s0, ctx_out)
        ol, _ = gated_delta_scan(d(ql), d(kl), d(vl), d(gl[i]), d(bl[i]), sc, True)
        o_lat = o_lat + d(ol)
        if ctx_out:
            o_ctx = o_ctx + d(oc)

    def finish(o, z):
        y = o * lax.rsqrt(jnp.mean(o * o, axis=-1, keepdims=True) + EPS) * norm_g
        y = y.reshape(o.shape[0], o.shape[1], BR_W)
        return (y * jax.nn.silu(z.astype(jnp.float32))).astype(z.dtype)

    return finish(o_lat, lat[3]), (finish(o_ctx, ctx[3]) if ctx_out else None)


def token_mixers(p_lat, p_ctx, row, col, ret_norm_g, sg_w, sg_b, sc_conv_w,
                 gdn_conv_w, gdn_a_log, gdn_dt_bias, gdn_norm_g, ctx_out):
    lat, cx = split_proj(p_lat), split_proj(p_ctx)
    a_lat, a_ctx = retention_mixer(lat[0:4], cx[0:4], row, col, ret_norm_g, ctx_out)
    d_lat, d_ctx = gdn_mixer(lat[11:19], cx[11:19], gdn_conv_w, gdn_a_log, gdn_dt_bias, gdn_norm_g, ctx_out)
    y_lat = jnp.concatenate([a_lat, spatial_gating(*lat[4:7], sg_w, sg_b),
                             short_conv(*lat[7:11], sc_conv_w), d_lat], axis=-1).astype(p_lat.dtype)
    if not ctx_out:
        return y_lat, None
    y_ctx = jnp.concatenate([a_ctx, spatial_gating(*cx[4:7], sg_w, sg_b),
                             short_conv(*cx[7:11], sc_conv_w), d_ctx], axis=-1).astype(p_ctx.dtype)
    return y_lat, y_ctx


def setup_inputs(seed: int = 0) -> dict:
    key = jax.random.key(seed)
    ks = jax.random.split(key, 18)

    def nrm(k, shape, s):
        return jax.random.normal(k, shape, jnp.float32) * s

    dt = jnp.exp(jax.random.uniform(ks[16], (DEPTH, 2, GDN_HEADS), jnp.float32,
                                    minval=math.log(1e-3), maxval=math.log(1e-1)))
    return {
        'x': nrm(ks[0], (BATCH, SEQ, D_MODEL), 1.0),
        'c': nrm(ks[1], (BATCH, D_MODEL), 1.0),
        'ctx': nrm(ks[2], (BATCH, CTX_LEN, D_MODEL), 1.0),
        'c_ctx': nrm(ks[3], (D_MODEL,), 1.0),
        'w_mod': nrm(ks[4], (DEPTH, D_MODEL, 3 * D_MODEL), 0.5 * D_MODEL ** -0.5),
        'b_mod': nrm(ks[5], (DEPTH, 3 * D_MODEL), 0.02),
        'g_pre': 1.0 + nrm(ks[6], (DEPTH, D_MODEL), 0.02),
        'g_post': 1.0 + nrm(ks[7], (DEPTH, D_MODEL), 0.02),
        'w_in': nrm(ks[8], (DEPTH, D_MODEL, IN_W), D_MODEL ** -0.5),
        'w_out': nrm(ks[9], (DEPTH, MIX_W, D_MODEL), MIX_W ** -0.5),
        'ret_norm_g': 1.0 + nrm(ks[10], (DEPTH, BR_W), 0.02),
        'sg_w': nrm(ks[11], (DEPTH, SG_HEADS, SG_CHUNK, SG_CHUNK), SG_CHUNK ** -0.5),
        'sg_b': 1.0 + nrm(ks[12], (DEPTH, SG_HEADS, SG_CHUNK), 0.1),
        'sc_conv_w': nrm(ks[13], (DEPTH, CONV_W, BR_W), CONV_W ** -0.5),
        'gdn_conv_w': nrm(ks[14], (DEPTH, CONV_W, 3 * BR_W), CONV_W ** -0.5),
        'gdn_a_log': jnp.log(jax.random.uniform(ks[15], (DEPTH, 2, GDN_HEADS), jnp.float32, minval=1.0, maxval=16.0)),
        'gdn_dt_bias': dt + jnp.log(-jnp.expm1(-dt)),
        'gdn_norm_g': 1.0 + nrm(ks[17], (DEPTH, HEAD_DIM), 0.02),
    }


def reference(x, c, ctx, c_ctx, w_mod, b_mod, g_pre, g_post, w_in, w_out, ret_norm_g, sg_w, sg_b,
              sc_conv_w, gdn_conv_w, gdn_a_log, gdn_dt_bias, gdn_norm_g):
    rows = x.shape[1] // GRID_W
    row = jnp.repeat(jnp.arange(rows), GRID_W)
    col = jnp.tile(jnp.arange(GRID_W), rows)
    silu_c = jax.nn.silu(c)
    silu_cc = jax.nn.silu(c_ctx)
    for l in range(DEPTH):
        ctx_out = l < DEPTH - 1
        shift, scale, gate = jnp.split(silu_c @ w_mod[l] + b_mod[l], 3, axis=-1)
        shift_c, scale_c, gate_c = jnp.split(silu_cc @ w_mod[l] + b_mod[l], 3, axis=-1)
        h = rms_norm(x, g_pre[l]) * (1.0 + scale[:, None]) + shift[:, None]
        hc = rms_norm(ctx, g_pre[l]) * (1.0 + scale_c) + shift_c
        y, yc = token_mixers(h @ w_in[l], hc @ w_in[l], row, col, ret_norm_g[l], sg_w[l], sg_b[l],
                             sc_conv_w[l], gdn_conv_w[l], gdn_a_log[l], gdn_dt_bias[l], gdn_norm_g[l], ctx_out)
        x = x + gate[:, None] * rms_norm(y @ w_out[l], g_post[l])
        if ctx_out:
            ctx = ctx + gate_c * rms_norm(yc @ w_out[l], g_post[l])
    return x
```

```python
import math
import numpy as np
import concourse.bass as bass
import concourse.mybir as mybir
from concourse.bass_utils import run_bass_kernel_spmd

F32 = mybir.dt.float32
BF16 = mybir.dt.bfloat16
AF = mybir.ActivationFunctionType
ALU = mybir.AluOpType
AX = mybir.AxisListType

P = 128
D = 1024
KC = 8
DEPTH = 4
BR = 256
HD = 64
NH = 4
IN_W = 3856
EPS = 1e-6
BIG = 30000.0
N_CORES = 8
import os
KDBG = os.environ.get("KDBG", "")


class Op:
    __slots__ = ("eng", "fn", "deps", "sig", "is_dma", "sem", "val", "needs_sig", "order")

    def __init__(self, eng, fn, is_dma=False):
        self.eng = eng
        self.fn = fn
        self.deps = []
        self.sig = None
        self.is_dma = is_dma
        self.sem = None
        self.val = None
        self.needs_sig = False
        self.order = 0


class Prog:
    ENGS = ("pe", "act", "dve", "pool", "sp")

    def __init__(self, nc):
        self.nc = nc
        self.ops = {e: [] for e in self.ENGS}
        self.last_w = {}
        self.readers = {}
        self.n = 0
        self.dma_slots = 24
        self.dma_rr = 0
        self.dma_last = [None] * self.dma_slots
        self.dma_cnt = [0] * self.dma_slots
        self.expand = {}
        self.key_last = {}
        self.expand["wbuf"] = [("wbuf", kc, c0) for kc in range(8) for c0 in range(0, 3856, 964)]
        self.expand["wout"] = [("wout", kc) for kc in range(8)]

    def _add(self, op, reads, writes):
        ex = self.expand
        for k in reads:
            if isinstance(k, tuple) and k[0] == "pa":
                self.key_last[k[1]] = self.n
        for k in writes:
            if isinstance(k, tuple) and k[0] == "pa":
                self.key_last[k[1]] = self.n
        reads = [x for k in reads for x in ex.get(k, (k,))]
        writes = [x for k in writes for x in ex.get(k, (k,))]
        deps = []
        for k in reads:
            w = self.last_w.get(k)
            if w is not None:
                deps.append(w)
            if k == "psb" or (isinstance(k, tuple) and k[0] == "ps"):
                for r in self.readers.get(k, ()):
                    if r.eng != op.eng:
                        deps.append(r)
        for k in writes:
            w = self.last_w.get(k)
            if w is not None:
                deps.append(w)
            for r in self.readers.get(k, ()):
                deps.append(r)
        seen = set()
        for d in deps:
            if d is op or id(d) in seen:
                continue
            if d.eng == "pe" and op.eng == "pe" and not d.is_dma and not op.is_dma:
                continue
            seen.add(id(d))
            op.deps.append(d)
            d.needs_sig = True
        for k in reads:
            self.readers.setdefault(k, []).append(op)
        for k in writes:
            self.last_w[k] = op
            self.readers[k] = []
        op.order = self.n
        self.n += 1
        self.ops[op.eng].append(op)
        return op

    def op(self, eng, fn, reads=(), writes=()):
        return self._add(Op(eng, fn), reads, writes)

    def dma(self, eng, out, in_, reads=(), writes=()):
        o = Op(eng, lambda e: e.dma_start(out=out, in_=in_), is_dma=True)
        s = self.dma_rr
        self.dma_rr = (self.dma_rr + 1) % self.dma_slots
        prev = self.dma_last[s]
        self.dma_cnt[s] += 1
        o.sem = s
        o.val = 16 * self.dma_cnt[s]
        o.needs_sig = True
        self._add(o, reads, writes)
        if prev is not None and prev not in o.deps:
            o.deps.append(prev)
        self.dma_last[s] = o
        return o

    def emit(self):
        nc = self.nc
        import contextlib
        with contextlib.ExitStack() as st:
            esem = {e: st.enter_context(nc.semaphore("s_" + e)) for e in ("pe", "act", "dve", "pool")}
            dsem = [st.enter_context(nc.semaphore("d%d" % i)) for i in range(self.dma_slots)]
            for e in ("pe", "act", "dve", "pool"):
                k = 0
                for o in self.ops[e]:
                    if o.is_dma:
                        continue
                    if o.needs_sig:
                        k += 1
                        o.sem = e
                        o.val = k
            block = st.enter_context(nc.Block())
            handles = {"pe": block.tensor, "act": block.scalar, "dve": block.vector,
                       "pool": block.gpsimd, "sp": block.sync}

            def make(ename):
                ops = self.ops[ename]

                def body(eng):
                    seen = {}
                    for o in ops:
                        for d in o.deps:
                            sem = dsem[d.sem] if d.is_dma else esem[d.sem]
                            key = ("d", d.sem) if d.is_dma else d.sem
                            if seen.get(key, 0) >= d.val:
                                continue
                            seen[key] = d.val
                            eng.wait_ge(sem, d.val)
                        ins = o.fn(eng)
                        if o.is_dma:
                            ins.then_inc(dsem[o.sem], 16)
                        elif o.needs_sig:
                            ins.then_inc(esem[o.sem], 1)
                    for s in range(self.dma_slots):
                        last = self.dma_last[s]
                        if last is not None and last.eng == ename:
                            if seen.get(("d", s), 0) < last.val:
                                eng.wait_ge(dsem[s], last.val)
                return body

            for e in self.ENGS:
                if self.ops[e]:
                    handles[e](make(e))


def _host_consts(TC, TL):
    NT = TC + TL
    c = {}
    c["ident"] = np.eye(P, dtype=np.float32)
    c["ones"] = np.ones((P, P), np.float32)
    bo = np.zeros((P, P), np.float32)
    bo[:64, :64] = 1.0
    bo[64:, 64:] = 1.0
    c["bones"] = bo
    sel = np.zeros((4, 4 * P), np.float32)
    for r in range(4):
        sel[r, r * P:(r + 1) * P] = 1.0
    c["sel"] = sel
    j = np.arange(P)[:, None]
    i = np.arange(P)[None, :]
    c["tri_f"] = (j <= i).astype(np.float32)
    c["tri_b"] = (j >= i).astype(np.float32)
    lg = np.log(1.0 - 2.0 ** (-5.0 - np.arange(NH, dtype=np.float64)))
    gam = np.exp(lg)
    pos = np.arange(P, dtype=np.float64)
    rm = np.zeros((2, P, NH, P), np.float32)
    rdec = np.zeros((2, P, 8), np.float32)
    for h in range(NH):
        ch = gam[h] ** (-128.0)
        rm[0, :, h, :] = ch * (i >= j)
        rm[1, :, h, :] = ch * (i <= j)
        rdec[0, :, h] = gam[h] ** (pos + 1.0)
        rdec[0, :, 4 + h] = 0.125 * gam[h] ** (127.0 - pos)
        rdec[1, :, h] = gam[h] ** (128.0 - pos)
        rdec[1, :, 4 + h] = 0.125 * gam[h] ** pos
    c["ret_mask"] = rm.reshape(2, P, NH * P)
    c["ret_dec"] = rdec
    cd = np.zeros((P, 2, 64), np.float32)
    for h in range(NH):
        cd[64 * (h % 2):64 * (h % 2) + 64, h // 2, :] = gam[h] ** 128.0
    c["ret_cdec_p"] = cd.reshape(P, 128)
    pi = np.arange(P)[:, None]
    fj = np.arange(P)[None, :]
    gm = np.zeros((2, 2, P, NH, P), np.float32)
    gm[0, 0] = (BIG * (fj >= pi))[:, None, :]
    gm[1, 0] = (BIG * (fj <= pi))[:, None, :]
    gm[0, 1] = (-BIG * (fj < pi))[:, None, :]
    gm[1, 1] = (-BIG * (fj > pi))[:, None, :]
    c["gdn_mask"] = gm.reshape(4, P, NH * P)
    nf = 16
    inv = (np.float32(10000.0) ** (-np.arange(nf, dtype=np.float32) / np.float32(nf))).astype(np.float32)
    C = np.ones((NT, P, 64), np.float32)
    S = np.zeros((NT, P, 64), np.float32)
    for t in range(TL):
        tok = np.arange(t * P, (t + 1) * P)
        row = (tok // 64).astype(np.float32)
        col = (tok % 64).astype(np.float32)
        ar = (row[:, None] * inv[None, :]).astype(np.float32)
        ac = (col[:, None] * inv[None, :]).astype(np.float32)
        C[TC + t] = np.concatenate([np.cos(ar), np.cos(ar), np.cos(ac), np.cos(ac)], axis=1)
        S[TC + t] = np.concatenate([-np.sin(ar), np.sin(ar), -np.sin(ac), np.sin(ac)], axis=1)
    c["rot_c"] = np.ascontiguousarray(C.transpose(1, 0, 2)).reshape(P, NT * 64)
    c["rot_s"] = np.ascontiguousarray(S.transpose(1, 0, 2)).reshape(P, NT * 64)
    return c


CONST_SHAPES = lambda NT: {
    "ident": [P, P], "ones": [P, P], "bones": [P, P], "sel": [4, 4 * P], "tri_f": [P, P], "tri_b": [P, P],
    "ret_mask": [2, P, NH * P], "ret_dec": [2, P, 8], "ret_cdec_p": [P, P], "gdn_mask": [4, P, NH * P],
    "rot_c": [P, NT * 64], "rot_s": [P, NT * 64],
}

RP_RETG = 0
RP_GDNG = 256
RP_ALOG = 320
RP_DTB = 328
RP_SGB = 336
RP_W = 1024 + 336 + 512
PP_GPRE = 0
PP_SCW = 8
PP_GDW = 14
PP_W = 32


def _host_layer_params(inp):
    L = inp["w_in"].shape[0]
    rowp = np.zeros((L, RP_W), np.float32)
    O = 1024
    rowp[:, 0:1024] = inp["g_post"]
    rowp[:, O + RP_RETG:O + RP_RETG + 256] = inp["ret_norm_g"]
    rowp[:, O + RP_GDNG:O + RP_GDNG + 64] = inp["gdn_norm_g"]
    rowp[:, O + RP_ALOG:O + RP_ALOG + 8] = inp["gdn_a_log"].reshape(L, 8)
    rowp[:, O + RP_DTB:O + RP_DTB + 8] = inp["gdn_dt_bias"].reshape(L, 8)
    rowp[:, O + RP_SGB:O + RP_SGB + 512] = inp["sg_b"].reshape(L, 512)
    pp = np.zeros((L, P, PP_W), np.float32)
    pp[:, :, PP_GPRE:PP_GPRE + 8] = inp["g_pre"].reshape(L, 8, P).transpose(0, 2, 1)
    pp[:, :, PP_SCW:PP_SCW + 6] = inp["sc_conv_w"].reshape(L, 3, 2, P).transpose(0, 3, 2, 1).reshape(L, P, 6)
    pp[:, :, PP_GDW:PP_GDW + 18] = inp["gdn_conv_w"].reshape(L, 3, 6, P).transpose(0, 3, 2, 1).reshape(L, P, 18)
    wsT = np.ascontiguousarray(inp["sg_w"].transpose(0, 3, 1, 2)).reshape(L, P, 4 * P)
    return rowp, pp, wsT


def build(TC, TL, layers, branches=("ret", "sg", "sc", "gdn"), taps=(), n_seq=2, do_out=True, passes=("F", "B"),
          ctx_ext=False):
    ps_last = _build(TC, TL, layers, branches, taps, n_seq, do_out, passes, None, ctx_ext)
    return _build(TC, TL, layers, branches, taps, n_seq, do_out, passes, ps_last, ctx_ext)


def _build(TC, TL, layers, branches, taps, n_seq, do_out, passes, ps_last, ctx_ext=False):
    import contextlib
    NT = TC + TL
    L = len(layers)
    nc = bass.Bass("TRN2", target_bir_lowering=False)
    pg = Prog(nc)
    st = contextlib.ExitStack()
    tap_out = {}

    def din(name, shape, dt=F32):
        return nc.dram_tensor(name, list(shape), dt, kind="ExternalInput").ap()

    x_d = din("x", [n_seq, TL * P, D])
    ctx_d = din("ctx", [n_seq, TC * P, D])
    cT_d = din("cT", [P, KC * 4])
    wmod_d = din("w_mod", [DEPTH, D, 3 * D])
    bmod_d = din("b_mod", [DEPTH, 3 * D])
    win_d = din("w_in", [DEPTH, D, IN_W])
    wout_d = din("w_out", [DEPTH, D, D])
    rowp_d = din("rowp", [DEPTH, RP_W])
    pp_d = din("pp", [DEPTH, P, PP_W])
    wsT_d = din("wsT", [DEPTH, P, 4 * P])
    cds = {k: din("c_" + k, shp) for k, shp in CONST_SHAPES(NT).items()}
    y_d = nc.dram_tensor("y", [n_seq, TL * P, D], F32, kind="ExternalOutput").ap()
    xs_d = nc.dram_tensor("xs_scr", [n_seq, TL * P, D], F32, kind="Internal").ap()
    cs_d = nc.dram_tensor("cs_scr", [n_seq, TC * P, D], F32, kind="ExternalOutput" if ctx_ext else "Internal").ap()
    of_d = nc.dram_tensor("of_scr", [NT, P, 512], F32, kind="Internal").ap()
    gq_d = nc.dram_tensor("gq_scr", [NT, P, 768], F32, kind="Internal").ap()

    def sb(name, shape, dt=F32):
        return st.enter_context(nc.sbuf_tensor("sb_" + name, list(shape), dt))

    def psum(name, shape, dt=F32):
        return st.enter_context(nc.psum_tensor(name, list(shape), dt))

    def MM(out, lhsT, rhs, start=True, stop=True, r=(), w=()):
        pg.op("pe", lambda e: e.matmul(out, lhsT, rhs, start=start, stop=stop), r, w)

    def TR(out, in_, idn, r=(), w=()):
        pg.op("pe", lambda e: e.transpose(out, in_, idn), r, w)

    def ACT(out, in_, func, bias=None, scale=None, accum=None, r=(), w=()):
        kw = {}
        if bias is not None:
            kw["bias"] = bias
        if scale is not None:
            kw["scale"] = scale
        if accum is not None:
            kw["accum_out"] = accum
        pg.op("act", lambda e: e.activation(out, in_, func, **kw), r, w)

    def TT(eng, out, a, b, op, r=(), w=()):
        pg.op(eng, lambda e: e.tensor_tensor(out, a, b, op), r, w)

    def TS(eng, out, a, s1, s2, op0, op1=None, r=(), w=()):
        if op1 is None:
            pg.op(eng, lambda e: e.tensor_scalar(out, a, s1, None, op0), r, w)
        else:
            pg.op(eng, lambda e: e.tensor_scalar(out, a, s1, s2, op0, op1), r, w)

    def STT(out, in0, scalar, in1, op0, op1, r=(), w=()):
        pg.op("dve", lambda e: e.scalar_tensor_tensor(out, in0, scalar, in1, op0, op1), r, w)

    def CP(eng, out, in_, r=(), w=()):
        if eng == "act":
            pg.op("act", lambda e: e.copy(out, in_), r, w)
        else:
            pg.op(eng, lambda e: e.tensor_copy(out, in_), r, w)

    def RSUM(out, in_, r=(), w=()):
        pg.op("dve", lambda e: e.reduce_sum(out, in_, AX.X), r, w)

    def RECIP(out, in_, r=(), w=()):
        pg.op("dve", lambda e: e.reciprocal(out, in_), r, w)

    def MSET(eng, ap, val, w=()):
        pg.op(eng, lambda e: e.memset(ap, val), (), w)

    def DMA(eng, out, in_, r=(), w=()):
        pg.dma(eng, out, in_, r, w)

    def tap(name, ap, shape, r, dt=F32):
        if name not in taps:
            return
        d = nc.dram_tensor("tap_" + name, list(shape), dt, kind="ExternalOutput").ap()
        tap_out[name] = "tap_" + name
        DMA("sp", d, ap, r=r, w=[("tap", name)])

    NPS = 7
    psf = [psum("psf%d" % i, [P, 512]) for i in range(NPS)]
    psb = psum("psb", [P, 1024], BF16)
    ps_rr = [0]
    ps_cnt = [0]
    ps_occ = [None] * NPS

    def nps():
        a = ps_cnt[0]
        ps_cnt[0] += 1
        if ps_last is None:
            i = a % NPS
        else:
            i = None
            for step in range(NPS):
                j = (ps_rr[0] + step) % NPS
                occ = ps_occ[j]
                if occ is None or ps_last.get(occ, -1) < pg.n:
                    i = j
                    break
            assert i is not None, "out of PSUM banks"
            ps_rr[0] = (i + 1) % NPS
        ps_occ[i] = a
        pg.expand[("pa", a)] = [("ps", i)]
        return psf[i], ("pa", a)

    cst = {}
    for k, shp in CONST_SHAPES(NT).items():
        if k in ("rot_c", "rot_s"):
            continue
        if len(shp) == 3:
            t_ = sb("k_" + k, [shp[1], shp[0], shp[2]])
            for a in range(shp[0]):
                DMA("sp", t_[:, a, :], cds[k][a], w=[("c", k)])
        else:
            t_ = sb("k_" + k, shp)
            DMA("sp", t_[:], cds[k], w=[("c", k)])
        cst[k] = t_
    ident = cst["ident"]
    ones = cst["ones"]
    ident_b = sb("ident_b", [P, P], BF16)
    CP("dve", ident_b[:], ident[:], r=[("c", "ident")], w=["ident_b"])
    KI = ("c", "ident")

    siluT = sb("siluT", [P, KC, 4])
    AT = sb("AT", [P, KC, 4])
    shT = sb("shT", [P, KC, 4])
    Gb = sb("Gb", [P, 3, D])
    rowp = sb("rowp", [P, RP_W - 1024])
    ppp = sb("ppp", [P, PP_W])
    wsT = sb("wsT", [P, 4, P])
    bsT = sb("bsT", [P, 2, P])
    negA = sb("negA", [P, 8])
    hT = sb("hT", [P, KC, NT * P + 2], BF16)
    wbuf = sb("wbuf", [P, KC, IN_W], BF16)
    wout = sb("wout", [P, KC, D], BF16)
    xt = [sb("xt%d" % i, [P, D]) for i in range(2)]
    st8 = [sb("st8_%d" % i, [P, 8]) for i in range(2)]
    yT = sb("yT", [P, KC, P], BF16)
    Sret = [sb("Sret%d" % i, [P, 2, 64]) for i in range(2)]
    Sgdn = [sb("Sgdn%d" % i, [P, 2, 64]) for i in range(2)]
    r_qz = sb("r_qz", [P, 4, P])
    g_knz = sb("g_knz", [P, 4, P])
    g_wz = sb("g_wz", [P, 4, P])
    g_qdz = sb("g_qdz", [P, 4, P])

    GR = 128
    ARENA = 36 * 1024 // 4
    arena = sb("arena", [P, ARENA])
    wk_cache = {}
    scope_ptr = {}

    def wk(name, shape=(P, 512), dt=F32):
        if name not in wk_cache:
            scope = name.split("_")[0]
            n = 1
            for d_ in shape[1:]:
                n *= d_
            if dt != F32:
                n = n // 2
            ng = (n + GR - 1) // GR
            off = scope_ptr.get(scope, 0)
            scope_ptr[scope] = off + ng * GR
            assert off + ng * GR <= ARENA, (name, off, ng * GR)
            ap = arena[0:shape[0], off:off + n]
            if dt != F32:
                ap = ap.bitcast(dt)
            if len(shape) == 3:
                ap = ap.rearrange("p (a b) -> p a b", b=shape[2])
            pg.expand[name] = [("ar", g) for g in range(off // GR, off // GR + ng)]
            wk_cache[name] = ap
        return wk_cache[name]

    cT = wk("cT", (P, KC * 4))
    DMA("sp", cT[:], cT_d, w=["cT"])
    e_ = wk("cT_e", (P, KC * 4))
    ACT(e_[:], cT[:], AF.Exp, scale=-1.0, r=["cT"], w=["cT_e"])
    ACT(e_[:], e_[:], AF.Ln, bias=1.0, r=["cT_e"], w=["cT_e"])
    ACT(e_[:], e_[:], AF.Exp, scale=-1.0, r=["cT_e"], w=["cT_e"])
    TT("dve", siluT[:].rearrange("p k r -> p (k r)"), cT[:], e_[:], ALU.mult, r=["cT", "cT_e"], w=["siluT"])
    MSET("dve", yT[:], 0.0, w=[("yT", 0), ("yT", 1), ("yT", 2), ("yT", 3)])
    MSET("dve", r_qz[:], 0.0, w=["r_qz"])
    MSET("dve", g_knz[:], 0.0, w=["g_knz"])
    MSET("dve", g_wz[:], 0.0, w=["g_wz"])
    MSET("dve", g_qdz[:], 0.0, w=["g_qdz"])
    MSET("dve", hT[:, :, 0:1], 0.0, w=["hTpadL"])
    MSET("dve", hT[:, :, NT * P + 1:NT * P + 2], 0.0, w=["hTpadR"])

    def layer_setup(l):
        DMA("sp", rowp[:], rowp_d[l, 1024:RP_W].partition_broadcast(P), w=["rowp"])
        DMA("sp", ppp[:], pp_d[l], w=["ppp"])
        DMA("sp", wsT[:].rearrange("p h i -> p (h i)"), wsT_d[l], w=["wsT"])
        wv = win_d[l].rearrange("(kc p) n -> p kc n", p=P)
        wov = wout_d[l].rearrange("(kc p) n -> p kc n", p=P)
        i_ = 0
        for kc in range(KC):
            for c0 in range(0, IN_W, 964):
                nm = "m_s%d" % (i_ % 2)
                stg = wk(nm, (P, 1024))
                DMA("sp", stg[:, 0:964], wv[:, kc, c0:c0 + 964], w=[nm])
                CP("act" if i_ % 2 else "dve", wbuf[:, kc, c0:c0 + 964], stg[:, 0:964], r=[nm], w=[("wbuf", kc, c0)])
                i_ += 1
            nm = "m_s%d" % (i_ % 2)
            stg = wk(nm, (P, 1024))
            DMA("sp", stg[:], wov[:, kc, :], w=[nm])
            CP("act" if i_ % 2 else "dve", wout[:, kc, :], stg[:], r=[nm], w=[("wout", kc)])
            i_ += 1
        gp = wk("m_gp", (P, D))
        DMA("sp", gp[:], rowp_d[l, 0:1024].partition_broadcast(P), w=["m_gp"])
        wmv = wmod_d[l].rearrange("(kc p) n -> p kc n", p=P)
        psT, kT = nps()
        for nt in range(12):
            b = nt % 2
            wt = wk("m_w%d" % b, (P, KC, 256))
            bm = wk("m_b%d" % b, (4, 256))
            mr = wk("m_r%d" % b, (4, 256))
            DMA("sp", wt[:], wmv[:, :, nt * 256:(nt + 1) * 256], w=["m_w%d" % b])
            DMA("sp", bm[:], bmod_d[l, nt * 256:(nt + 1) * 256].partition_broadcast(4), w=["m_b%d" % b])
            ps, pk = nps()
            for kc in range(KC):
                MM(ps[0:4, 0:256], siluT[:, kc, :], wt[:, kc, :], start=(kc == 0), stop=(kc == KC - 1),
                   r=["siluT", "m_w%d" % b], w=[pk])
            TT("dve", mr[:], ps[0:4, 0:256], bm[:], ALU.add, r=[pk, "m_b%d" % b], w=["m_r%d" % b])
            g, j = nt // 4, nt % 4
            if g < 2:
                for q in range(2):
                    kc = 2 * j + q
                    TR(psT[:, (g * KC + kc) * 4:(g * KC + kc) * 4 + 4], mr[0:4, q * P:(q + 1) * P], ident[0:4, 0:4],
                       r=["m_r%d" % b, KI], w=[kT])
            else:
                for r_ in range(3):
                    ps2, pk2 = nps()
                    MM(ps2[:, 0:256], cst["sel"][0:4, r_ * P:(r_ + 1) * P], mr[0:4, :], r=["m_r%d" % b, ("c", "sel")], w=[pk2])
                    TT("dve", Gb[:, r_, j * 256:(j + 1) * 256], ps2[:, 0:256], gp[:, j * 256:(j + 1) * 256], ALU.mult,
                       r=[pk2, "m_gp"], w=["Gb"])
        CP("dve", shT[:].rearrange("p k r -> p (k r)"), psT[:, 0:32], r=[kT], w=["shT"])
        STT(AT[:], psT[:, 32:64].rearrange("p (k r) -> p k r", r=4), 1.0,
            ppp[:, PP_GPRE:PP_GPRE + 8].unsqueeze(2).to_broadcast([P, KC, 4]), ALU.add, ALU.mult,
            r=[kT, "ppp"], w=["AT"])
        for c in range(2):
            for hh in range(2):
                h = 2 * c + hh
                CP("dve", bsT[64 * hh:64 * hh + 64, c, :], rowp[64 * hh:64 * hh + 64, RP_SGB + h * P:RP_SGB + (h + 1) * P],
                   r=["rowp"], w=["bsT"])
        ACT(negA[:], rowp[:, RP_ALOG:RP_ALOG + 8], AF.Exp, r=["rowp"], w=["negA"])
        TS("dve", negA[:], negA[:], -1.0, None, ALU.mult, r=["negA"], w=["negA"])
        tap("AT%d" % l, AT[:].rearrange("p k r -> p (k r)"), [P, 32], ["AT"])
        tap("shT%d" % l, shT[:].rearrange("p k r -> p (k r)"), [P, 32], ["shT"])
        tap("Gb%d" % l, Gb[:].rearrange("p a n -> p (a n)"), [P, 3 * D], ["Gb"])

    def src_tile(li, s, t):
        if t < TC:
            base = ctx_d if li == 0 else cs_d
            return base[s, t * P:(t + 1) * P, :], ("dx", "c", s, t)
        tt = t - TC
        base = x_d if li == 0 else xs_d
        return base[s, tt * P:(tt + 1) * P, :], ("dx", "x", s, tt)

    def dst_tile(li, s, t):
        if t < TC:
            return cs_d[s, t * P:(t + 1) * P, :], ("dx", "c", s, t)
        tt = t - TC
        base = y_d if li == L - 1 else xs_d
        return base[s, tt * P:(tt + 1) * P, :], ("dx", "x", s, tt)

    xt_rr = [0]

    def load_x(li, s, t):
        b = xt_rr[0]
        xt_rr[0] ^= 1
        src, dk = src_tile(li, s, t)
        DMA("sp", xt[b][:], src, r=[dk], w=[("xt", b)])
        return xt[b], ("xt", b), st8[b], ("st8", b)

    def rstd_from(s8, sk, col_in, scale):
        ACT(s8[:, col_in + 1:col_in + 2], s8[:, col_in:col_in + 1], AF.Ln, bias=EPS, scale=scale, r=[sk], w=[sk])
        ACT(s8[:, col_in + 2:col_in + 3], s8[:, col_in + 1:col_in + 2], AF.Exp, scale=-0.5, r=[sk], w=[sk])

    def stage_h(li, s):
        for t in range(NT):
            r_ = 2 if t < TC else s
            xa, xk, s8, sk = load_x(li, s, t)
            ACT(wk("h_junk", (P, D))[:], xa[:], AF.Square, accum=s8[:, 0:1], r=[xk], w=["h_junk", sk])
            rstd_from(s8, sk, 0, 1.0 / D)
            xn = wk("h_xn", (P, D), BF16)
            TS("dve", xn[:], xa[:], s8[:, 2:3], None, ALU.mult, r=[xk, sk], w=["h_xn"])
            for kc in range(KC):
                TR(psb[:, kc * P:(kc + 1) * P], xn[:, kc * P:(kc + 1) * P], ident_b[:], r=["h_xn", "ident_b"], w=["psb"])
            for kc in range(KC):
                o_ = hT[:, kc, 1 + t * P:1 + (t + 1) * P]
                i_ = psb[:, kc * P:(kc + 1) * P]
                if kc % 2:
                    ACT(o_, i_, AF.Identity, bias=shT[:, kc, r_:r_ + 1], scale=AT[:, kc, r_:r_ + 1],
                        r=["psb", "AT", "shT"], w=[("hT", t)])
                else:
                    TS("dve", o_, i_, AT[:, kc, r_:r_ + 1], shT[:, kc, r_:r_ + 1], ALU.mult, ALU.add,
                       r=["psb", "AT", "shT"], w=[("hT", t)])

    def proj_tok(t, c0, n, ps_ap, pk):
        for kc in range(KC):
            MM(ps_ap, hT[:, kc, 1 + t * P:1 + (t + 1) * P], wbuf[:, kc, c0:c0 + n],
               start=(kc == 0), stop=(kc == KC - 1), r=[("hT", t), "wbuf"], w=[pk])

    def hkeys(t):
        return [("hT", t), ("hT", t - 1) if t > 0 else "hTpadL", ("hT", t + 1) if t < NT - 1 else "hTpadR"]

    def proj_feat(t, c0, ps_ap, pk, halo):
        lo, n = (t * P, P + 2) if halo else (t * P + 1, P)
        for kc in range(KC):
            MM(ps_ap, wbuf[:, kc, c0:c0 + P], hT[:, kc, lo:lo + n],
               start=(kc == 0), stop=(kc == KC - 1), r=(hkeys(t) if halo else [("hT", t)]) + ["wbuf"], w=[pk])

    def silu_of(src, src_keys, name, shape, tile=None):
        e = tile if tile is not None else wk(name, shape)
        ACT(e[:], src, AF.Exp, scale=-1.0, r=src_keys, w=[name])
        ACT(e[:], e[:], AF.Ln, bias=1.0, r=[name], w=[name])
        ACT(e[:], e[:], AF.Exp, scale=-1.0, r=[name], w=[name])
        return e

    def gelu_of(src, src_keys, name, shape):
        xs = wk(name + "_x", shape)
        CP("act", xs[:], src, r=src_keys, w=[name + "_x"])
        a = wk(name + "_a", shape)
        ACT(a[:], xs[:], AF.Square, r=[name + "_x"], w=[name + "_a"])
        TS("dve", a[:], a[:], 0.044715, 1.0, ALU.mult, ALU.add, r=[name + "_a"], w=[name + "_a"])
        TT("pool", a[:], a[:], xs[:], ALU.mult, r=[name + "_a", name + "_x"], w=[name + "_a"])
        ACT(a[:], a[:], AF.Exp, scale=-1.5957691216057308, r=[name + "_a"], w=[name + "_a"])
        ACT(a[:], a[:], AF.Ln, bias=1.0, r=[name + "_a"], w=[name + "_a"])
        ACT(a[:], a[:], AF.Exp, scale=-1.0, r=[name + "_a"], w=[name + "_a"])
        TT("dve", xs[:], xs[:], a[:], ALU.mult, r=[name + "_a", name + "_x"], w=[name + "_x"])
        return xs, name + "_x"

    def ret_dir(d, t, first, pass_b, cur, need_out):
        S_c, S_n = Sret[cur], Sret[cur ^ 1]
        kS_c, kS_n = ("Sret", cur), ("Sret", cur ^ 1)
        if pass_b and need_out:
            DMA("sp", wk("r_ofl", (P, 256))[:], of_d[t, :, 0:256], r=[("of", t, 0)], w=["r_ofl"])
        ps1, k1 = nps()
        ps2, k2 = nps()
        proj_tok(t, 0, 512, ps1[:], k1)
        nv = 512 if pass_b else 256
        proj_tok(t, 512, nv, ps2[:, 0:nv], k2)
        C = wk("r_rc", (P, 64))
        Sg_ = wk("r_rs", (P, 64))
        DMA("sp", C[:], cds["rot_c"][:, t * 64:(t + 1) * 64], w=["r_rc"])
        DMA("sp", Sg_[:], cds["rot_s"][:, t * 64:(t + 1) * 64], w=["r_rs"])
        Sg = Sg_.rearrange("p (a s f) -> p a s f", a=2, s=2, f=16)
        t1 = wk("r_t1")
        t2 = wk("r_t2")
        qk = wk("r_qk")
        TT("dve", t1[:].rearrange("p (h d) -> p h d", d=64), ps1[:].rearrange("p (h d) -> p h d", d=64),
           C.unsqueeze(1).to_broadcast([P, 8, 64]), ALU.mult, r=[k1, "r_rc"], w=["r_t1"])
        x5 = ps1[:].rearrange("p (h a s f) -> p h a s f", a=2, s=2, f=16)
        t5 = t2[:].rearrange("p (h a s f) -> p h a s f", a=2, s=2, f=16)
        for s_ in range(2):
            TT("dve", t5[:, :, :, s_, :], x5[:, :, :, 1 - s_, :],
               Sg[:, :, s_, :].unsqueeze(1).to_broadcast([P, 8, 2, 16]), ALU.mult,
               r=[k1, "r_rs"], w=["r_t2"])
        TT("pool", t1[:], t1[:], t2[:], ALU.add, r=["r_t1", "r_t2"], w=["r_t1"])
        TT("dve", qk[:].rearrange("p (h d) -> p h d", d=64), t1[:].rearrange("p (h d) -> p h d", d=64),
           cst["ret_dec"][:, d, :].unsqueeze(2).to_broadcast([P, 8, 64]), ALU.mult,
           r=["r_t1", ("c", "ret_dec")], w=["r_qk"])
        v_sb = wk("r_v", (P, 256))
        CP("act", v_sb[:], ps2[:, 0:256], r=[k2], w=["r_v"])
        if "rstop1" in KDBG:
            return
        psq, kq = nps()
        for c in range(4):
            TR(psq[:, c * P:(c + 1) * P], qk[:, c * P:(c + 1) * P], ident[:], r=["r_qk", KI], w=[kq])
        if "rstopA" in KDBG:
            return
        kT = wk("r_kT", (P, 2, P))
        CP("act", kT[:], psq[:, 256:512].rearrange("p (c n) -> p c n", c=2), r=[kq], w=["r_kT"])
        if "rstopB" in KDBG:
            return
        for hh in range(2):
            pb = 64 * hh
            if "qz2d" in KDBG:
                for c in range(2):
                    CP("dve", r_qz[pb:pb + 64, 2 * c + hh, :], psq[pb:pb + 64, c * P:(c + 1) * P], r=[kq], w=["r_qz"])
            else:
                CP("act" if "qzact" in KDBG else "dve", r_qz[pb:pb + 64, hh::2, :],
                   psq[pb:pb + 64, 0:256].rearrange("p (c n) -> p c n", c=2), r=[kq], w=["r_qz"])
        if "rstop2" in KDBG:
            return
        pss, ks = nps()
        for h in range(NH):
            c = h // 2
            MM(pss[:, h * P:(h + 1) * P], kT[:, c, :], r_qz[:, h, :], r=["r_kT", "r_qz"], w=[ks])
        SmT = wk("r_sm")
        TT("dve", SmT[:], pss[:], cst["ret_mask"][:, d, :], ALU.mult, r=[ks, ("c", "ret_mask")], w=["r_sm"])
        if "rstop3" in KDBG:
            return
        if need_out:
            pso, ko = nps()
            for h in range(NH):
                c, pb = h // 2, 64 * (h % 2)
                MM(pso[:, h * 64:(h + 1) * 64], SmT[:, h * P:(h + 1) * P], v_sb[:, h * 64:(h + 1) * 64],
                   start=True, stop=first, r=["r_sm", "r_v"], w=[ko])
                if not first:
                    MM(pso[:, h * 64:(h + 1) * 64], r_qz[:, h, :], S_c[:, c, :],
                       start=False, stop=True, r=["r_qz", kS_c], w=[ko])
        if "rstop4" in KDBG:
            return
        psu, ku = nps()
        for c in range(2):
            MM(psu[:, c * P:(c + 1) * P], qk[:, 256 + c * P:256 + (c + 1) * P], v_sb[:, c * P:(c + 1) * P],
               r=["r_qk", "r_v"], w=[ku])
        for hh in range(2):
            pb = 64 * hh
            src = psu[pb:pb + 64, 0:256].rearrange("p (c x) -> p c x", c=2)[:, :, pb:pb + 64]
            if first:
                CP("dve", S_n[pb:pb + 64, :, :], src, r=[ku], w=[kS_n])
            else:
                tmp = wk("r_stmp", (P, 2, 64))
                TT("pool", tmp[pb:pb + 64, :, :], S_c[pb:pb + 64, :, :],
                   cst["ret_cdec_p"][pb:pb + 64, :].rearrange("p (c e) -> p c e", e=64), ALU.mult,
                   r=[kS_c, ("c", "ret_cdec_p")], w=["r_stmp"])
                TT("dve", S_n[pb:pb + 64, :, :], tmp[pb:pb + 64, :, :], src, ALU.add, r=["r_stmp", ku], w=[kS_n])
        if not need_out or "rstop5" in KDBG:
            return None
        if not pass_b:
            o_sb = wk("r_of", (P, 256))
            CP("act", o_sb[:], pso[:, 0:256], r=[ko], w=["r_of"])
            if "noof" not in KDBG:
                DMA("sp", of_d[t, :, 0:256], o_sb[:], r=["r_of"], w=[("of", t, 0)])
            tap("ret_of_%d" % t, o_sb[:], [P, 256], ["r_of"])
            return None
        ofl = wk("r_ofl", (P, 256))
        o = wk("r_o", (P, 256))
        TT("dve", o[:], pso[:, 0:256], ofl[:], ALU.add, r=[ko, "r_ofl"], w=["r_o"])
        tap("ret_o_%d" % t, o[:], [P, 256], ["r_o"])
        o3 = o[:].rearrange("p (h e) -> p h e", e=64)
        s8 = wk("r_s8", (P, 16))
        RSUM(s8[:, 0:4], o3, r=["r_o"], w=["r_s8"])
        TS("dve", s8[:, 0:4], s8[:, 0:4], 1.0 / 64, None, ALU.mult, r=["r_s8"], w=["r_s8"])
        TT("dve", o3, o3, s8[:, 0:4].unsqueeze(2).to_broadcast([P, 4, 64]), ALU.subtract, r=["r_o", "r_s8"], w=["r_o"])
        sq = wk("r_sq", (P, 256))
        ACT(sq[:], o[:], AF.Square, r=["r_o"], w=["r_sq"])
        RSUM(s8[:, 4:8], sq[:].rearrange("p (h e) -> p h e", e=64), r=["r_sq"], w=["r_s8"])
        ACT(s8[:, 8:12], s8[:, 4:8], AF.Ln, bias=EPS, scale=1.0 / 64, r=["r_s8"], w=["r_s8"])
        ACT(s8[:, 12:16], s8[:, 8:12], AF.Exp, scale=-0.5, r=["r_s8"], w=["r_s8"])
        TT("dve", o3, o3, s8[:, 12:16].unsqueeze(2).to_broadcast([P, 4, 64]), ALU.mult, r=["r_o", "r_s8"], w=["r_o"])
        TT("pool", o[:], o[:], rowp[:, RP_RETG:RP_RETG + 256], ALU.mult, r=["r_o", "rowp"], w=["r_o"])
        sg_ = silu_of(ps2[:, 256:512], [k2], "r_sz", (P, 256))
        TT("dve", sg_[:], sg_[:], ps2[:, 256:512], ALU.mult, r=["r_sz", k2], w=["r_sz"])
        yb = wk("r_yb", (P, 256), BF16)
        TT("dve", yb[:], o[:], sg_[:], ALU.mult, r=["r_o", "r_sz"], w=["r_yb"])
        for c in range(2):
            TR(psb[:, c * P:(c + 1) * P], yb[:, c * P:(c + 1) * P], ident_b[:], r=["r_yb", "ident_b"], w=["psb"])
        CP("act", yT[:, 0:2, :], psb[:, 0:256].rearrange("p (c n) -> p c n", n=P), r=["psb"], w=[("yT", 0)])
        return None

    def sg_tile(t):
        psu, ku = nps()
        for c in range(2):
            proj_feat(t, 1024 + c * P, psu[:, c * P:(c + 1) * P], ku, False)
        for c in range(2):
            proj_feat(t, 1536 + c * P, psu[:, (2 + c) * P:(3 + c) * P], ku, False)
        psv, kv = nps()
        proj_tok(t, 1280, 256, psv[:, 0:256], kv)
        gv, kgv = gelu_of(psv[:, 0:256], [kv], "s_gv", (P, 256))
        s8 = wk("s_s8", (P, 8))
        RSUM(s8[:, 0:1], gv[:], r=[kgv], w=["s_s8"])
        TS("dve", s8[:, 0:1], s8[:, 0:1], 1.0 / 256, None, ALU.mult, r=["s_s8"], w=["s_s8"])
        TS("dve", gv[:], gv[:], s8[:, 0:1], None, ALU.subtract, r=[kgv, "s_s8"], w=[kgv])
        ACT(wk("s_j", (P, 256))[:], gv[:], AF.Square, accum=s8[:, 1:2], r=[kgv], w=["s_j", "s_s8"])
        ACT(s8[:, 2:3], s8[:, 1:2], AF.Ln, bias=EPS, scale=1.0 / 256, r=["s_s8"], w=["s_s8"])
        ACT(s8[:, 3:4], s8[:, 2:3], AF.Exp, scale=-0.5, r=["s_s8"], w=["s_s8"])
        TS("dve", gv[:], gv[:], s8[:, 3:4], None, ALU.mult, r=[kgv, "s_s8"], w=[kgv])
        tap("sg_vn_%d" % t, gv[:], [P, 256], [kgv])
        pss, ks = nps()
        for h in range(NH):
            c = h // 2
            MM(pss[:, h * P:(h + 1) * P], gv[:, c * P:(c + 1) * P], wsT[:, h, :], r=[kgv, "wsT"], w=[ks])
        sT = wk("s_sT", (P, 2, P))
        for hh in range(2):
            pb = 64 * hh
            src = pss[pb:pb + 64, :].rearrange("p (c x i) -> p c x i", c=2, x=2)[:, :, hh, :]
            TT("dve", sT[pb:pb + 64, :, :], src, bsT[pb:pb + 64, :, :], ALU.add, r=[ks, "bsT"], w=["s_sT"])
        gu, kgu = gelu_of(psu[:, 0:256], [ku], "s_gu", (P, 256))
        sz = silu_of(psu[:, 256:512], [ku], "s_sz", (P, 256))
        TT("dve", sz[:], sz[:], psu[:, 256:512], ALU.mult, r=["s_sz", ku], w=["s_sz"])
        TT("pool", gu[:], gu[:], sT[:].rearrange("p c i -> p (c i)"), ALU.mult, r=[kgu, "s_sT"], w=[kgu])
        TT("dve", yT[:, 2:4, :], gu[:].rearrange("p (c i) -> p c i", c=2), sz[:].rearrange("p (c i) -> p c i", c=2),
           ALU.mult, r=[kgu, "s_sz"], w=[("yT", 1)])

    def sc_tile(t, seg_first, seg_last):
        W = P + 2
        psA, kA = nps()
        psB, kB = nps()
        psC, kC = nps()
        for c in range(2):
            proj_feat(t, 2048 + c * P, psA[:, c * W:(c + 1) * W], kA, True)
        proj_feat(t, 2304, psA[:, 2 * W:3 * W], kA, True)
        proj_feat(t, 2304 + P, psB[:, 0:W], kB, True)
        for c in range(2):
            proj_feat(t, 1792 + c * P, psB[:, W + c * P:W + (c + 1) * P], kB, False)
            proj_feat(t, 2560 + c * P, psC[:, c * P:(c + 1) * P], kC, False)
        hs = wk("c_h", (P, 2, W))
        CP("act", hs[:, 0, :], psA[:, 2 * W:3 * W], r=[kA], w=["c_h"])
        CP("act", hs[:, 1, :], psB[:, 0:W], r=[kB], w=["c_h"])
        ch = wk("c_ch", (P, 2, W))
        TT("dve", ch[:], psA[:, 0:2 * W].rearrange("p (c n) -> p c n", c=2), hs[:], ALU.mult, r=[kA, "c_h"], w=["c_ch"])
        if seg_first:
            MSET("dve", ch[:, :, 0:1], 0.0, w=["c_ch"])
        if seg_last:
            MSET("dve", ch[:, :, W - 1:W], 0.0, w=["c_ch"])
        cv = wk("c_cv", (P, 2, P))
        for c in range(2):
            w_ = lambda k: ppp[:, PP_SCW + c * 3 + k:PP_SCW + c * 3 + k + 1]
            TS("dve", cv[:, c, :], ch[:, c, 0:P], w_(0), None, ALU.mult, r=["c_ch", "ppp"], w=["c_cv"])
            STT(cv[:, c, :], ch[:, c, 1:P + 1], w_(1), cv[:, c, :], ALU.mult, ALU.add, r=["c_ch", "ppp", "c_cv"], w=["c_cv"])
            STT(cv[:, c, :], ch[:, c, 2:P + 2], w_(2), cv[:, c, :], ALU.mult, ALU.add, r=["c_ch", "ppp", "c_cv"], w=["c_cv"])
        tap("sc_cv_%d" % t, cv[:].rearrange("p c n -> p (c n)"), [P, 256], ["c_cv"])
        sz = silu_of(psC[:, 0:256], [kC], "c_sz", (P, 256))
        TT("dve", sz[:], sz[:], psC[:, 0:256], ALU.mult, r=["c_sz", kC], w=["c_sz"])
        TT("dve", cv[:].rearrange("p c n -> p (c n)"), cv[:].rearrange("p c n -> p (c n)"), psB[:, W:W + 256], ALU.mult,
           r=["c_cv", kB], w=["c_cv"])
        TT("dve", yT[:, 4:6, :], cv[:], sz[:].rearrange("p (c i) -> p c i", c=2), ALU.mult, r=["c_cv", "c_sz"],
           w=[("yT", 2)])

    def gdn_dir(d, t, first, pass_b, cur, need_out, seg_first, seg_last):
        W = P + 2
        S_c, S_n = Sgdn[cur], Sgdn[cur ^ 1]
        kS_c, kS_n = ("Sgdn", cur), ("Sgdn", cur ^ 1)
        psg, kg = nps()
        proj_tok(t, 3840, 16, psg[:, 0:16], kg)
        if not pass_b:
            qkv = wk("g_qkv", (P, 6, W))
            for b in range(2):
                ps, pk = nps()
                for c in range(3):
                    proj_feat(t, 2816 + (3 * b + c) * P, ps[:, c * W:(c + 1) * W], pk, True)
                CP("act", qkv[:, 3 * b:3 * b + 3, :], ps[:, 0:3 * W].rearrange("p (c n) -> p c n", c=3), r=[pk], w=["g_qkv"])
            if seg_first:
                MSET("dve", qkv[:, :, 0:1], 0.0, w=["g_qkv"])
            if seg_last:
                MSET("dve", qkv[:, :, W - 1:W], 0.0, w=["g_qkv"])
        else:
            if need_out:
                DMA("sp", wk("g_ofl", (P, 256))[:], of_d[t, :, 256:512], r=[("of", t, 1)], w=["g_ofl"])
            cv = wk("g_cv", (P, 6, P))
            qkn = wk("g_qkn", (P, 4, P))
            DMA("sp", qkn[:].rearrange("p c n -> p (c n)"), gq_d[t, :, 0:512], r=[("gq", t, 0)], w=["g_qkn"])
            DMA("sp", cv[:, 4:6, :], gq_d[t, :, 512:768].rearrange("p (c n) -> p c n", c=2), r=[("gq", t, 1)], w=["g_cv"])
        g8 = wk("g_g8", (P, 48))
        K8 = "g_g8"
        a_ps = psg[:, 4 * d:4 * d + 4]
        b_ps = psg[:, 8 + 4 * d:12 + 4 * d]
        TT("dve", g8[:, 0:4], a_ps, rowp[:, RP_DTB + 4 * d:RP_DTB + 4 * d + 4], ALU.add, r=[kg, "rowp"], w=[K8])
        ACT(g8[:, 0:4], g8[:, 0:4], AF.Exp, r=[K8], w=[K8])
        ACT(g8[:, 0:4], g8[:, 0:4], AF.Ln, bias=1.0, r=[K8], w=[K8])
        TT("dve", g8[:, 0:4], g8[:, 0:4], negA[:, 4 * d:4 * d + 4], ALU.mult, r=[K8, "negA"], w=[K8])
        ACT(g8[:, 4:8], b_ps, AF.Exp, scale=-1.0, r=[kg], w=[K8])
        ACT(g8[:, 4:8], g8[:, 4:8], AF.Ln, bias=1.0, r=[K8], w=[K8])
        TS("dve", g8[:, 4:8], g8[:, 4:8], -1.0, None, ALU.mult, r=[K8], w=[K8])
        ACT(g8[:, 8:12], g8[:, 4:8], AF.Exp, r=[K8], w=[K8])
        psc, kc_ = nps()
        MM(psc[:, 0:4], cst["tri_b" if d else "tri_f"][:], g8[:, 0:4], r=[K8, ("c", "tri_b" if d else "tri_f")], w=[kc_])
        MM(psc[:, 4:8], ones[:], g8[:, 0:4], r=[K8, ("c", "ones")], w=[kc_])
        CP("dve", g8[:, 12:16], psc[:, 0:4], r=[kc_], w=[K8])
        TT("dve", g8[:, 16:20], g8[:, 4:8], g8[:, 12:16], ALU.add, r=[K8], w=[K8])
        ACT(g8[:, 20:24], g8[:, 16:20], AF.Exp, r=[K8], w=[K8])
        TT("dve", g8[:, 24:28], psc[:, 4:8], g8[:, 12:16], ALU.subtract, r=[kc_, K8], w=[K8])
        ACT(g8[:, 24:28], g8[:, 24:28], AF.Exp, r=[K8], w=[K8])
        TS("dve", g8[:, 32:36], g8[:, 12:16], -1.0, None, ALU.mult, r=[K8], w=[K8])
        cdP = wk("g_cdP", (P, 2))
        for hh in range(2):
            pb = 64 * hh
            ACT(cdP[pb:pb + 64, 0:2], psc[pb:pb + 64, 4 + hh:8:2], AF.Exp, r=[kc_], w=["g_cdP"])
        tap("gdn_g8_%d_%d" % (d, t), g8[:], [P, 48], [K8])
        dg = wk("g_dg")
        TT("dve", dg[:].rearrange("p (h i) -> p h i", h=4), ident[:].unsqueeze(1).to_broadcast([P, 4, P]),
           g8[:, 12:16].unsqueeze(2).to_broadcast([P, 4, P]), ALU.mult, r=[KI, K8], w=["g_dg"])
        psA, kA = nps()
        MM(psA[:], ones[:], dg[:], start=True, stop=False, r=["g_dg", ("c", "ones")], w=[kA])
        MM(psA[:], ident[:], cst["gdn_mask"][:, 2 * d, :], start=False, stop=True, r=[KI, ("c", "gdn_mask")], w=[kA])
        Nm = wk("g_N")
        NmT = wk("g_NT")
        X = wk("g_X")
        kX = "g_X"
        for h in range(NH):
            ACT(Nm[:, h * P:(h + 1) * P], psA[:, h * P:(h + 1) * P], AF.Exp, bias=g8[:, 16 + h:17 + h], scale=-1.0,
                r=[kA, K8], w=["g_N"])
        if not pass_b:
            cv = wk("g_cv", (P, 6, P))
            for c in range(6):
                w_ = lambda k: ppp[:, PP_GDW + c * 3 + k:PP_GDW + c * 3 + k + 1]
                TS("dve", cv[:, c, :], qkv[:, c, 0:P], w_(0), None, ALU.mult, r=["g_qkv", "ppp"], w=["g_cv"])
                STT(cv[:, c, :], qkv[:, c, 1:P + 1], w_(1), cv[:, c, :], ALU.mult, ALU.add, r=["g_qkv", "ppp", "g_cv"], w=["g_cv"])
                STT(cv[:, c, :], qkv[:, c, 2:P + 2], w_(2), cv[:, c, :], ALU.mult, ALU.add, r=["g_qkv", "ppp", "g_cv"], w=["g_cv"])
            cvf = cv[:].rearrange("p c n -> p (c n)")
            sg_ = silu_of(cvf, ["g_cv"], "g_qkv", None, tile=qkv.rearrange("p c n -> p (c n)")[:, 0:6 * P])
            TT("dve", cvf, cvf, sg_[:], ALU.mult, r=["g_cv", "g_qkv"], w=["g_cv"])
            sq = wk("g_sq")
            ACT(sq[:], cv[:, 0:4, :].rearrange("p c n -> p (c n)"), AF.Square, r=["g_cv"], w=["g_sq"])
            psn, kn_ = nps()
            for c in range(4):
                MM(psn[:, c * P:(c + 1) * P], cst["bones"][:], sq[:, c * P:(c + 1) * P], r=["g_sq", ("c", "bones")], w=[kn_])
            rn = sq
            ACT(rn[:], psn[:], AF.Ln, bias=EPS, r=[kn_], w=["g_sq"])
            ACT(rn[:], rn[:], AF.Exp, scale=-0.5, r=["g_sq"], w=["g_sq"])
            qkn = wk("g_qkn", (P, 4, P))
            TT("dve", qkn[:].rearrange("p c n -> p (c n)"), cv[:, 0:4, :].rearrange("p c n -> p (c n)"), rn[:], ALU.mult,
               r=["g_cv", "g_sq"], w=["g_qkn"])
            DMA("sp", gq_d[t, :, 0:512], qkn[:].rearrange("p c n -> p (c n)"), r=["g_qkn"], w=[("gq", t, 0)])
            DMA("sp", gq_d[t, :, 512:768].rearrange("p (c n) -> p c n", c=2), cv[:, 4:6, :], r=["g_cv"], w=[("gq", t, 1)])
        tap("gdn_qkn_%d_%d" % (d, t), qkn[:].rearrange("p c n -> p (c n)"), [P, 512], ["g_qkn"])
        for hh in range(2):
            pb = 64 * hh
            CP("pool", g_knz[pb:pb + 64, hh::2, :], qkn[pb:pb + 64, 2:4, :], r=["g_qkn"], w=["g_knz"])
        pst, kt = nps()
        for c in range(2):
            TR(pst[:, c * P:(c + 1) * P], qkn[:, 2 + c, :], ident[:], r=["g_qkn", KI], w=[kt])
            TR(pst[:, 256 + c * P:256 + (c + 1) * P], cv[:, 4 + c, :], ident[:], r=["g_cv", KI], w=[kt])
        vb = wk("g_vb", (P, 256))
        kbg = wk("g_kbg", (P, 256))
        ktl = wk("g_ktl", (P, 256))
        kn3 = pst[:, 0:256].rearrange("p (h e) -> p h e", e=64)
        TT("dve", vb[:].rearrange("p (h e) -> p h e", e=64), pst[:, 256:512].rearrange("p (h e) -> p h e", e=64),
           g8[:, 8:12].unsqueeze(2).to_broadcast([P, 4, 64]), ALU.mult, r=[kt, K8], w=["g_vb"])
        TT("dve", kbg[:].rearrange("p (h e) -> p h e", e=64), kn3, g8[:, 20:24].unsqueeze(2).to_broadcast([P, 4, 64]),
           ALU.mult, r=[kt, K8], w=["g_kbg"])
        TT("dve", ktl[:].rearrange("p (h e) -> p h e", e=64), kn3, g8[:, 24:28].unsqueeze(2).to_broadcast([P, 4, 64]),
           ALU.mult, r=[kt, K8], w=["g_ktl"])
        psG, kG = nps()
        for h in range(NH):
            c, pb = h // 2, 64 * (h % 2)
            MM(psG[:, h * P:(h + 1) * P], g_knz[:, h, :], qkn[:, 2 + c, :], r=["g_qkn", "g_knz"], w=[kG])
        STT(Nm[:], psG[:], -1.0, Nm[:], ALU.mult, ALU.mult, r=[kG, "g_N"], w=["g_N"])
        tap("gdn_N_%d_%d" % (d, t), Nm[:], [P, 512], ["g_N"])
        psT, kT_ = nps()
        for h in range(NH):
            TR(psT[:, h * P:(h + 1) * P], Nm[:, h * P:(h + 1) * P], ident[:], r=["g_N", KI], w=[kT_])
        CP("act", NmT[:], psT[:], r=[kT_], w=["g_NT"])
        TT("dve", X[:].rearrange("p (h i) -> p h i", h=4), psT[:].rearrange("p (h i) -> p h i", h=4),
           ident[:].unsqueeze(1).to_broadcast([P, 4, P]), ALU.add, r=[kT_, KI], w=["g_X"])
        side = []
        if need_out:
            EG = wk("g_EG", (P, 2, P))
            PT = wk("g_PT")

            def side1():
                psR, kR = nps()
                MM(psR[:], ones[:], dg[:], r=["g_dg", ("c", "ones")], w=[kR])
                for h in range(NH):
                    c, pb = h // 2, 64 * (h % 2)
                    ACT(EG[pb:pb + 64, c, :], psR[pb:pb + 64, h * P:(h + 1) * P], AF.Exp, r=[kR], w=["g_EG"])
                for hh in range(2):
                    pb = 64 * hh
                    STT(g_qdz[pb:pb + 64, hh::2, :], qkn[pb:pb + 64, 0:2, :], 0.125, EG[pb:pb + 64, :, :], ALU.mult, ALU.mult,
                        r=["g_qkn", "g_EG"], w=["g_qdz"])

            def side2():
                psP, kP = nps()
                MM(psP[:], ones[:], dg[:], start=True, stop=False, r=["g_dg", ("c", "ones")], w=[kP])
                MM(psP[:], ident[:], cst["gdn_mask"][:, 2 * d + 1, :], start=False, stop=True, r=[KI, ("c", "gdn_mask")], w=[kP])
                for h in range(NH):
                    ACT(PT[:, h * P:(h + 1) * P], psP[:, h * P:(h + 1) * P], AF.Exp, bias=g8[:, 32 + h:33 + h],
                        r=[kP, K8], w=["g_PT"])

            def side3():
                psq, kq = nps()
                for h in range(NH):
                    c, pb = h // 2, 64 * (h % 2)
                    MM(psq[:, h * P:(h + 1) * P], g_knz[:, h, :], qkn[:, c, :], r=["g_qkn", "g_knz"], w=[kq])
                STT(PT[:], psq[:], 0.125, PT[:], ALU.mult, ALU.mult, r=[kq, "g_PT"], w=["g_PT"])
            side = [side1, side2, side3]
        def sq_(k):
            ps1_, k1_ = nps()
            for h in range(NH):
                sl = slice(h * P, (h + 1) * P)
                MM(ps1_[:, sl], NmT[:, sl], Nm[:, sl], r=["g_N", "g_NT"], w=[k1_])
            ps2_, k2_ = None, None
            if k < 6:
                ps2_, k2_ = nps()
                for h in range(NH):
                    sl = slice(h * P, (h + 1) * P)
                    MM(ps2_[:, sl], Nm[:, sl], NmT[:, sl], r=["g_N", "g_NT"], w=[k2_])
            return ps1_, k1_, ps2_, k2_

        cur_sq = sq_(1)
        for k in range(1, 7):
            ps1_, k1_, ps2_, k2_ = cur_sq
            CP("act", Nm[:], ps1_[:], r=[k1_], w=["g_N"])
            if k < 6:
                CP("dve", NmT[:], ps2_[:], r=[k2_], w=["g_NT"])
                cur_sq = sq_(k + 1)
            ps3_, k3_ = nps()
            for h in range(NH):
                sl = slice(h * P, (h + 1) * P)
                MM(ps3_[:, sl], Nm[:, sl], X[:, sl], r=["g_N", "g_X"], w=[k3_])
            TT("dve", X[:], X[:], ps3_[:], ALU.add, r=["g_X", k3_], w=["g_X"])
            if side and k >= 2:
                side.pop(0)()
        while side:
            side.pop(0)()
        tap("gdn_XT_%d_%d" % (d, t), X[:], [P, 512], [kX])
        psu, ku = nps()
        for h in range(NH):
            MM(psu[:, h * 64:(h + 1) * 64], X[:, h * P:(h + 1) * P], vb[:, h * 64:(h + 1) * 64], r=[kX, "g_vb"], w=[ku])
        u_sb = wk("g_u", (P, 256))
        CP("act", u_sb[:], psu[:, 0:256], r=[ku], w=["g_u"])
        vnew = u_sb
        if not first:
            psw, kw = nps()
            for h in range(NH):
                c = h // 2
                MM(psw[:, h * P:(h + 1) * P], kbg[:, c * P:(c + 1) * P], X[:, h * P:(h + 1) * P], r=[kX, "g_kbg"], w=[kw])
            for hh in range(2):
                pb = 64 * hh
                src = psw[pb:pb + 64, :].rearrange("p (c x i) -> p c x i", c=2, x=2)[:, :, hh, :]
                CP("act", g_wz[pb:pb + 64, hh::2, :], src, r=[kw], w=["g_wz"])
            pws, kws = nps()
            for h in range(NH):
                c = h // 2
                MM(pws[:, h * 64:(h + 1) * 64], g_wz[:, h, :], S_c[:, c, :], r=["g_wz", kS_c], w=[kws])
            TT("dve", u_sb[:], u_sb[:], pws[:, 0:256], ALU.subtract, r=["g_u", kws], w=["g_u"])
        tap("gdn_vnew_%d_%d" % (d, t), vnew[:], [P, 256], ["g_u"])
        if need_out:
            pso, ko = nps()
            for h in range(NH):
                c, pb = h // 2, 64 * (h % 2)
                MM(pso[:, h * 64:(h + 1) * 64], PT[:, h * P:(h + 1) * P], vnew[:, h * 64:(h + 1) * 64],
                   start=True, stop=first, r=["g_PT", "g_u"], w=[ko])
                if not first:
                    MM(pso[:, h * 64:(h + 1) * 64], g_qdz[:, h, :], S_c[:, c, :],
                       start=False, stop=True, r=["g_qdz", kS_c], w=[ko])
        pss, ks = nps()
        for c in range(2):
            MM(pss[:, c * P:(c + 1) * P], ktl[:, c * P:(c + 1) * P], vnew[:, c * P:(c + 1) * P], r=["g_ktl", "g_u"], w=[ks])
        for hh in range(2):
            pb = 64 * hh
            src = pss[pb:pb + 64, 0:256].rearrange("p (c x) -> p c x", c=2)[:, :, pb:pb + 64]
            if first:
                CP("dve", S_n[pb:pb + 64, :, :], src, r=[ks], w=[kS_n])
            else:
                tmp = wk("g_stmp", (P, 2, 64))
                TT("dve", tmp[pb:pb + 64, :, :], S_c[pb:pb + 64, :, :],
                   cdP[pb:pb + 64, 0:2].unsqueeze(2).to_broadcast([64, 2, 64]), ALU.mult, r=[kS_c, "g_cdP"], w=["g_stmp"])
                TT("dve", S_n[pb:pb + 64, :, :], tmp[pb:pb + 64, :, :], src, ALU.add, r=["g_stmp", ks], w=[kS_n])
        if not need_out:
            return
        if not pass_b:
            o_sb = wk("g_of", (P, 256))
            CP("act", o_sb[:], pso[:, 0:256], r=[ko], w=["g_of"])
            DMA("sp", of_d[t, :, 256:512], o_sb[:], r=["g_of"], w=[("of", t, 1)])
            tap("gdn_of_%d" % t, o_sb[:], [P, 256], ["g_of"])
            return
        psz, kz = nps()
        proj_tok(t, 3584, 256, psz[:, 0:256], kz)
        ofl = wk("g_ofl", (P, 256))
        o = wk("g_o", (P, 256))
        TT("dve", o[:], pso[:, 0:256], ofl[:], ALU.add, r=[ko, "g_ofl"], w=["g_o"])
        tap("gdn_o_%d" % t, o[:], [P, 256], ["g_o"])
        o3 = o[:].rearrange("p (h e) -> p h e", e=64)
        s8 = wk("g_s8", (P, 16))
        sq2 = wk("g_sq2", (P, 256))
        ACT(sq2[:], o[:], AF.Square, r=["g_o"], w=["g_sq2"])
        RSUM(s8[:, 0:4], sq2[:].rearrange("p (h e) -> p h e", e=64), r=["g_sq2"], w=["g_s8"])
        ACT(s8[:, 4:8], s8[:, 0:4], AF.Ln, bias=EPS, scale=1.0 / 64, r=["g_s8"], w=["g_s8"])
        ACT(s8[:, 8:12], s8[:, 4:8], AF.Exp, scale=-0.5, r=["g_s8"], w=["g_s8"])
        TT("dve", o3, o3, s8[:, 8:12].unsqueeze(2).to_broadcast([P, 4, 64]), ALU.mult, r=["g_o", "g_s8"], w=["g_o"])
        TT("pool", o3, o3, rowp[:, RP_GDNG:RP_GDNG + 64].unsqueeze(1).to_broadcast([P, 4, 64]), ALU.mult,
           r=["g_o", "rowp"], w=["g_o"])
        sz = silu_of(psz[:, 0:256], [kz], "g_sz", (P, 256))
        TT("dve", sz[:], sz[:], psz[:, 0:256], ALU.mult, r=["g_sz", kz], w=["g_sz"])
        yb = wk("g_yb", (P, 256), BF16)
        TT("dve", yb[:], o[:], sz[:], ALU.mult, r=["g_o", "g_sz"], w=["g_yb"])
        for c in range(2):
            TR(psb[:, 512 + c * P:512 + (c + 1) * P], yb[:, c * P:(c + 1) * P], ident_b[:], r=["g_yb", "ident_b"], w=["psb"])
        CP("act", yT[:, 6:8, :], psb[:, 512:768].rearrange("p (c n) -> p c n", n=P), r=["psb"], w=[("yT", 3)])

    def outproj(li, s, t):
        r_ = 2 if t < TC else s
        xa, xk, s8, sk = load_x(li, s, t)
        pso = []
        for nt in range(2):
            ps, pk = nps()
            for kc in range(KC):
                MM(ps[:], yT[:, kc, :], wout[:, kc, nt * 512:(nt + 1) * 512], start=(kc == 0), stop=(kc == KC - 1),
                   r=[("yT", 0), ("yT", 1), ("yT", 2), ("yT", 3), "wout"], w=[pk])
            pso.append((ps, pk))
            ACT(wk("o_junk", (P, D))[:, nt * 512:(nt + 1) * 512], ps[:], AF.Square, accum=s8[:, 4 + nt:5 + nt], r=[pk], w=["o_junk", sk])
        TT("dve", s8[:, 0:1], s8[:, 4:5], s8[:, 5:6], ALU.add, r=[sk], w=[sk])
        rstd_from(s8, sk, 0, 1.0 / D)
        on = wk("o_on", (P, D))
        for nt in range(2):
            ps, pk = pso[nt]
            STT(on[:, nt * 512:(nt + 1) * 512], ps[:], s8[:, 2:3], Gb[:, r_, nt * 512:(nt + 1) * 512], ALU.mult, ALU.mult,
                r=[pk, sk, "Gb"], w=["o_on"])
        TT("pool", on[:], on[:], xa[:], ALU.add, r=["o_on", xk], w=["o_on"])
        dst, dk = dst_tile(li, s, t)
        DMA("sp", dst, on[:], r=["o_on"], w=[dk])

    for li, l in enumerate(layers):
        last = (l == DEPTH - 1)
        layer_setup(l)
        for s in range(n_seq):
            stage_h(li, s)
            orderF = list(range(NT))
            orderB = list(range(TC - 1, -1, -1)) + list(range(NT - 1, TC - 1, -1))
            segf = lambda t: t == 0 or t == TC
            segl = lambda t: t == TC - 1 or t == NT - 1
            if s == 0:
                tap("hT_%d" % li, hT[:, :, 1:NT * P + 1], [P, KC, NT * P], [("hT", t) for t in range(NT)], BF16)
            if "F" in passes:
                curR = curG = 0
                for n, t in enumerate(orderF):
                    need = not (last and t < TC)
                    if "ret" in branches:
                        ret_dir(0, t, n == 0, False, curR, need)
                        curR ^= 1
                    if "gdn" in branches:
                        gdn_dir(0, t, n == 0, False, curG, need, segf(t), segl(t))
                        curG ^= 1
            if "B" in passes:
                curR = curG = 0
                for n, t in enumerate(orderB):
                    need = not (last and t < TC)
                    if "ret" in branches:
                        ret_dir(1, t, n == 0, True, curR, need)
                        curR ^= 1
                    if "gdn" in branches:
                        gdn_dir(1, t, n == 0, True, curG, need, segf(t), segl(t))
                        curG ^= 1
                    if need:
                        if "sg" in branches:
                            sg_tile(t)
                        if "sc" in branches:
                            sc_tile(t, segf(t), segl(t))
                        tap("yT_%d_%d_%d" % (li, s, t), yT[:].rearrange("p k n -> p (k n)"), [P, KC * P],
                            [("yT", 0), ("yT", 1), ("yT", 2), ("yT", 3)], BF16)
                        if do_out:
                            outproj(li, s, t)
    if ps_last is None:
        st.close()
        return dict(pg.key_last)
    pg.emit()
    st.close()
    build.last_counts = {e: len(v) for e, v in pg.ops.items()}
    return nc, tap_out


def make_in_maps(inp, TC, TL, n_seq, n_cores):
    consts = _host_consts(TC, TL)
    rowp, pp, wsT = _host_layer_params(inp)
    shared = {
        "w_mod": np.ascontiguousarray(inp["w_mod"], np.float32),
        "b_mod": np.ascontiguousarray(inp["b_mod"], np.float32),
        "w_in": np.ascontiguousarray(inp["w_in"], np.float32),
        "w_out": np.ascontiguousarray(inp["w_out"], np.float32),
        "rowp": rowp, "pp": pp, "wsT": wsT,
    }
    for k, v in consts.items():
        shared["c_" + k] = np.ascontiguousarray(v, np.float32)
    maps = []
    for c in range(n_cores):
        b0 = c * n_seq
        cc = np.zeros((4, D), np.float32)
        cc[0:n_seq] = inp["c"][b0:b0 + n_seq]
        cc[2] = inp["c_ctx"]
        cT = np.ascontiguousarray(cc.reshape(4, KC, P).transpose(2, 1, 0)).reshape(P, KC * 4)
        m = dict(shared)
        m["x"] = np.ascontiguousarray(inp["x"][b0:b0 + n_seq], np.float32)
        m["ctx"] = np.ascontiguousarray(inp["ctx"][b0:b0 + n_seq], np.float32)
        m["cT"] = cT
        maps.append(m)
    return maps


_NC_CACHE = {}


N_LAUNCH = 1


def kernel(**inputs):
    inp = {k: np.asarray(v) for k, v in inputs.items()}
    B, T, _ = inp["x"].shape
    TL = T // P
    TC = inp["ctx"].shape[1] // P
    n_seq = B // N_CORES
    maps = make_in_maps(inp, TC, TL, n_seq, N_CORES)
    per = DEPTH // N_LAUNCH
    for g in range(N_LAUNCH):
        layers = list(range(g * per, (g + 1) * per))
        key = (TC, TL, n_seq, tuple(layers))
        if key not in _NC_CACHE:
            _NC_CACHE[key] = build(TC, TL, layers, n_seq=n_seq, ctx_ext=(N_LAUNCH > 1))[0]
        res = run_bass_kernel_spmd(_NC_CACHE[key], maps, core_ids=list(range(N_CORES)))
        if g < N_LAUNCH - 1:
            for c in range(N_CORES):
                maps[c]["x"] = np.asarray(res.results[c]["y"])
                maps[c]["ctx"] = np.asarray(res.results[c]["cs_scr"])
    out = np.concatenate([np.asarray(r["y"]) for r in res.results], axis=0)
    return out.astype(np.float32)
```

```python
import math
import numpy as np
import concourse.bass as bass
import concourse.mybir as mybir
from concourse.bass_utils import run_bass_kernel_spmd

F32 = mybir.dt.float32
BF16 = mybir.dt.bfloat16
AF = mybir.ActivationFunctionType
ALU = mybir.AluOpType
AX = mybir.AxisListType

P = 128
D = 1024
KC = 8
DEPTH = 4
BR = 256
HD = 64
NH = 4
IN_W = 3856
EPS = 1e-6
BIG = 30000.0
N_CORES = 8
import os
KDBG = os.environ.get("KDBG", "")


class Op:
    __slots__ = ("eng", "fn", "deps", "sig", "is_dma", "sem", "val", "needs_sig", "order")

    def __init__(self, eng, fn, is_dma=False):
        self.eng = eng
        self.fn = fn
        self.deps = []
        self.sig = None
        self.is_dma = is_dma
        self.sem = None
        self.val = None
        self.needs_sig = False
        self.order = 0


class Prog:
    ENGS = ("pe", "act", "dve", "pool", "sp")

    def __init__(self, nc):
        self.nc = nc
        self.ops = {e: [] for e in self.ENGS}
        self.last_w = {}
        self.readers = {}
        self.n = 0
        self.dma_slots = 24
        self.dma_rr = 0
        self.dma_last = [None] * self.dma_slots
        self.dma_cnt = [0] * self.dma_slots
        self.expand = {}
        self.key_last = {}
        self.expand["wbuf"] = [("wbuf", kc, c0) for kc in range(8) for c0 in range(0, 3856, 964)]
        self.expand["wout"] = [("wout", kc) for kc in range(8)]

    def _add(self, op, reads, writes):
        ex = self.expand
        for k in reads:
            if isinstance(k, tuple) and k[0] == "pa":
                self.key_last[k[1]] = self.n
        for k in writes:
            if isinstance(k, tuple) and k[0] == "pa":
                self.key_last[k[1]] = self.n
        reads = [x for k in reads for x in ex.get(k, (k,))]
        writes = [x for k in writes for x in ex.get(k, (k,))]
        deps = []
        for k in reads:
            w = self.last_w.get(k)
            if w is not None:
                deps.append(w)
            if k == "psb" or (isinstance(k, tuple) and k[0] == "ps"):
                for r in self.readers.get(k, ()):
                    if r.eng != op.eng:
                        deps.append(r)
        for k in writes:
            w = self.last_w.get(k)
            if w is not None:
                deps.append(w)
            for r in self.readers.get(k, ()):
                deps.append(r)
        seen = set()
        for d in deps:
            if d is op or id(d) in seen:
                continue
            if d.eng == "pe" and op.eng == "pe" and not d.is_dma and not op.is_dma:
                continue
            seen.add(id(d))
            op.deps.append(d)
            d.needs_sig = True
        for k in reads:
            self.readers.setdefault(k, []).append(op)
        for k in writes:
            self.last_w[k] = op
            self.readers[k] = []
        op.order = self.n
        self.n += 1
        self.ops[op.eng].append(op)
        return op

    def op(self, eng, fn, reads=(), writes=()):
        return self._add(Op(eng, fn), reads, writes)

    def dma(self, eng, out, in_, reads=(), writes=()):
        o = Op(eng, lambda e: e.dma_start(out=out, in_=in_), is_dma=True)
        s = self.dma_rr
        self.dma_rr = (self.dma_rr + 1) % self.dma_slots
        prev = self.dma_last[s]
        self.dma_cnt[s] += 1
        o.sem = s
        o.val = 16 * self.dma_cnt[s]
        o.needs_sig = True
        self._add(o, reads, writes)
        if prev is not None and prev not in o.deps:
            o.deps.append(prev)
        self.dma_last[s] = o
        return o

    def emit(self):
        nc = self.nc
        import contextlib
        with contextlib.ExitStack() as st:
            esem = {e: st.enter_context(nc.semaphore("s_" + e)) for e in ("pe", "act", "dve", "pool")}
            dsem = [st.enter_context(nc.semaphore("d%d" % i)) for i in range(self.dma_slots)]
            for e in ("pe", "act", "dve", "pool"):
                k = 0
                for o in self.ops[e]:
                    if o.is_dma:
                        continue
                    if o.needs_sig:
                        k += 1
                        o.sem = e
                        o.val = k
            block = st.enter_context(nc.Block())
            handles = {"pe": block.tensor, "act": block.scalar, "dve": block.vector,
                       "pool": block.gpsimd, "sp": block.sync}

            def make(ename):
                ops = self.ops[ename]

                def body(eng):
                    seen = {}
                    for o in ops:
                        for d in o.deps:
                            sem = dsem[d.sem] if d.is_dma else esem[d.sem]
                            key = ("d", d.sem) if d.is_dma else d.sem
                            if seen.get(key, 0) >= d.val:
                                continue
                            seen[key] = d.val
                            eng.wait_ge(sem, d.val)
                        ins = o.fn(eng)
                        if o.is_dma:
                            ins.then_inc(dsem[o.sem], 16)
                        elif o.needs_sig:
                            ins.then_inc(esem[o.sem], 1)
                    for s in range(self.dma_slots):
                        last = self.dma_last[s]
                        if last is not None and last.eng == ename:
                            if seen.get(("d", s), 0) < last.val:
                                eng.wait_ge(dsem[s], last.val)
                return body

            for e in self.ENGS:
                if self.ops[e]:
                    handles[e](make(e))


def _host_consts(TC, TL):
    NT = TC + TL
    c = {}
    c["ident"] = np.eye(P, dtype=np.float32)
    c["ones"] = np.ones((P, P), np.float32)
    bo = np.zeros((P, P), np.float32)
    bo[:64, :64] = 1.0
    bo[64:, 64:] = 1.0
    c["bones"] = bo
    sel = np.zeros((4, 4 * P), np.float32)
    for r in range(4):
        sel[r, r * P:(r + 1) * P] = 1.0
    c["sel"] = sel
    j = np.arange(P)[:, None]
    i = np.arange(P)[None, :]
    c["tri_f"] = (j <= i).astype(np.float32)
    c["tri_b"] = (j >= i).astype(np.float32)
    lg = np.log(1.0 - 2.0 ** (-5.0 - np.arange(NH, dtype=np.float64)))
    gam = np.exp(lg)
    pos = np.arange(P, dtype=np.float64)
    rm = np.zeros((2, P, NH, P), np.float32)
    rdec = np.zeros((2, P, 8), np.float32)
    for h in range(NH):
        ch = gam[h] ** (-128.0)
        rm[0, :, h, :] = ch * (i >= j)
        rm[1, :, h, :] = ch * (i <= j)
        rdec[0, :, h] = gam[h] ** (pos + 1.0)
        rdec[0, :, 4 + h] = 0.125 * gam[h] ** (127.0 - pos)
        rdec[1, :, h] = gam[h] ** (128.0 - pos)
        rdec[1, :, 4 + h] = 0.125 * gam[h] ** pos
    c["ret_mask"] = rm.reshape(2, P, NH * P)
    c["ret_dec"] = rdec
    cd = np.zeros((P, 2, 64), np.float32)
    for h in range(NH):
        cd[64 * (h % 2):64 * (h % 2) + 64, h // 2, :] = gam[h] ** 128.0
    c["ret_cdec_p"] = cd.reshape(P, 128)
    pi = np.arange(P)[:, None]
    fj = np.arange(P)[None, :]
    gm = np.zeros((2, 2, P, NH, P), np.float32)
    gm[0, 0] = (BIG * (fj >= pi))[:, None, :]
    gm[1, 0] = (BIG * (fj <= pi))[:, None, :]
    gm[0, 1] = (-BIG * (fj < pi))[:, None, :]
    gm[1, 1] = (-BIG * (fj > pi))[:, None, :]
    c["gdn_mask"] = gm.reshape(4, P, NH * P)
    nf = 16
    inv = (np.float32(10000.0) ** (-np.arange(nf, dtype=np.float32) / np.float32(nf))).astype(np.float32)
    C = np.ones((NT, P, 64), np.float32)
    S = np.zeros((NT, P, 64), np.float32)
    for t in range(TL):
        tok = np.arange(t * P, (t + 1) * P)
        row = (tok // 64).astype(np.float32)
        col = (tok % 64).astype(np.float32)
        ar = (row[:, None] * inv[None, :]).astype(np.float32)
        ac = (col[:, None] * inv[None, :]).astype(np.float32)
        C[TC + t] = np.concatenate([np.cos(ar), np.cos(ar), np.cos(ac), np.cos(ac)], axis=1)
        S[TC + t] = np.concatenate([-np.sin(ar), np.sin(ar), -np.sin(ac), np.sin(ac)], axis=1)
    c["rot_c"] = np.ascontiguousarray(C.transpose(1, 0, 2)).reshape(P, NT * 64)
    c["rot_s"] = np.ascontiguousarray(S.transpose(1, 0, 2)).reshape(P, NT * 64)
    return c


CONST_SHAPES = lambda NT: {
    "ident": [P, P], "ones": [P, P], "bones": [P, P], "sel": [4, 4 * P], "tri_f": [P, P], "tri_b": [P, P],
    "ret_mask": [2, P, NH * P], "ret_dec": [2, P, 8], "ret_cdec_p": [P, P], "gdn_mask": [4, P, NH * P],
    "rot_c": [P, NT * 64], "rot_s": [P, NT * 64],
}

RP_RETG = 0
RP_GDNG = 256
RP_ALOG = 320
RP_DTB = 328
RP_SGB = 336
RP_W = 1024 + 336 + 512
PP_GPRE = 0
PP_SCW = 8
PP_GDW = 14
PP_W = 32


def _host_layer_params(inp):
    L = inp["w_in"].shape[0]
    rowp = np.zeros((L, RP_W), np.float32)
    O = 1024
    rowp[:, 0:1024] = inp["g_post"]
    rowp[:, O + RP_RETG:O + RP_RETG + 256] = inp["ret_norm_g"]
    rowp[:, O + RP_GDNG:O + RP_GDNG + 64] = inp["gdn_norm_g"]
    rowp[:, O + RP_ALOG:O + RP_ALOG + 8] = inp["gdn_a_log"].reshape(L, 8)
    rowp[:, O + RP_DTB:O + RP_DTB + 8] = inp["gdn_dt_bias"].reshape(L, 8)
    rowp[:, O + RP_SGB:O + RP_SGB + 512] = inp["sg_b"].reshape(L, 512)
    pp = np.zeros((L, P, PP_W), np.float32)
    pp[:, :, PP_GPRE:PP_GPRE + 8] = inp["g_pre"].reshape(L, 8, P).transpose(0, 2, 1)
    pp[:, :, PP_SCW:PP_SCW + 6] = inp["sc_conv_w"].reshape(L, 3, 2, P).transpose(0, 3, 2, 1).reshape(L, P, 6)
    pp[:, :, PP_GDW:PP_GDW + 18] = inp["gdn_conv_w"].reshape(L, 3, 6, P).transpose(0, 3, 2, 1).reshape(L, P, 18)
    wsT = np.ascontiguousarray(inp["sg_w"].transpose(0, 3, 1, 2)).reshape(L, P, 4 * P)
    return rowp, pp, wsT


def build(TC, TL, layers, branches=("ret", "sg", "sc", "gdn"), taps=(), n_seq=2, do_out=True, passes=("F", "B"),
          ctx_ext=False):
    ps_last = _build(TC, TL, layers, branches, taps, n_seq, do_out, passes, None, ctx_ext)
    return _build(TC, TL, layers, branches, taps, n_seq, do_out, passes, ps_last, ctx_ext)


def _build(TC, TL, layers, branches, taps, n_seq, do_out, passes, ps_last, ctx_ext=False):
    import contextlib
    NT = TC + TL
    L = len(layers)
    nc = bass.Bass("TRN2", target_bir_lowering=False)
    pg = Prog(nc)
    st = contextlib.ExitStack()
    tap_out = {}

    def din(name, shape, dt=F32):
        return nc.dram_tensor(name, list(shape), dt, kind="ExternalInput").ap()

    x_d = din("x", [n_seq, TL * P, D])
    ctx_d = din("ctx", [n_seq, TC * P, D])
    cT_d = din("cT", [P, KC * 4])
    wmod_d = din("w_mod", [DEPTH, D, 3 * D])
    bmod_d = din("b_mod", [DEPTH, 3 * D])
    win_d = din("w_in", [DEPTH, D, IN_W])
    wout_d = din("w_out", [DEPTH, D, D])
    rowp_d = din("rowp", [DEPTH, RP_W])
    pp_d = din("pp", [DEPTH, P, PP_W])
    wsT_d = din("wsT", [DEPTH, P, 4 * P])
    cds = {k: din("c_" + k, shp) for k, shp in CONST_SHAPES(NT).items()}
    y_d = nc.dram_tensor("y", [n_seq, TL * P, D], F32, kind="ExternalOutput").ap()
    xs_d = nc.dram_tensor("xs_scr", [n_seq, TL * P, D], F32, kind="Internal").ap()
    cs_d = nc.dram_tensor("cs_scr", [n_seq, TC * P, D], F32, kind="ExternalOutput" if ctx_ext else "Internal").ap()
    of_d = nc.dram_tensor("of_scr", [NT, P, 512], F32, kind="Internal").ap()
    gq_d = nc.dram_tensor("gq_scr", [NT, P, 768], F32, kind="Internal").ap()

    def sb(name, shape, dt=F32):
        return st.enter_context(nc.sbuf_tensor("sb_" + name, list(shape), dt))

    def psum(name, shape, dt=F32):
        return st.enter_context(nc.psum_tensor(name, list(shape), dt))

    def MM(out, lhsT, rhs, start=True, stop=True, r=(), w=()):
        pg.op("pe", lambda e: e.matmul(out, lhsT, rhs, start=start, stop=stop), r, w)

    def TR(out, in_, idn, r=(), w=()):
        pg.op("pe", lambda e: e.transpose(out, in_, idn), r, w)

    def ACT(out, in_, func, bias=None, scale=None, accum=None, r=(), w=()):
        kw = {}
        if bias is not None:
            kw["bias"] = bias
        if scale is not None:
            kw["scale"] = scale
        if accum is not None:
            kw["accum_out"] = accum
        pg.op("act", lambda e: e.activation(out, in_, func, **kw), r, w)

    def TT(eng, out, a, b, op, r=(), w=()):
        pg.op(eng, lambda e: e.tensor_tensor(out, a, b, op), r, w)

    def TS(eng, out, a, s1, s2, op0, op1=None, r=(), w=()):
        if op1 is None:
            pg.op(eng, lambda e: e.tensor_scalar(out, a, s1, None, op0), r, w)
        else:
            pg.op(eng, lambda e: e.tensor_scalar(out, a, s1, s2, op0, op1), r, w)

    def STT(out, in0, scalar, in1, op0, op1, r=(), w=()):
        pg.op("dve", lambda e: e.scalar_tensor_tensor(out, in0, scalar, in1, op0, op1), r, w)

    def CP(eng, out, in_, r=(), w=()):
        if eng == "act":
            pg.op("act", lambda e: e.copy(out, in_), r, w)
        else:
            pg.op(eng, lambda e: e.tensor_copy(out, in_), r, w)

    def RSUM(out, in_, r=(), w=()):
        pg.op("dve", lambda e: e.reduce_sum(out, in_, AX.X), r, w)

    def RECIP(out, in_, r=(), w=()):
        pg.op("dve", lambda e: e.reciprocal(out, in_), r, w)

    def MSET(eng, ap, val, w=()):
        pg.op(eng, lambda e: e.memset(ap, val), (), w)

    def DMA(eng, out, in_, r=(), w=()):
        pg.dma(eng, out, in_, r, w)

    def tap(name, ap, shape, r, dt=F32):
        if name not in taps:
            return
        d = nc.dram_tensor("tap_" + name, list(shape), dt, kind="ExternalOutput").ap()
        tap_out[name] = "tap_" + name
        DMA("sp", d, ap, r=r, w=[("tap", name)])

    NPS = 7
    psf = [psum("psf%d" % i, [P, 512]) for i in range(NPS)]
    psb = psum("psb", [P, 1024], BF16)
    ps_rr = [0]
    ps_cnt = [0]
    ps_occ = [None] * NPS

    def nps():
        a = ps_cnt[0]
        ps_cnt[0] += 1
        if ps_last is None:
            i = a % NPS
        else:
            i = None
            for step in range(NPS):
                j = (ps_rr[0] + step) % NPS
                occ = ps_occ[j]
                if occ is None or ps_last.get(occ, -1) < pg.n:
                    i = j
                    break
            assert i is not None, "out of PSUM banks"
            ps_rr[0] = (i + 1) % NPS
        ps_occ[i] = a
        pg.expand[("pa", a)] = [("ps", i)]
        return psf[i], ("pa", a)

    cst = {}
    for k, shp in CONST_SHAPES(NT).items():
        if k in ("rot_c", "rot_s"):
            continue
        if len(shp) == 3:
            t_ = sb("k_" + k, [shp[1], shp[0], shp[2]])
            for a in range(shp[0]):
                DMA("sp", t_[:, a, :], cds[k][a], w=[("c", k)])
        else:
            t_ = sb("k_" + k, shp)
            DMA("sp", t_[:], cds[k], w=[("c", k)])
        cst[k] = t_
    ident = cst["ident"]
    ones = cst["ones"]
    ident_b = sb("ident_b", [P, P], BF16)
    CP("dve", ident_b[:], ident[:], r=[("c", "ident")], w=["ident_b"])
    KI = ("c", "ident")

    siluT = sb("siluT", [P, KC, 4])
    AT = sb("AT", [P, KC, 4])
    shT = sb("shT", [P, KC, 4])
    Gb = sb("Gb", [P, 3, D])
    rowp = sb("rowp", [P, RP_W - 1024])
    ppp = sb("ppp", [P, PP_W])
    wsT = sb("wsT", [P, 4, P])
    bsT = sb("bsT", [P, 2, P])
    negA = sb("negA", [P, 8])
    hT = sb("hT", [P, KC, NT * P + 2], BF16)
    wbuf = sb("wbuf", [P, KC, IN_W], BF16)
    wout = sb("wout", [P, KC, D], BF16)
    xt = [sb("xt%d" % i, [P, D]) for i in range(2)]
    st8 = [sb("st8_%d" % i, [P, 8]) for i in range(2)]
    yT = sb("yT", [P, KC, P], BF16)
    Sret = [sb("Sret%d" % i, [P, 2, 64]) for i in range(2)]
    Sgdn = [sb("Sgdn%d" % i, [P, 2, 64]) for i in range(2)]
    r_qz = sb("r_qz", [P, 4, P])
    g_knz = sb("g_knz", [P, 4, P])
    g_wz = sb("g_wz", [P, 4, P])
    g_qdz = sb("g_qdz", [P, 4, P])

    GR = 128
    ARENA = 36 * 1024 // 4
    arena = sb("arena", [P, ARENA])
    wk_cache = {}
    scope_ptr = {}

    def wk(name, shape=(P, 512), dt=F32):
        if name not in wk_cache:
            scope = name.split("_")[0]
            n = 1
            for d_ in shape[1:]:
                n *= d_
            if dt != F32:
                n = n // 2
            ng = (n + GR - 1) // GR
            off = scope_ptr.get(scope, 0)
            scope_ptr[scope] = off + ng * GR
            assert off + ng * GR <= ARENA, (name, off, ng * GR)
            ap = arena[0:shape[0], off:off + n]
            if dt != F32:
                ap = ap.bitcast(dt)
            if len(shape) == 3:
                ap = ap.rearrange("p (a b) -> p a b", b=shape[2])
            pg.expand[name] = [("ar", g) for g in range(off // GR, off // GR + ng)]
            wk_cache[name] = ap
        return wk_cache[name]

    cT = wk("cT", (P, KC * 4))
    DMA("sp", cT[:], cT_d, w=["cT"])
    e_ = wk("cT_e", (P, KC * 4))
    ACT(e_[:], cT[:], AF.Exp, scale=-1.0, r=["cT"], w=["cT_e"])
    ACT(e_[:], e_[:], AF.Ln, bias=1.0, r=["cT_e"], w=["cT_e"])
    ACT(e_[:], e_[:], AF.Exp, scale=-1.0, r=["cT_e"], w=["cT_e"])
    TT("dve", siluT[:].rearrange("p k r -> p (k r)"), cT[:], e_[:], ALU.mult, r=["cT", "cT_e"], w=["siluT"])
    MSET("dve", yT[:], 0.0, w=[("yT", 0), ("yT", 1), ("yT", 2), ("yT", 3)])
    MSET("dve", r_qz[:], 0.0, w=["r_qz"])
    MSET("dve", g_knz[:], 0.0, w=["g_knz"])
    MSET("dve", g_wz[:], 0.0, w=["g_wz"])
    MSET("dve", g_qdz[:], 0.0, w=["g_qdz"])
    MSET("dve", hT[:, :, 0:1], 0.0, w=["hTpadL"])
    MSET("dve", hT[:, :, NT * P + 1:NT * P + 2], 0.0, w=["hTpadR"])

    def layer_setup(l):
        DMA("sp", rowp[:], rowp_d[l, 1024:RP_W].partition_broadcast(P), w=["rowp"])
        DMA("sp", ppp[:], pp_d[l], w=["ppp"])
        DMA("sp", wsT[:].rearrange("p h i -> p (h i)"), wsT_d[l], w=["wsT"])
        wv = win_d[l].rearrange("(kc p) n -> p kc n", p=P)
        wov = wout_d[l].rearrange("(kc p) n -> p kc n", p=P)
        i_ = 0
        for kc in range(KC):
            for c0 in range(0, IN_W, 964):
                nm = "m_s%d" % (i_ % 2)
                stg = wk(nm, (P, 1024))
                DMA("sp", stg[:, 0:964], wv[:, kc, c0:c0 + 964], w=[nm])
                CP("act" if i_ % 2 else "dve", wbuf[:, kc, c0:c0 + 964], stg[:, 0:964], r=[nm], w=[("wbuf", kc, c0)])
                i_ += 1
            nm = "m_s%d" % (i_ % 2)
            stg = wk(nm, (P, 1024))
            DMA("sp", stg[:], wov[:, kc, :], w=[nm])
            CP("act" if i_ % 2 else "dve", wout[:, kc, :], stg[:], r=[nm], w=[("wout", kc)])
            i_ += 1
        gp = wk("m_gp", (P, D))
        DMA("sp", gp[:], rowp_d[l, 0:1024].partition_broadcast(P), w=["m_gp"])
        wmv = wmod_d[l].rearrange("(kc p) n -> p kc n", p=P)
        psT, kT = nps()
        for nt in range(12):
            b = nt % 2
            wt = wk("m_w%d" % b, (P, KC, 256))
            bm = wk("m_b%d" % b, (4, 256))
            mr = wk("m_r%d" % b, (4, 256))
            DMA("sp", wt[:], wmv[:, :, nt * 256:(nt + 1) * 256], w=["m_w%d" % b])
            DMA("sp", bm[:], bmod_d[l, nt * 256:(nt + 1) * 256].partition_broadcast(4), w=["m_b%d" % b])
            ps, pk = nps()
            for kc in range(KC):
                MM(ps[0:4, 0:256], siluT[:, kc, :], wt[:, kc, :], start=(kc == 0), stop=(kc == KC - 1),
                   r=["siluT", "m_w%d" % b], w=[pk])
            TT("dve", mr[:], ps[0:4, 0:256], bm[:], ALU.add, r=[pk, "m_b%d" % b], w=["m_r%d" % b])
            g, j = nt // 4, nt % 4
            if g < 2:
                for q in range(2):
                    kc = 2 * j + q
                    TR(psT[:, (g * KC + kc) * 4:(g * KC + kc) * 4 + 4], mr[0:4, q * P:(q + 1) * P], ident[0:4, 0:4],
                       r=["m_r%d" % b, KI], w=[kT])
            else:
                for r_ in range(3):
                    ps2, pk2 = nps()
                    MM(ps2[:, 0:256], cst["sel"][0:4, r_ * P:(r_ + 1) * P], mr[0:4, :], r=["m_r%d" % b, ("c", "sel")], w=[pk2])
                    TT("dve", Gb[:, r_, j * 256:(j + 1) * 256], ps2[:, 0:256], gp[:, j * 256:(j + 1) * 256], ALU.mult,
                       r=[pk2, "m_gp"], w=["Gb"])
        CP("dve", shT[:].rearrange("p k r -> p (k r)"), psT[:, 0:32], r=[kT], w=["shT"])
        STT(AT[:], psT[:, 32:64].rearrange("p (k r) -> p k r", r=4), 1.0,
            ppp[:, PP_GPRE:PP_GPRE + 8].unsqueeze(2).to_broadcast([P, KC, 4]), ALU.add, ALU.mult,
            r=[kT, "ppp"], w=["AT"])
        for c in range(2):
            for hh in range(2):
                h = 2 * c + hh
                CP("dve", bsT[64 * hh:64 * hh + 64, c, :], rowp[64 * hh:64 * hh + 64, RP_SGB + h * P:RP_SGB + (h + 1) * P],
                   r=["rowp"], w=["bsT"])
        ACT(negA[:], rowp[:, RP_ALOG:RP_ALOG + 8], AF.Exp, r=["rowp"], w=["negA"])
        TS("dve", negA[:], negA[:], -1.0, None, ALU.mult, r=["negA"], w=["negA"])
        tap("AT%d" % l, AT[:].rearrange("p k r -> p (k r)"), [P, 32], ["AT"])
        tap("shT%d" % l, shT[:].rearrange("p k r -> p (k r)"), [P, 32], ["shT"])
        tap("Gb%d" % l, Gb[:].rearrange("p a n -> p (a n)"), [P, 3 * D], ["Gb"])

    def src_tile(li, s, t):
        if t < TC:
            base = ctx_d if li == 0 else cs_d
            return base[s, t * P:(t + 1) * P, :], ("dx", "c", s, t)
        tt = t - TC
        base = x_d if li == 0 else xs_d
        return base[s, tt * P:(tt + 1) * P, :], ("dx", "x", s, tt)

    def dst_tile(li, s, t):
        if t < TC:
            return cs_d[s, t * P:(t + 1) * P, :], ("dx", "c", s, t)
        tt = t - TC
        base = y_d if li == L - 1 else xs_d
        return base[s, tt * P:(tt + 1) * P, :], ("dx", "x", s, tt)

    xt_rr = [0]

    def load_x(li, s, t):
        b = xt_rr[0]
        xt_rr[0] ^= 1
        src, dk = src_tile(li, s, t)
        DMA("sp", xt[b][:], src, r=[dk], w=[("xt", b)])
        return xt[b], ("xt", b), st8[b], ("st8", b)

    def rstd_from(s8, sk, col_in, scale):
        ACT(s8[:, col_in + 1:col_in + 2], s8[:, col_in:col_in + 1], AF.Ln, bias=EPS, scale=scale, r=[sk], w=[sk])
        ACT(s8[:, col_in + 2:col_in + 3], s8[:, col_in + 1:col_in + 2], AF.Exp, scale=-0.5, r=[sk], w=[sk])

    def stage_h(li, s):
        for t in range(NT):
            r_ = 2 if t < TC else s
            xa, xk, s8, sk = load_x(li, s, t)
            ACT(wk("h_junk", (P, D))[:], xa[:], AF.Square, accum=s8[:, 0:1], r=[xk], w=["h_junk", sk])
            rstd_from(s8, sk, 0, 1.0 / D)
            xn = wk("h_xn", (P, D), BF16)
            TS("dve", xn[:], xa[:], s8[:, 2:3], None, ALU.mult, r=[xk, sk], w=["h_xn"])
            for kc in range(KC):
                TR(psb[:, kc * P:(kc + 1) * P], xn[:, kc * P:(kc + 1) * P], ident_b[:], r=["h_xn", "ident_b"], w=["psb"])
            for kc in range(KC):
                o_ = hT[:, kc, 1 + t * P:1 + (t + 1) * P]
                i_ = psb[:, kc * P:(kc + 1) * P]
                if kc % 2:
                    ACT(o_, i_, AF.Identity, bias=shT[:, kc, r_:r_ + 1], scale=AT[:, kc, r_:r_ + 1],
                        r=["psb", "AT", "shT"], w=[("hT", t)])
                else:
                    TS("dve", o_, i_, AT[:, kc, r_:r_ + 1], shT[:, kc, r_:r_ + 1], ALU.mult, ALU.add,
                       r=["psb", "AT", "shT"], w=[("hT", t)])

    def proj_tok(t, c0, n, ps_ap, pk):
        for kc in range(KC):
            MM(ps_ap, hT[:, kc, 1 + t * P:1 + (t + 1) * P], wbuf[:, kc, c0:c0 + n],
               start=(kc == 0), stop=(kc == KC - 1), r=[("hT", t), "wbuf"], w=[pk])

    def hkeys(t):
        return [("hT", t), ("hT", t - 1) if t > 0 else "hTpadL", ("hT", t + 1) if t < NT - 1 else "hTpadR"]

    def proj_feat(t, c0, ps_ap, pk, halo):
        lo, n = (t * P, P + 2) if halo else (t * P + 1, P)
        for kc in range(KC):
            MM(ps_ap, wbuf[:, kc, c0:c0 + P], hT[:, kc, lo:lo + n],
               start=(kc == 0), stop=(kc == KC - 1), r=(hkeys(t) if halo else [("hT", t)]) + ["wbuf"], w=[pk])

    def silu_of(src, src_keys, name, shape, tile=None):
        e = tile if tile is not None else wk(name, shape)
        ACT(e[:], src, AF.Exp, scale=-1.0, r=src_keys, w=[name])
        ACT(e[:], e[:], AF.Ln, bias=1.0, r=[name], w=[name])
        ACT(e[:], e[:], AF.Exp, scale=-1.0, r=[name], w=[name])
        return e

    def gelu_of(src, src_keys, name, shape):
        xs = wk(name + "_x", shape)
        CP("act", xs[:], src, r=src_keys, w=[name + "_x"])
        a = wk(name + "_a", shape)
        ACT(a[:], xs[:], AF.Square, r=[name + "_x"], w=[name + "_a"])
        TS("dve", a[:], a[:], 0.044715, 1.0, ALU.mult, ALU.add, r=[name + "_a"], w=[name + "_a"])
        TT("pool", a[:], a[:], xs[:], ALU.mult, r=[name + "_a", name + "_x"], w=[name + "_a"])
        ACT(a[:], a[:], AF.Exp, scale=-1.5957691216057308, r=[name + "_a"], w=[name + "_a"])
        ACT(a[:], a[:], AF.Ln, bias=1.0, r=[name + "_a"], w=[name + "_a"])
        ACT(a[:], a[:], AF.Exp, scale=-1.0, r=[name + "_a"], w=[name + "_a"])
        TT("dve", xs[:], xs[:], a[:], ALU.mult, r=[name + "_a", name + "_x"], w=[name + "_x"])
        return xs, name + "_x"

    def ret_dir(d, t, first, pass_b, cur, need_out):
        S_c, S_n = Sret[cur], Sret[cur ^ 1]
        kS_c, kS_n = ("Sret", cur), ("Sret", cur ^ 1)
        ps1, k1 = nps()
        ps2, k2 = nps()
        proj_tok(t, 0, 512, ps1[:], k1)
        nv = 512 if pass_b else 256
        proj_tok(t, 512, nv, ps2[:, 0:nv], k2)
        C = wk("r_rc", (P, 64))
        Sg_ = wk("r_rs", (P, 64))
        DMA("sp", C[:], cds["rot_c"][:, t * 64:(t + 1) * 64], w=["r_rc"])
        DMA("sp", Sg_[:], cds["rot_s"][:, t * 64:(t + 1) * 64], w=["r_rs"])
        Sg = Sg_.rearrange("p (a s f) -> p a s f", a=2, s=2, f=16)
        t1 = wk("r_t1")
        t2 = wk("r_t2")
        qk = wk("r_qk")
        TT("dve", t1[:].rearrange("p (h d) -> p h d", d=64), ps1[:].rearrange("p (h d) -> p h d", d=64),
           C.unsqueeze(1).to_broadcast([P, 8, 64]), ALU.mult, r=[k1, "r_rc"], w=["r_t1"])
        x5 = ps1[:].rearrange("p (h a s f) -> p h a s f", a=2, s=2, f=16)
        t5 = t2[:].rearrange("p (h a s f) -> p h a s f", a=2, s=2, f=16)
        for s_ in range(2):
            TT("dve", t5[:, :, :, s_, :], x5[:, :, :, 1 - s_, :],
               Sg[:, :, s_, :].unsqueeze(1).to_broadcast([P, 8, 2, 16]), ALU.mult,
               r=[k1, "r_rs"], w=["r_t2"])
        TT("pool", t1[:], t1[:], t2[:], ALU.add, r=["r_t1", "r_t2"], w=["r_t1"])
        TT("dve", qk[:].rearrange("p (h d) -> p h d", d=64), t1[:].rearrange("p (h d) -> p h d", d=64),
           cst["ret_dec"][:, d, :].unsqueeze(2).to_broadcast([P, 8, 64]), ALU.mult,
           r=["r_t1", ("c", "ret_dec")], w=["r_qk"])
        v_sb = wk("r_v", (P, 256))
        CP("act", v_sb[:], ps2[:, 0:256], r=[k2], w=["r_v"])
        if "rstop1" in KDBG:
            return
        psq, kq = nps()
        for c in range(4):
            TR(psq[:, c * P:(c + 1) * P], qk[:, c * P:(c + 1) * P], ident[:], r=["r_qk", KI], w=[kq])
        if "rstopA" in KDBG:
            return
        kT = wk("r_kT", (P, 2, P))
        CP("act", kT[:], psq[:, 256:512].rearrange("p (c n) -> p c n", c=2), r=[kq], w=["r_kT"])
        if "rstopB" in KDBG:
            return
        for hh in range(2):
            pb = 64 * hh
            if "qz2d" in KDBG:
                for c in range(2):
                    CP("dve", r_qz[pb:pb + 64, 2 * c + hh, :], psq[pb:pb + 64, c * P:(c + 1) * P], r=[kq], w=["r_qz"])
            else:
                CP("act" if "qzact" in KDBG else "dve", r_qz[pb:pb + 64, hh::2, :],
                   psq[pb:pb + 64, 0:256].rearrange("p (c n) -> p c n", c=2), r=[kq], w=["r_qz"])
        if "rstop2" in KDBG:
            return
        pss, ks = nps()
        for h in range(NH):
            c = h // 2
            MM(pss[:, h * P:(h + 1) * P], kT[:, c, :], r_qz[:, h, :], r=["r_kT", "r_qz"], w=[ks])
        SmT = wk("r_sm")
        TT("dve", SmT[:], pss[:], cst["ret_mask"][:, d, :], ALU.mult, r=[ks, ("c", "ret_mask")], w=["r_sm"])
        if "rstop3" in KDBG:
            return
        if need_out:
            pso, ko = nps()
            for h in range(NH):
                c, pb = h // 2, 64 * (h % 2)
                MM(pso[:, h * 64:(h + 1) * 64], SmT[:, h * P:(h + 1) * P], v_sb[:, h * 64:(h + 1) * 64],
                   start=True, stop=first, r=["r_sm", "r_v"], w=[ko])
                if not first:
                    MM(pso[:, h * 64:(h + 1) * 64], r_qz[:, h, :], S_c[:, c, :],
                       start=False, stop=True, r=["r_qz", kS_c], w=[ko])
        if "rstop4" in KDBG:
            return
        psu, ku = nps()
        for c in range(2):
            MM(psu[:, c * P:(c + 1) * P], qk[:, 256 + c * P:256 + (c + 1) * P], v_sb[:, c * P:(c + 1) * P],
               r=["r_qk", "r_v"], w=[ku])
        for hh in range(2):
            pb = 64 * hh
            src = psu[pb:pb + 64, 0:256].rearrange("p (c x) -> p c x", c=2)[:, :, pb:pb + 64]
            if first:
                CP("dve", S_n[pb:pb + 64, :, :], src, r=[ku], w=[kS_n])
            else:
                tmp = wk("r_stmp", (P, 2, 64))
                TT("pool", tmp[pb:pb + 64, :, :], S_c[pb:pb + 64, :, :],
                   cst["ret_cdec_p"][pb:pb + 64, :].rearrange("p (c e) -> p c e", e=64), ALU.mult,
                   r=[kS_c, ("c", "ret_cdec_p")], w=["r_stmp"])
                TT("dve", S_n[pb:pb + 64, :, :], tmp[pb:pb + 64, :, :], src, ALU.add, r=["r_stmp", ku], w=[kS_n])
        if not need_out or "rstop5" in KDBG:
            return None
        if not pass_b:
            o_sb = wk("r_of", (P, 256))
            CP("act", o_sb[:], pso[:, 0:256], r=[ko], w=["r_of"])
            if "noof" not in KDBG:
                DMA("sp", of_d[t, :, 0:256], o_sb[:], r=["r_of"], w=[("of", t, 0)])
            tap("ret_of_%d" % t, o_sb[:], [P, 256], ["r_of"])
            return None
        ofl = wk("r_ofl", (P, 256))
        DMA("sp", ofl[:], of_d[t, :, 0:256], r=[("of", t, 0)], w=["r_ofl"])
        o = wk("r_o", (P, 256))
        TT("dve", o[:], pso[:, 0:256], ofl[:], ALU.add, r=[ko, "r_ofl"], w=["r_o"])
        tap("ret_o_%d" % t, o[:], [P, 256], ["r_o"])
        o3 = o[:].rearrange("p (h e) -> p h e", e=64)
        s8 = wk("r_s8", (P, 16))
        RSUM(s8[:, 0:4], o3, r=["r_o"], w=["r_s8"])
        TS("dve", s8[:, 0:4], s8[:, 0:4], 1.0 / 64, None, ALU.mult, r=["r_s8"], w=["r_s8"])
        TT("dve", o3, o3, s8[:, 0:4].unsqueeze(2).to_broadcast([P, 4, 64]), ALU.subtract, r=["r_o", "r_s8"], w=["r_o"])
        sq = wk("r_sq", (P, 256))
        ACT(sq[:], o[:], AF.Square, r=["r_o"], w=["r_sq"])
        RSUM(s8[:, 4:8], sq[:].rearrange("p (h e) -> p h e", e=64), r=["r_sq"], w=["r_s8"])
        ACT(s8[:, 8:12], s8[:, 4:8], AF.Ln, bias=EPS, scale=1.0 / 64, r=["r_s8"], w=["r_s8"])
        ACT(s8[:, 12:16], s8[:, 8:12], AF.Exp, scale=-0.5, r=["r_s8"], w=["r_s8"])
        TT("dve", o3, o3, s8[:, 12:16].unsqueeze(2).to_broadcast([P, 4, 64]), ALU.mult, r=["r_o", "r_s8"], w=["r_o"])
        TT("pool", o[:], o[:], rowp[:, RP_RETG:RP_RETG + 256], ALU.mult, r=["r_o", "rowp"], w=["r_o"])
        sg_ = silu_of(ps2[:, 256:512], [k2], "r_sz", (P, 256))
        TT("dve", sg_[:], sg_[:], ps2[:, 256:512], ALU.mult, r=["r_sz", k2], w=["r_sz"])
        yb = wk("r_yb", (P, 256), BF16)
        TT("dve", yb[:], o[:], sg_[:], ALU.mult, r=["r_o", "r_sz"], w=["r_yb"])
        for c in range(2):
            TR(psb[:, c * P:(c + 1) * P], yb[:, c * P:(c + 1) * P], ident_b[:], r=["r_yb", "ident_b"], w=["psb"])
        CP("act", yT[:, 0:2, :], psb[:, 0:256].rearrange("p (c n) -> p c n", n=P), r=["psb"], w=[("yT", 0)])
        return None

    def sg_tile(t):
        psu, ku = nps()
        for c in range(2):
            proj_feat(t, 1024 + c * P, psu[:, c * P:(c + 1) * P], ku, False)
        for c in range(2):
            proj_feat(t, 1536 + c * P, psu[:, (2 + c) * P:(3 + c) * P], ku, False)
        psv, kv = nps()
        proj_tok(t, 1280, 256, psv[:, 0:256], kv)
        gv, kgv = gelu_of(psv[:, 0:256], [kv], "s_gv", (P, 256))
        s8 = wk("s_s8", (P, 8))
        RSUM(s8[:, 0:1], gv[:], r=[kgv], w=["s_s8"])
        TS("dve", s8[:, 0:1], s8[:, 0:1], 1.0 / 256, None, ALU.mult, r=["s_s8"], w=["s_s8"])
        TS("dve", gv[:], gv[:], s8[:, 0:1], None, ALU.subtract, r=[kgv, "s_s8"], w=[kgv])
        ACT(wk("s_j", (P, 256))[:], gv[:], AF.Square, accum=s8[:, 1:2], r=[kgv], w=["s_j", "s_s8"])
        ACT(s8[:, 2:3], s8[:, 1:2], AF.Ln, bias=EPS, scale=1.0 / 256, r=["s_s8"], w=["s_s8"])
        ACT(s8[:, 3:4], s8[:, 2:3], AF.Exp, scale=-0.5, r=["s_s8"], w=["s_s8"])
        TS("dve", gv[:], gv[:], s8[:, 3:4], None, ALU.mult, r=[kgv, "s_s8"], w=[kgv])
        tap("sg_vn_%d" % t, gv[:], [P, 256], [kgv])
        pss, ks = nps()
        for h in range(NH):
            c = h // 2
            MM(pss[:, h * P:(h + 1) * P], gv[:, c * P:(c + 1) * P], wsT[:, h, :], r=[kgv, "wsT"], w=[ks])
        sT = wk("s_sT", (P, 2, P))
        for hh in range(2):
            pb = 64 * hh
            src = pss[pb:pb + 64, :].rearrange("p (c x i) -> p c x i", c=2, x=2)[:, :, hh, :]
            TT("dve", sT[pb:pb + 64, :, :], src, bsT[pb:pb + 64, :, :], ALU.add, r=[ks, "bsT"], w=["s_sT"])
        gu, kgu = gelu_of(psu[:, 0:256], [ku], "s_gu", (P, 256))
        sz = silu_of(psu[:, 256:512], [ku], "s_sz", (P, 256))
        TT("dve", sz[:], sz[:], psu[:, 256:512], ALU.mult, r=["s_sz", ku], w=["s_sz"])
        TT("pool", gu[:], gu[:], sT[:].rearrange("p c i -> p (c i)"), ALU.mult, r=[kgu, "s_sT"], w=[kgu])
        TT("dve", yT[:, 2:4, :], gu[:].rearrange("p (c i) -> p c i", c=2), sz[:].rearrange("p (c i) -> p c i", c=2),
           ALU.mult, r=[kgu, "s_sz"], w=[("yT", 1)])

    def sc_tile(t, seg_first, seg_last):
        W = P + 2
        psA, kA = nps()
        psB, kB = nps()
        psC, kC = nps()
        for c in range(2):
            proj_feat(t, 2048 + c * P, psA[:, c * W:(c + 1) * W], kA, True)
        proj_feat(t, 2304, psA[:, 2 * W:3 * W], kA, True)
        proj_feat(t, 2304 + P, psB[:, 0:W], kB, True)
        for c in range(2):
            proj_feat(t, 1792 + c * P, psB[:, W + c * P:W + (c + 1) * P], kB, False)
            proj_feat(t, 2560 + c * P, psC[:, c * P:(c + 1) * P], kC, False)
        hs = wk("c_h", (P, 2, W))
        CP("act", hs[:, 0, :], psA[:, 2 * W:3 * W], r=[kA], w=["c_h"])
        CP("act", hs[:, 1, :], psB[:, 0:W], r=[kB], w=["c_h"])
        ch = wk("c_ch", (P, 2, W))
        TT("dve", ch[:], psA[:, 0:2 * W].rearrange("p (c n) -> p c n", c=2), hs[:], ALU.mult, r=[kA, "c_h"], w=["c_ch"])
        if seg_first:
            MSET("dve", ch[:, :, 0:1], 0.0, w=["c_ch"])
        if seg_last:
            MSET("dve", ch[:, :, W - 1:W], 0.0, w=["c_ch"])
        cv = wk("c_cv", (P, 2, P))
        for c in range(2):
            w_ = lambda k: ppp[:, PP_SCW + c * 3 + k:PP_SCW + c * 3 + k + 1]
            TS("dve", cv[:, c, :], ch[:, c, 0:P], w_(0), None, ALU.mult, r=["c_ch", "ppp"], w=["c_cv"])
            STT(cv[:, c, :], ch[:, c, 1:P + 1], w_(1), cv[:, c, :], ALU.mult, ALU.add, r=["c_ch", "ppp", "c_cv"], w=["c_cv"])
            STT(cv[:, c, :], ch[:, c, 2:P + 2], w_(2), cv[:, c, :], ALU.mult, ALU.add, r=["c_ch", "ppp", "c_cv"], w=["c_cv"])
        tap("sc_cv_%d" % t, cv[:].rearrange("p c n -> p (c n)"), [P, 256], ["c_cv"])
        sz = silu_of(psC[:, 0:256], [kC], "c_sz", (P, 256))
        TT("dve", sz[:], sz[:], psC[:, 0:256], ALU.mult, r=["c_sz", kC], w=["c_sz"])
        TT("dve", cv[:].rearrange("p c n -> p (c n)"), cv[:].rearrange("p c n -> p (c n)"), psB[:, W:W + 256], ALU.mult,
           r=["c_cv", kB], w=["c_cv"])
        TT("dve", yT[:, 4:6, :], cv[:], sz[:].rearrange("p (c i) -> p c i", c=2), ALU.mult, r=["c_cv", "c_sz"],
           w=[("yT", 2)])

    def gdn_dir(d, t, first, pass_b, cur, need_out, seg_first, seg_last):
        W = P + 2
        S_c, S_n = Sgdn[cur], Sgdn[cur ^ 1]
        kS_c, kS_n = ("Sgdn", cur), ("Sgdn", cur ^ 1)
        psg, kg = nps()
        proj_tok(t, 3840, 16, psg[:, 0:16], kg)
        if not pass_b:
            qkv = wk("g_qkv", (P, 6, W))
            for b in range(2):
                ps, pk = nps()
                for c in range(3):
                    proj_feat(t, 2816 + (3 * b + c) * P, ps[:, c * W:(c + 1) * W], pk, True)
                CP("act", qkv[:, 3 * b:3 * b + 3, :], ps[:, 0:3 * W].rearrange("p (c n) -> p c n", c=3), r=[pk], w=["g_qkv"])
            if seg_first:
                MSET("dve", qkv[:, :, 0:1], 0.0, w=["g_qkv"])
            if seg_last:
                MSET("dve", qkv[:, :, W - 1:W], 0.0, w=["g_qkv"])
        else:
            cv = wk("g_cv", (P, 6, P))
            qkn = wk("g_qkn", (P, 4, P))
            DMA("sp", qkn[:].rearrange("p c n -> p (c n)"), gq_d[t, :, 0:512], r=[("gq", t, 0)], w=["g_qkn"])
            DMA("sp", cv[:, 4:6, :], gq_d[t, :, 512:768].rearrange("p (c n) -> p c n", c=2), r=[("gq", t, 1)], w=["g_cv"])
        g8 = wk("g_g8", (P, 48))
        K8 = "g_g8"
        a_ps = psg[:, 4 * d:4 * d + 4]
        b_ps = psg[:, 8 + 4 * d:12 + 4 * d]
        TT("dve", g8[:, 0:4], a_ps, rowp[:, RP_DTB + 4 * d:RP_DTB + 4 * d + 4], ALU.add, r=[kg, "rowp"], w=[K8])
        ACT(g8[:, 0:4], g8[:, 0:4], AF.Exp, r=[K8], w=[K8])
        ACT(g8[:, 0:4], g8[:, 0:4], AF.Ln, bias=1.0, r=[K8], w=[K8])
        TT("dve", g8[:, 0:4], g8[:, 0:4], negA[:, 4 * d:4 * d + 4], ALU.mult, r=[K8, "negA"], w=[K8])
        ACT(g8[:, 4:8], b_ps, AF.Exp, scale=-1.0, r=[kg], w=[K8])
        ACT(g8[:, 4:8], g8[:, 4:8], AF.Ln, bias=1.0, r=[K8], w=[K8])
        TS("dve", g8[:, 4:8], g8[:, 4:8], -1.0, None, ALU.mult, r=[K8], w=[K8])
        ACT(g8[:, 8:12], g8[:, 4:8], AF.Exp, r=[K8], w=[K8])
        psc, kc_ = nps()
        MM(psc[:, 0:4], cst["tri_b" if d else "tri_f"][:], g8[:, 0:4], r=[K8, ("c", "tri_b" if d else "tri_f")], w=[kc_])
        MM(psc[:, 4:8], ones[:], g8[:, 0:4], r=[K8, ("c", "ones")], w=[kc_])
        CP("dve", g8[:, 12:16], psc[:, 0:4], r=[kc_], w=[K8])
        TT("dve", g8[:, 16:20], g8[:, 4:8], g8[:, 12:16], ALU.add, r=[K8], w=[K8])
        ACT(g8[:, 20:24], g8[:, 16:20], AF.Exp, r=[K8], w=[K8])
        TT("dve", g8[:, 24:28], psc[:, 4:8], g8[:, 12:16], ALU.subtract, r=[kc_, K8], w=[K8])
        ACT(g8[:, 24:28], g8[:, 24:28], AF.Exp, r=[K8], w=[K8])
        TS("dve", g8[:, 32:36], g8[:, 12:16], -1.0, None, ALU.mult, r=[K8], w=[K8])
        cdP = wk("g_cdP", (P, 2))
        for hh in range(2):
            pb = 64 * hh
            ACT(cdP[pb:pb + 64, 0:2], psc[pb:pb + 64, 4 + hh:8:2], AF.Exp, r=[kc_], w=["g_cdP"])
        tap("gdn_g8_%d_%d" % (d, t), g8[:], [P, 48], [K8])
        dg = wk("g_dg")
        TT("dve", dg[:].rearrange("p (h i) -> p h i", h=4), ident[:].unsqueeze(1).to_broadcast([P, 4, P]),
           g8[:, 12:16].unsqueeze(2).to_broadcast([P, 4, P]), ALU.mult, r=[KI, K8], w=["g_dg"])
        psA, kA = nps()
        MM(psA[:], ones[:], dg[:], start=True, stop=False, r=["g_dg", ("c", "ones")], w=[kA])
        MM(psA[:], ident[:], cst["gdn_mask"][:, 2 * d, :], start=False, stop=True, r=[KI, ("c", "gdn_mask")], w=[kA])
        Nm = wk("g_N")
        NmT = wk("g_NT")
        X = wk("g_X")
        kX = "g_X"
        for h in range(NH):
            ACT(Nm[:, h * P:(h + 1) * P], psA[:, h * P:(h + 1) * P], AF.Exp, bias=g8[:, 16 + h:17 + h], scale=-1.0,
                r=[kA, K8], w=["g_N"])
        if not pass_b:
            cv = wk("g_cv", (P, 6, P))
            for c in range(6):
                w_ = lambda k: ppp[:, PP_GDW + c * 3 + k:PP_GDW + c * 3 + k + 1]
                TS("dve", cv[:, c, :], qkv[:, c, 0:P], w_(0), None, ALU.mult, r=["g_qkv", "ppp"], w=["g_cv"])
                STT(cv[:, c, :], qkv[:, c, 1:P + 1], w_(1), cv[:, c, :], ALU.mult, ALU.add, r=["g_qkv", "ppp", "g_cv"], w=["g_cv"])
                STT(cv[:, c, :], qkv[:, c, 2:P + 2], w_(2), cv[:, c, :], ALU.mult, ALU.add, r=["g_qkv", "ppp", "g_cv"], w=["g_cv"])
            cvf = cv[:].rearrange("p c n -> p (c n)")
            sg_ = silu_of(cvf, ["g_cv"], "g_qkv", None, tile=qkv.rearrange("p c n -> p (c n)")[:, 0:6 * P])
            TT("dve", cvf, cvf, sg_[:], ALU.mult, r=["g_cv", "g_qkv"], w=["g_cv"])
            sq = wk("g_sq")
            ACT(sq[:], cv[:, 0:4, :].rearrange("p c n -> p (c n)"), AF.Square, r=["g_cv"], w=["g_sq"])
            psn, kn_ = nps()
            for c in range(4):
                MM(psn[:, c * P:(c + 1) * P], cst["bones"][:], sq[:, c * P:(c + 1) * P], r=["g_sq", ("c", "bones")], w=[kn_])
            rn = sq
            ACT(rn[:], psn[:], AF.Ln, bias=EPS, r=[kn_], w=["g_sq"])
            ACT(rn[:], rn[:], AF.Exp, scale=-0.5, r=["g_sq"], w=["g_sq"])
            qkn = wk("g_qkn", (P, 4, P))
            TT("dve", qkn[:].rearrange("p c n -> p (c n)"), cv[:, 0:4, :].rearrange("p c n -> p (c n)"), rn[:], ALU.mult,
               r=["g_cv", "g_sq"], w=["g_qkn"])
            DMA("sp", gq_d[t, :, 0:512], qkn[:].rearrange("p c n -> p (c n)"), r=["g_qkn"], w=[("gq", t, 0)])
            DMA("sp", gq_d[t, :, 512:768].rearrange("p (c n) -> p c n", c=2), cv[:, 4:6, :], r=["g_cv"], w=[("gq", t, 1)])
        tap("gdn_qkn_%d_%d" % (d, t), qkn[:].rearrange("p c n -> p (c n)"), [P, 512], ["g_qkn"])
        for hh in range(2):
            pb = 64 * hh
            CP("pool", g_knz[pb:pb + 64, hh::2, :], qkn[pb:pb + 64, 2:4, :], r=["g_qkn"], w=["g_knz"])
        pst, kt = nps()
        for c in range(2):
            TR(pst[:, c * P:(c + 1) * P], qkn[:, 2 + c, :], ident[:], r=["g_qkn", KI], w=[kt])
            TR(pst[:, 256 + c * P:256 + (c + 1) * P], cv[:, 4 + c, :], ident[:], r=["g_cv", KI], w=[kt])
        vb = wk("g_vb", (P, 256))
        kbg = wk("g_kbg", (P, 256))
        ktl = wk("g_ktl", (P, 256))
        kn3 = pst[:, 0:256].rearrange("p (h e) -> p h e", e=64)
        TT("dve", vb[:].rearrange("p (h e) -> p h e", e=64), pst[:, 256:512].rearrange("p (h e) -> p h e", e=64),
           g8[:, 8:12].unsqueeze(2).to_broadcast([P, 4, 64]), ALU.mult, r=[kt, K8], w=["g_vb"])
        TT("dve", kbg[:].rearrange("p (h e) -> p h e", e=64), kn3, g8[:, 20:24].unsqueeze(2).to_broadcast([P, 4, 64]),
           ALU.mult, r=[kt, K8], w=["g_kbg"])
        TT("dve", ktl[:].rearrange("p (h e) -> p h e", e=64), kn3, g8[:, 24:28].unsqueeze(2).to_broadcast([P, 4, 64]),
           ALU.mult, r=[kt, K8], w=["g_ktl"])
        psG, kG = nps()
        for h in range(NH):
            c, pb = h // 2, 64 * (h % 2)
            MM(psG[:, h * P:(h + 1) * P], g_knz[:, h, :], qkn[:, 2 + c, :], r=["g_qkn", "g_knz"], w=[kG])
        STT(Nm[:], psG[:], -1.0, Nm[:], ALU.mult, ALU.mult, r=[kG, "g_N"], w=["g_N"])
        tap("gdn_N_%d_%d" % (d, t), Nm[:], [P, 512], ["g_N"])
        psT, kT_ = nps()
        for h in range(NH):
            TR(psT[:, h * P:(h + 1) * P], Nm[:, h * P:(h + 1) * P], ident[:], r=["g_N", KI], w=[kT_])
        CP("act", NmT[:], psT[:], r=[kT_], w=["g_NT"])
        TT("dve", X[:].rearrange("p (h i) -> p h i", h=4), psT[:].rearrange("p (h i) -> p h i", h=4),
           ident[:].unsqueeze(1).to_broadcast([P, 4, P]), ALU.add, r=[kT_, KI], w=["g_X"])
        side = []
        if need_out:
            EG = wk("g_EG", (P, 2, P))
            PT = wk("g_PT")

            def side1():
                psR, kR = nps()
                MM(psR[:], ones[:], dg[:], r=["g_dg", ("c", "ones")], w=[kR])
                for h in range(NH):
                    c, pb = h // 2, 64 * (h % 2)
                    ACT(EG[pb:pb + 64, c, :], psR[pb:pb + 64, h * P:(h + 1) * P], AF.Exp, r=[kR], w=["g_EG"])
                for hh in range(2):
                    pb = 64 * hh
                    STT(g_qdz[pb:pb + 64, hh::2, :], qkn[pb:pb + 64, 0:2, :], 0.125, EG[pb:pb + 64, :, :], ALU.mult, ALU.mult,
                        r=["g_qkn", "g_EG"], w=["g_qdz"])

            def side2():
                psP, kP = nps()
                MM(psP[:], ones[:], dg[:], start=True, stop=False, r=["g_dg", ("c", "ones")], w=[kP])
                MM(psP[:], ident[:], cst["gdn_mask"][:, 2 * d + 1, :], start=False, stop=True, r=[KI, ("c", "gdn_mask")], w=[kP])
                for h in range(NH):
                    ACT(PT[:, h * P:(h + 1) * P], psP[:, h * P:(h + 1) * P], AF.Exp, bias=g8[:, 32 + h:33 + h],
                        r=[kP, K8], w=["g_PT"])

            def side3():
                psq, kq = nps()
                for h in range(NH):
                    c, pb = h // 2, 64 * (h % 2)
                    MM(psq[:, h * P:(h + 1) * P], g_knz[:, h, :], qkn[:, c, :], r=["g_qkn", "g_knz"], w=[kq])
                STT(PT[:], psq[:], 0.125, PT[:], ALU.mult, ALU.mult, r=[kq, "g_PT"], w=["g_PT"])
            side = [side1, side2, side3]
        def sq_(k):
            ps1_, k1_ = nps()
            for h in range(NH):
                sl = slice(h * P, (h + 1) * P)
                MM(ps1_[:, sl], NmT[:, sl], Nm[:, sl], r=["g_N", "g_NT"], w=[k1_])
            ps2_, k2_ = None, None
            if k < 6:
                ps2_, k2_ = nps()
                for h in range(NH):
                    sl = slice(h * P, (h + 1) * P)
                    MM(ps2_[:, sl], Nm[:, sl], NmT[:, sl], r=["g_N", "g_NT"], w=[k2_])
            return ps1_, k1_, ps2_, k2_

        cur_sq = sq_(1)
        for k in range(1, 7):
            ps1_, k1_, ps2_, k2_ = cur_sq
            CP("act", Nm[:], ps1_[:], r=[k1_], w=["g_N"])
            if k < 6:
                CP("dve", NmT[:], ps2_[:], r=[k2_], w=["g_NT"])
                cur_sq = sq_(k + 1)
            ps3_, k3_ = nps()
            for h in range(NH):
                sl = slice(h * P, (h + 1) * P)
                MM(ps3_[:, sl], Nm[:, sl], X[:, sl], r=["g_N", "g_X"], w=[k3_])
            TT("dve", X[:], X[:], ps3_[:], ALU.add, r=["g_X", k3_], w=["g_X"])
            if side and k >= 2:
                side.pop(0)()
        while side:
            side.pop(0)()
        tap("gdn_XT_%d_%d" % (d, t), X[:], [P, 512], [kX])
        psu, ku = nps()
        for h in range(NH):
            MM(psu[:, h * 64:(h + 1) * 64], X[:, h * P:(h + 1) * P], vb[:, h * 64:(h + 1) * 64], r=[kX, "g_vb"], w=[ku])
        u_sb = wk("g_u", (P, 256))
        CP("act", u_sb[:], psu[:, 0:256], r=[ku], w=["g_u"])
        vnew = u_sb
        if not first:
            psw, kw = nps()
            for h in range(NH):
                c = h // 2
                MM(psw[:, h * P:(h + 1) * P], kbg[:, c * P:(c + 1) * P], X[:, h * P:(h + 1) * P], r=[kX, "g_kbg"], w=[kw])
            for hh in range(2):
                pb = 64 * hh
                src = psw[pb:pb + 64, :].rearrange("p (c x i) -> p c x i", c=2, x=2)[:, :, hh, :]
                CP("act", g_wz[pb:pb + 64, hh::2, :], src, r=[kw], w=["g_wz"])
            pws, kws = nps()
            for h in range(NH):
                c = h // 2
                MM(pws[:, h * 64:(h + 1) * 64], g_wz[:, h, :], S_c[:, c, :], r=["g_wz", kS_c], w=[kws])
            TT("dve", u_sb[:], u_sb[:], pws[:, 0:256], ALU.subtract, r=["g_u", kws], w=["g_u"])
        tap("gdn_vnew_%d_%d" % (d, t), vnew[:], [P, 256], ["g_u"])
        if need_out:
            pso, ko = nps()
            for h in range(NH):
                c, pb = h // 2, 64 * (h % 2)
                MM(pso[:, h * 64:(h + 1) * 64], PT[:, h * P:(h + 1) * P], vnew[:, h * 64:(h + 1) * 64],
                   start=True, stop=first, r=["g_PT", "g_u"], w=[ko])
                if not first:
                    MM(pso[:, h * 64:(h + 1) * 64], g_qdz[:, h, :], S_c[:, c, :],
                       start=False, stop=True, r=["g_qdz", kS_c], w=[ko])
        pss, ks = nps()
        for c in range(2):
            MM(pss[:, c * P:(c + 1) * P], ktl[:, c * P:(c + 1) * P], vnew[:, c * P:(c + 1) * P], r=["g_ktl", "g_u"], w=[ks])
        for hh in range(2):
            pb = 64 * hh
            src = pss[pb:pb + 64, 0:256].rearrange("p (c x) -> p c x", c=2)[:, :, pb:pb + 64]
            if first:
                CP("dve", S_n[pb:pb + 64, :, :], src, r=[ks], w=[kS_n])
            else:
                tmp = wk("g_stmp", (P, 2, 64))
                TT("dve", tmp[pb:pb + 64, :, :], S_c[pb:pb + 64, :, :],
                   cdP[pb:pb + 64, 0:2].unsqueeze(2).to_broadcast([64, 2, 64]), ALU.mult, r=[kS_c, "g_cdP"], w=["g_stmp"])
                TT("dve", S_n[pb:pb + 64, :, :], tmp[pb:pb + 64, :, :], src, ALU.add, r=["g_stmp", ks], w=[kS_n])
        if not need_out:
            return
        if not pass_b:
            o_sb = wk("g_of", (P, 256))
            CP("act", o_sb[:], pso[:, 0:256], r=[ko], w=["g_of"])
            DMA("sp", of_d[t, :, 256:512], o_sb[:], r=["g_of"], w=[("of", t, 1)])
            tap("gdn_of_%d" % t, o_sb[:], [P, 256], ["g_of"])
            return
        psz, kz = nps()
        proj_tok(t, 3584, 256, psz[:, 0:256], kz)
        ofl = wk("g_ofl", (P, 256))
        DMA("sp", ofl[:], of_d[t, :, 256:512], r=[("of", t, 1)], w=["g_ofl"])
        o = wk("g_o", (P, 256))
        TT("dve", o[:], pso[:, 0:256], ofl[:], ALU.add, r=[ko, "g_ofl"], w=["g_o"])
        tap("gdn_o_%d" % t, o[:], [P, 256], ["g_o"])
        o3 = o[:].rearrange("p (h e) -> p h e", e=64)
        s8 = wk("g_s8", (P, 16))
        sq2 = wk("g_sq2", (P, 256))
        ACT(sq2[:], o[:], AF.Square, r=["g_o"], w=["g_sq2"])
        RSUM(s8[:, 0:4], sq2[:].rearrange("p (h e) -> p h e", e=64), r=["g_sq2"], w=["g_s8"])
        ACT(s8[:, 4:8], s8[:, 0:4], AF.Ln, bias=EPS, scale=1.0 / 64, r=["g_s8"], w=["g_s8"])
        ACT(s8[:, 8:12], s8[:, 4:8], AF.Exp, scale=-0.5, r=["g_s8"], w=["g_s8"])
        TT("dve", o3, o3, s8[:, 8:12].unsqueeze(2).to_broadcast([P, 4, 64]), ALU.mult, r=["g_o", "g_s8"], w=["g_o"])
        TT("pool", o3, o3, rowp[:, RP_GDNG:RP_GDNG + 64].unsqueeze(1).to_broadcast([P, 4, 64]), ALU.mult,
           r=["g_o", "rowp"], w=["g_o"])
        sz = silu_of(psz[:, 0:256], [kz], "g_sz", (P, 256))
        TT("dve", sz[:], sz[:], psz[:, 0:256], ALU.mult, r=["g_sz", kz], w=["g_sz"])
        yb = wk("g_yb", (P, 256), BF16)
        TT("dve", yb[:], o[:], sz[:], ALU.mult, r=["g_o", "g_sz"], w=["g_yb"])
        for c in range(2):
            TR(psb[:, 512 + c * P:512 + (c + 1) * P], yb[:, c * P:(c + 1) * P], ident_b[:], r=["g_yb", "ident_b"], w=["psb"])
        CP("act", yT[:, 6:8, :], psb[:, 512:768].rearrange("p (c n) -> p c n", n=P), r=["psb"], w=[("yT", 3)])

    def outproj(li, s, t):
        r_ = 2 if t < TC else s
        xa, xk, s8, sk = load_x(li, s, t)
        pso = []
        for nt in range(2):
            ps, pk = nps()
            for kc in range(KC):
                MM(ps[:], yT[:, kc, :], wout[:, kc, nt * 512:(nt + 1) * 512], start=(kc == 0), stop=(kc == KC - 1),
                   r=[("yT", 0), ("yT", 1), ("yT", 2), ("yT", 3), "wout"], w=[pk])
            pso.append((ps, pk))
            ACT(wk("o_junk", (P, D))[:, nt * 512:(nt + 1) * 512], ps[:], AF.Square, accum=s8[:, 4 + nt:5 + nt], r=[pk], w=["o_junk", sk])
        TT("dve", s8[:, 0:1], s8[:, 4:5], s8[:, 5:6], ALU.add, r=[sk], w=[sk])
        rstd_from(s8, sk, 0, 1.0 / D)
        on = wk("o_on", (P, D))
        for nt in range(2):
            ps, pk = pso[nt]
            STT(on[:, nt * 512:(nt + 1) * 512], ps[:], s8[:, 2:3], Gb[:, r_, nt * 512:(nt + 1) * 512], ALU.mult, ALU.mult,
                r=[pk, sk, "Gb"], w=["o_on"])
        TT("pool", on[:], on[:], xa[:], ALU.add, r=["o_on", xk], w=["o_on"])
        dst, dk = dst_tile(li, s, t)
        DMA("sp", dst, on[:], r=["o_on"], w=[dk])

    for li, l in enumerate(layers):
        last = (l == DEPTH - 1)
        layer_setup(l)
        for s in range(n_seq):
            stage_h(li, s)
            orderF = list(range(NT))
            orderB = list(range(TC - 1, -1, -1)) + list(range(NT - 1, TC - 1, -1))
            segf = lambda t: t == 0 or t == TC
            segl = lambda t: t == TC - 1 or t == NT - 1
            if s == 0:
                tap("hT_%d" % li, hT[:, :, 1:NT * P + 1], [P, KC, NT * P], [("hT", t) for t in range(NT)], BF16)
            if "F" in passes:
                curR = curG = 0
                for n, t in enumerate(orderF):
                    need = not (last and t < TC)
                    if "ret" in branches:
                        ret_dir(0, t, n == 0, False, curR, need)
                        curR ^= 1
                    if "gdn" in branches:
                        gdn_dir(0, t, n == 0, False, curG, need, segf(t), segl(t))
                        curG ^= 1
            if "B" in passes:
                curR = curG = 0
                for n, t in enumerate(orderB):
                    need = not (last and t < TC)
                    if "ret" in branches:
                        ret_dir(1, t, n == 0, True, curR, need)
                        curR ^= 1
                    if "gdn" in branches:
                        gdn_dir(1, t, n == 0, True, curG, need, segf(t), segl(t))
                        curG ^= 1
                    if need:
                        if "sg" in branches:
                            sg_tile(t)
                        if "sc" in branches:
                            sc_tile(t, segf(t), segl(t))
                        tap("yT_%d_%d_%d" % (li, s, t), yT[:].rearrange("p k n -> p (k n)"), [P, KC * P],
                            [("yT", 0), ("yT", 1), ("yT", 2), ("yT", 3)], BF16)
                        if do_out:
                            outproj(li, s, t)
    if ps_last is None:
        st.close()
        return dict(pg.key_last)
    pg.emit()
    st.close()
    build.last_counts = {e: len(v) for e, v in pg.ops.items()}
    return nc, tap_out


def make_in_maps(inp, TC, TL, n_seq, n_cores):
    consts = _host_consts(TC, TL)
    rowp, pp, wsT = _host_layer_params(inp)
    shared = {
        "w_mod": np.ascontiguousarray(inp["w_mod"], np.float32),
        "b_mod": np.ascontiguousarray(inp["b_mod"], np.float32),
        "w_in": np.ascontiguousarray(inp["w_in"], np.float32),
        "w_out": np.ascontiguousarray(inp["w_out"], np.float32),
        "rowp": rowp, "pp": pp, "wsT": wsT,
    }
    for k, v in consts.items():
        shared["c_" + k] = np.ascontiguousarray(v, np.float32)
    maps = []
    for c in range(n_cores):
        b0 = c * n_seq
        cc = np.zeros((4, D), np.float32)
        cc[0:n_seq] = inp["c"][b0:b0 + n_seq]
        cc[2] = inp["c_ctx"]
        cT = np.ascontiguousarray(cc.reshape(4, KC, P).transpose(2, 1, 0)).reshape(P, KC * 4)
        m = dict(shared)
        m["x"] = np.ascontiguousarray(inp["x"][b0:b0 + n_seq], np.float32)
        m["ctx"] = np.ascontiguousarray(inp["ctx"][b0:b0 + n_seq], np.float32)
        m["cT"] = cT
        maps.append(m)
    return maps


_NC_CACHE = {}


N_LAUNCH = 1


def kernel(**inputs):
    inp = {k: np.asarray(v) for k, v in inputs.items()}
    B, T, _ = inp["x"].shape
    TL = T // P
    TC = inp["ctx"].shape[1] // P
    n_seq = B // N_CORES
    maps = make_in_maps(inp, TC, TL, n_seq, N_CORES)
    per = DEPTH // N_LAUNCH
    for g in range(N_LAUNCH):
        layers = list(range(g * per, (g + 1) * per))
        key = (TC, TL, n_seq, tuple(layers))
        if key not in _NC_CACHE:
            _NC_CACHE[key] = build(TC, TL, layers, n_seq=n_seq, ctx_ext=(N_LAUNCH > 1))[0]
        res = run_bass_kernel_spmd(_NC_CACHE[key], maps, core_ids=list(range(N_CORES)))
        if g < N_LAUNCH - 1:
            for c in range(N_CORES):
                maps[c]["x"] = np.asarray(res.results[c]["y"])
                maps[c]["ctx"] = np.asarray(res.results[c]["cs_scr"])
    out = np.concatenate([np.asarray(r["y"]) for r in res.results], axis=0)
    return out.astype(np.float32)
```

```python
import math
import numpy as np
import concourse.bass as bass
import concourse.mybir as mybir
from concourse.bass_utils import run_bass_kernel_spmd

F32 = mybir.dt.float32
BF16 = mybir.dt.bfloat16
AF = mybir.ActivationFunctionType
ALU = mybir.AluOpType
AX = mybir.AxisListType

P = 128
D = 1024
KC = 8
DEPTH = 4
BR = 256
HD = 64
NH = 4
IN_W = 3856
EPS = 1e-6
BIG = 30000.0
N_CORES = 8
import os
KDBG = os.environ.get("KDBG", "")


class Op:
    __slots__ = ("eng", "fn", "deps", "sig", "is_dma", "sem", "val", "needs_sig", "order")

    def __init__(self, eng, fn, is_dma=False):
        self.eng = eng
        self.fn = fn
        self.deps = []
        self.sig = None
        self.is_dma = is_dma
        self.sem = None
        self.val = None
        self.needs_sig = False
        self.order = 0


class Prog:
    ENGS = ("pe", "act", "dve", "pool", "sp")

    def __init__(self, nc):
        self.nc = nc
        self.ops = {e: [] for e in self.ENGS}
        self.last_w = {}
        self.readers = {}
        self.n = 0
        self.dma_slots = 24
        self.dma_rr = 0
        self.dma_last = [None] * self.dma_slots
        self.dma_cnt = [0] * self.dma_slots
        self.expand = {}
        self.key_last = {}
        self.expand["wbuf"] = [("wbuf", kc, c0) for kc in range(8) for c0 in range(0, 3856, 964)]
        self.expand["wout"] = [("wout", kc) for kc in range(8)]

    def _add(self, op, reads, writes):
        ex = self.expand
        for k in reads:
            if isinstance(k, tuple) and k[0] == "pa":
                self.key_last[k[1]] = self.n
        for k in writes:
            if isinstance(k, tuple) and k[0] == "pa":
                self.key_last[k[1]] = self.n
        reads = [x for k in reads for x in ex.get(k, (k,))]
        writes = [x for k in writes for x in ex.get(k, (k,))]
        deps = []
        for k in reads:
            w = self.last_w.get(k)
            if w is not None:
                deps.append(w)
            if k == "psb" or (isinstance(k, tuple) and k[0] == "ps"):
                for r in self.readers.get(k, ()):
                    if r.eng != op.eng:
                        deps.append(r)
        for k in writes:
            w = self.last_w.get(k)
            if w is not None:
                deps.append(w)
            for r in self.readers.get(k, ()):
                deps.append(r)
        seen = set()
        for d in deps:
            if d is op or id(d) in seen:
                continue
            if d.eng == "pe" and op.eng == "pe" and not d.is_dma and not op.is_dma:
                continue
            seen.add(id(d))
            op.deps.append(d)
            d.needs_sig = True
        for k in reads:
            self.readers.setdefault(k, []).append(op)
        for k in writes:
            self.last_w[k] = op
            self.readers[k] = []
        op.order = self.n
        self.n += 1
        self.ops[op.eng].append(op)
        return op

    def op(self, eng, fn, reads=(), writes=()):
        return self._add(Op(eng, fn), reads, writes)

    def dma(self, eng, out, in_, reads=(), writes=()):
        o = Op(eng, lambda e: e.dma_start(out=out, in_=in_), is_dma=True)
        s = self.dma_rr
        self.dma_rr = (self.dma_rr + 1) % self.dma_slots
        prev = self.dma_last[s]
        self.dma_cnt[s] += 1
        o.sem = s
        o.val = 16 * self.dma_cnt[s]
        o.needs_sig = True
        self._add(o, reads, writes)
        if prev is not None and prev not in o.deps:
            o.deps.append(prev)
        self.dma_last[s] = o
        return o

    def emit(self):
        nc = self.nc
        import contextlib
        with contextlib.ExitStack() as st:
            esem = {e: st.enter_context(nc.semaphore("s_" + e)) for e in ("pe", "act", "dve", "pool")}
            dsem = [st.enter_context(nc.semaphore("d%d" % i)) for i in range(self.dma_slots)]
            for e in ("pe", "act", "dve", "pool"):
                k = 0
                for o in self.ops[e]:
                    if o.is_dma:
                        continue
                    if o.needs_sig:
                        k += 1
                        o.sem = e
                        o.val = k
            block = st.enter_context(nc.Block())
            handles = {"pe": block.tensor, "act": block.scalar, "dve": block.vector,
                       "pool": block.gpsimd, "sp": block.sync}

            def make(ename):
                ops = self.ops[ename]

                def body(eng):
                    seen = {}
                    for o in ops:
                        for d in o.deps:
                            sem = dsem[d.sem] if d.is_dma else esem[d.sem]
                            key = ("d", d.sem) if d.is_dma else d.sem
                            if seen.get(key, 0) >= d.val:
                                continue
                            seen[key] = d.val
                            eng.wait_ge(sem, d.val)
                        ins = o.fn(eng)
                        if o.is_dma:
                            ins.then_inc(dsem[o.sem], 16)
                        elif o.needs_sig:
                            ins.then_inc(esem[o.sem], 1)
                    for s in range(self.dma_slots):
                        last = self.dma_last[s]
                        if last is not None and last.eng == ename:
                            if seen.get(("d", s), 0) < last.val:
                                eng.wait_ge(dsem[s], last.val)
                return body

            for e in self.ENGS:
                if self.ops[e]:
                    handles[e](make(e))


def _host_consts(TC, TL):
    NT = TC + TL
    c = {}
    c["ident"] = np.eye(P, dtype=np.float32)
    c["ones"] = np.ones((P, P), np.float32)
    bo = np.zeros((P, P), np.float32)
    bo[:64, :64] = 1.0
    bo[64:, 64:] = 1.0
    c["bones"] = bo
    sel = np.zeros((4, 4 * P), np.float32)
    for r in range(4):
        sel[r, r * P:(r + 1) * P] = 1.0
    c["sel"] = sel
    j = np.arange(P)[:, None]
    i = np.arange(P)[None, :]
    c["tri_f"] = (j <= i).astype(np.float32)
    c["tri_b"] = (j >= i).astype(np.float32)
    lg = np.log(1.0 - 2.0 ** (-5.0 - np.arange(NH, dtype=np.float64)))
    gam = np.exp(lg)
    pos = np.arange(P, dtype=np.float64)
    rm = np.zeros((2, P, NH, P), np.float32)
    rdec = np.zeros((2, P, 8), np.float32)
    for h in range(NH):
        ch = gam[h] ** (-128.0)
        rm[0, :, h, :] = ch * (i >= j)
        rm[1, :, h, :] = ch * (i <= j)
        rdec[0, :, h] = gam[h] ** (pos + 1.0)
        rdec[0, :, 4 + h] = 0.125 * gam[h] ** (127.0 - pos)
        rdec[1, :, h] = gam[h] ** (128.0 - pos)
        rdec[1, :, 4 + h] = 0.125 * gam[h] ** pos
    c["ret_mask"] = rm.reshape(2, P, NH * P)
    c["ret_dec"] = rdec
    cd = np.zeros((P, 2, 64), np.float32)
    for h in range(NH):
        cd[64 * (h % 2):64 * (h % 2) + 64, h // 2, :] = gam[h] ** 128.0
    c["ret_cdec_p"] = cd.reshape(P, 128)
    pi = np.arange(P)[:, None]
    fj = np.arange(P)[None, :]
    gm = np.zeros((2, 2, P, NH, P), np.float32)
    gm[0, 0] = (BIG * (fj >= pi))[:, None, :]
    gm[1, 0] = (BIG * (fj <= pi))[:, None, :]
    gm[0, 1] = (-BIG * (fj < pi))[:, None, :]
    gm[1, 1] = (-BIG * (fj > pi))[:, None, :]
    c["gdn_mask"] = gm.reshape(4, P, NH * P)
    nf = 16
    inv = (np.float32(10000.0) ** (-np.arange(nf, dtype=np.float32) / np.float32(nf))).astype(np.float32)
    C = np.ones((NT, P, 64), np.float32)
    S = np.zeros((NT, P, 64), np.float32)
    for t in range(TL):
        tok = np.arange(t * P, (t + 1) * P)
        row = (tok // 64).astype(np.float32)
        col = (tok % 64).astype(np.float32)
        ar = (row[:, None] * inv[None, :]).astype(np.float32)
        ac = (col[:, None] * inv[None, :]).astype(np.float32)
        C[TC + t] = np.concatenate([np.cos(ar), np.cos(ar), np.cos(ac), np.cos(ac)], axis=1)
        S[TC + t] = np.concatenate([-np.sin(ar), np.sin(ar), -np.sin(ac), np.sin(ac)], axis=1)
    c["rot_c"] = np.ascontiguousarray(C.transpose(1, 0, 2)).reshape(P, NT * 64)
    c["rot_s"] = np.ascontiguousarray(S.transpose(1, 0, 2)).reshape(P, NT * 64)
    return c


CONST_SHAPES = lambda NT: {
    "ident": [P, P], "ones": [P, P], "bones": [P, P], "sel": [4, 4 * P], "tri_f": [P, P], "tri_b": [P, P],
    "ret_mask": [2, P, NH * P], "ret_dec": [2, P, 8], "ret_cdec_p": [P, P], "gdn_mask": [4, P, NH * P],
    "rot_c": [P, NT * 64], "rot_s": [P, NT * 64],
}

RP_RETG = 0
RP_GDNG = 256
RP_ALOG = 320
RP_DTB = 328
RP_SGB = 336
RP_W = 1024 + 336 + 512
PP_GPRE = 0
PP_SCW = 8
PP_GDW = 14
PP_W = 32


def _host_layer_params(inp):
    L = inp["w_in"].shape[0]
    rowp = np.zeros((L, RP_W), np.float32)
    O = 1024
    rowp[:, 0:1024] = inp["g_post"]
    rowp[:, O + RP_RETG:O + RP_RETG + 256] = inp["ret_norm_g"]
    rowp[:, O + RP_GDNG:O + RP_GDNG + 64] = inp["gdn_norm_g"]
    rowp[:, O + RP_ALOG:O + RP_ALOG + 8] = inp["gdn_a_log"].reshape(L, 8)
    rowp[:, O + RP_DTB:O + RP_DTB + 8] = inp["gdn_dt_bias"].reshape(L, 8)
    rowp[:, O + RP_SGB:O + RP_SGB + 512] = inp["sg_b"].reshape(L, 512)
    pp = np.zeros((L, P, PP_W), np.float32)
    pp[:, :, PP_GPRE:PP_GPRE + 8] = inp["g_pre"].reshape(L, 8, P).transpose(0, 2, 1)
    pp[:, :, PP_SCW:PP_SCW + 6] = inp["sc_conv_w"].reshape(L, 3, 2, P).transpose(0, 3, 2, 1).reshape(L, P, 6)
    pp[:, :, PP_GDW:PP_GDW + 18] = inp["gdn_conv_w"].reshape(L, 3, 6, P).transpose(0, 3, 2, 1).reshape(L, P, 18)
    wsT = np.ascontiguousarray(inp["sg_w"].transpose(0, 3, 1, 2)).reshape(L, P, 4 * P)
    return rowp, pp, wsT


def build(TC, TL, layers, branches=("ret", "sg", "sc", "gdn"), taps=(), n_seq=2, do_out=True, passes=("F", "B"),
          ctx_ext=False):
    ps_last = _build(TC, TL, layers, branches, taps, n_seq, do_out, passes, None, ctx_ext)
    return _build(TC, TL, layers, branches, taps, n_seq, do_out, passes, ps_last, ctx_ext)


def _build(TC, TL, layers, branches, taps, n_seq, do_out, passes, ps_last, ctx_ext=False):
    import contextlib
    NT = TC + TL
    L = len(layers)
    nc = bass.Bass("TRN2", target_bir_lowering=False)
    pg = Prog(nc)
    st = contextlib.ExitStack()
    tap_out = {}

    def din(name, shape, dt=F32):
        return nc.dram_tensor(name, list(shape), dt, kind="ExternalInput").ap()

    x_d = din("x", [n_seq, TL * P, D])
    ctx_d = din("ctx", [n_seq, TC * P, D])
    cT_d = din("cT", [P, KC * 4])
    wmod_d = din("w_mod", [DEPTH, D, 3 * D])
    bmod_d = din("b_mod", [DEPTH, 3 * D])
    win_d = din("w_in", [DEPTH, D, IN_W])
    wout_d = din("w_out", [DEPTH, D, D])
    rowp_d = din("rowp", [DEPTH, RP_W])
    pp_d = din("pp", [DEPTH, P, PP_W])
    wsT_d = din("wsT", [DEPTH, P, 4 * P])
    cds = {k: din("c_" + k, shp) for k, shp in CONST_SHAPES(NT).items()}
    y_d = nc.dram_tensor("y", [n_seq, TL * P, D], F32, kind="ExternalOutput").ap()
    xs_d = nc.dram_tensor("xs_scr", [n_seq, TL * P, D], F32, kind="Internal").ap()
    cs_d = nc.dram_tensor("cs_scr", [n_seq, TC * P, D], F32, kind="ExternalOutput" if ctx_ext else "Internal").ap()
    of_d = nc.dram_tensor("of_scr", [NT, P, 512], F32, kind="Internal").ap()
    gq_d = nc.dram_tensor("gq_scr", [NT, P, 768], F32, kind="Internal").ap()

    def sb(name, shape, dt=F32):
        return st.enter_context(nc.sbuf_tensor("sb_" + name, list(shape), dt))

    def psum(name, shape, dt=F32):
        return st.enter_context(nc.psum_tensor(name, list(shape), dt))

    def MM(out, lhsT, rhs, start=True, stop=True, r=(), w=()):
        pg.op("pe", lambda e: e.matmul(out, lhsT, rhs, start=start, stop=stop), r, w)

    def TR(out, in_, idn, r=(), w=()):
        pg.op("pe", lambda e: e.transpose(out, in_, idn), r, w)

    def ACT(out, in_, func, bias=None, scale=None, accum=None, r=(), w=()):
        kw = {}
        if bias is not None:
            kw["bias"] = bias
        if scale is not None:
            kw["scale"] = scale
        if accum is not None:
            kw["accum_out"] = accum
        pg.op("act", lambda e: e.activation(out, in_, func, **kw), r, w)

    def TT(eng, out, a, b, op, r=(), w=()):
        pg.op(eng, lambda e: e.tensor_tensor(out, a, b, op), r, w)

    def TS(eng, out, a, s1, s2, op0, op1=None, r=(), w=()):
        if op1 is None:
            pg.op(eng, lambda e: e.tensor_scalar(out, a, s1, None, op0), r, w)
        else:
            pg.op(eng, lambda e: e.tensor_scalar(out, a, s1, s2, op0, op1), r, w)

    def STT(out, in0, scalar, in1, op0, op1, r=(), w=()):
        pg.op("dve", lambda e: e.scalar_tensor_tensor(out, in0, scalar, in1, op0, op1), r, w)

    def CP(eng, out, in_, r=(), w=()):
        if eng == "act":
            pg.op("act", lambda e: e.copy(out, in_), r, w)
        else:
            pg.op(eng, lambda e: e.tensor_copy(out, in_), r, w)

    def RSUM(out, in_, r=(), w=()):
        pg.op("dve", lambda e: e.reduce_sum(out, in_, AX.X), r, w)

    def RECIP(out, in_, r=(), w=()):
        pg.op("dve", lambda e: e.reciprocal(out, in_), r, w)

    def MSET(eng, ap, val, w=()):
        pg.op(eng, lambda e: e.memset(ap, val), (), w)

    def DMA(eng, out, in_, r=(), w=()):
        pg.dma(eng, out, in_, r, w)

    def tap(name, ap, shape, r, dt=F32):
        if name not in taps:
            return
        d = nc.dram_tensor("tap_" + name, list(shape), dt, kind="ExternalOutput").ap()
        tap_out[name] = "tap_" + name
        DMA("sp", d, ap, r=r, w=[("tap", name)])

    NPS = 7
    psf = [psum("psf%d" % i, [P, 512]) for i in range(NPS)]
    psb = psum("psb", [P, 1024], BF16)
    ps_rr = [0]
    ps_cnt = [0]
    ps_occ = [None] * NPS

    def nps():
        a = ps_cnt[0]
        ps_cnt[0] += 1
        if ps_last is None:
            i = a % NPS
        else:
            i = None
            for step in range(NPS):
                j = (ps_rr[0] + step) % NPS
                occ = ps_occ[j]
                if occ is None or ps_last.get(occ, -1) < pg.n:
                    i = j
                    break
            assert i is not None, "out of PSUM banks"
            ps_rr[0] = (i + 1) % NPS
        ps_occ[i] = a
        pg.expand[("pa", a)] = [("ps", i)]
        return psf[i], ("pa", a)

    cst = {}
    for k, shp in CONST_SHAPES(NT).items():
        if k in ("rot_c", "rot_s"):
            continue
        if len(shp) == 3:
            t_ = sb("k_" + k, [shp[1], shp[0], shp[2]])
            for a in range(shp[0]):
                DMA("sp", t_[:, a, :], cds[k][a], w=[("c", k)])
        else:
            t_ = sb("k_" + k, shp)
            DMA("sp", t_[:], cds[k], w=[("c", k)])
        cst[k] = t_
    ident = cst["ident"]
    ones = cst["ones"]
    ident_b = sb("ident_b", [P, P], BF16)
    CP("dve", ident_b[:], ident[:], r=[("c", "ident")], w=["ident_b"])
    KI = ("c", "ident")

    siluT = sb("siluT", [P, KC, 4])
    AT = sb("AT", [P, KC, 4])
    shT = sb("shT", [P, KC, 4])
    Gb = sb("Gb", [P, 3, D])
    rowp = sb("rowp", [P, RP_W - 1024])
    ppp = sb("ppp", [P, PP_W])
    wsT = sb("wsT", [P, 4, P])
    bsT = sb("bsT", [P, 2, P])
    negA = sb("negA", [P, 8])
    hT = sb("hT", [P, KC, NT * P + 2], BF16)
    wbuf = sb("wbuf", [P, KC, IN_W], BF16)
    wout = sb("wout", [P, KC, D], BF16)
    xt = [sb("xt%d" % i, [P, D]) for i in range(2)]
    st8 = [sb("st8_%d" % i, [P, 8]) for i in range(2)]
    yT = sb("yT", [P, KC, P], BF16)
    Sret = [sb("Sret%d" % i, [P, 2, 64]) for i in range(2)]
    Sgdn = [sb("Sgdn%d" % i, [P, 2, 64]) for i in range(2)]
    r_qz = sb("r_qz", [P, 4, P])
    g_knz = sb("g_knz", [P, 4, P])
    g_wz = sb("g_wz", [P, 4, P])
    g_qdz = sb("g_qdz", [P, 4, P])

    GR = 128
    ARENA = 36 * 1024 // 4
    arena = sb("arena", [P, ARENA])
    wk_cache = {}
    scope_ptr = {}

    def wk(name, shape=(P, 512), dt=F32):
        if name not in wk_cache:
            scope = name.split("_")[0]
            n = 1
            for d_ in shape[1:]:
                n *= d_
            if dt != F32:
                n = n // 2
            ng = (n + GR - 1) // GR
            off = scope_ptr.get(scope, 0)
            scope_ptr[scope] = off + ng * GR
            assert off + ng * GR <= ARENA, (name, off, ng * GR)
            ap = arena[0:shape[0], off:off + n]
            if dt != F32:
                ap = ap.bitcast(dt)
            if len(shape) == 3:
                ap = ap.rearrange("p (a b) -> p a b", b=shape[2])
            pg.expand[name] = [("ar", g) for g in range(off // GR, off // GR + ng)]
            wk_cache[name] = ap
        return wk_cache[name]

    cT = wk("cT", (P, KC * 4))
    DMA("sp", cT[:], cT_d, w=["cT"])
    e_ = wk("cT_e", (P, KC * 4))
    ACT(e_[:], cT[:], AF.Exp, scale=-1.0, r=["cT"], w=["cT_e"])
    ACT(e_[:], e_[:], AF.Ln, bias=1.0, r=["cT_e"], w=["cT_e"])
    ACT(e_[:], e_[:], AF.Exp, scale=-1.0, r=["cT_e"], w=["cT_e"])
    TT("dve", siluT[:].rearrange("p k r -> p (k r)"), cT[:], e_[:], ALU.mult, r=["cT", "cT_e"], w=["siluT"])
    MSET("dve", yT[:], 0.0, w=[("yT", 0), ("yT", 1), ("yT", 2), ("yT", 3)])
    MSET("dve", r_qz[:], 0.0, w=["r_qz"])
    MSET("dve", g_knz[:], 0.0, w=["g_knz"])
    MSET("dve", g_wz[:], 0.0, w=["g_wz"])
    MSET("dve", g_qdz[:], 0.0, w=["g_qdz"])
    MSET("dve", hT[:, :, 0:1], 0.0, w=["hTpadL"])
    MSET("dve", hT[:, :, NT * P + 1:NT * P + 2], 0.0, w=["hTpadR"])

    def layer_setup(l):
        DMA("sp", rowp[:], rowp_d[l, 1024:RP_W].partition_broadcast(P), w=["rowp"])
        DMA("sp", ppp[:], pp_d[l], w=["ppp"])
        DMA("sp", wsT[:].rearrange("p h i -> p (h i)"), wsT_d[l], w=["wsT"])
        wv = win_d[l].rearrange("(kc p) n -> p kc n", p=P)
        wov = wout_d[l].rearrange("(kc p) n -> p kc n", p=P)
        i_ = 0
        for kc in range(KC):
            for c0 in range(0, IN_W, 964):
                nm = "m_s%d" % (i_ % 2)
                stg = wk(nm, (P, 1024))
                DMA("sp", stg[:, 0:964], wv[:, kc, c0:c0 + 964], w=[nm])
                CP("act" if i_ % 2 else "dve", wbuf[:, kc, c0:c0 + 964], stg[:, 0:964], r=[nm], w=[("wbuf", kc, c0)])
                i_ += 1
            nm = "m_s%d" % (i_ % 2)
            stg = wk(nm, (P, 1024))
            DMA("sp", stg[:], wov[:, kc, :], w=[nm])
            CP("act" if i_ % 2 else "dve", wout[:, kc, :], stg[:], r=[nm], w=[("wout", kc)])
            i_ += 1
        gp = wk("m_gp", (P, D))
        DMA("sp", gp[:], rowp_d[l, 0:1024].partition_broadcast(P), w=["m_gp"])
        wmv = wmod_d[l].rearrange("(kc p) n -> p kc n", p=P)
        psT, kT = nps()
        for nt in range(12):
            b = nt % 2
            wt = wk("m_w%d" % b, (P, KC, 256))
            bm = wk("m_b%d" % b, (4, 256))
            mr = wk("m_r%d" % b, (4, 256))
            DMA("sp", wt[:], wmv[:, :, nt * 256:(nt + 1) * 256], w=["m_w%d" % b])
            DMA("sp", bm[:], bmod_d[l, nt * 256:(nt + 1) * 256].partition_broadcast(4), w=["m_b%d" % b])
            ps, pk = nps()
            for kc in range(KC):
                MM(ps[0:4, 0:256], siluT[:, kc, :], wt[:, kc, :], start=(kc == 0), stop=(kc == KC - 1),
                   r=["siluT", "m_w%d" % b], w=[pk])
            TT("dve", mr[:], ps[0:4, 0:256], bm[:], ALU.add, r=[pk, "m_b%d" % b], w=["m_r%d" % b])
            g, j = nt // 4, nt % 4
            if g < 2:
                for q in range(2):
                    kc = 2 * j + q
                    TR(psT[:, (g * KC + kc) * 4:(g * KC + kc) * 4 + 4], mr[0:4, q * P:(q + 1) * P], ident[0:4, 0:4],
                       r=["m_r%d" % b, KI], w=[kT])
            else:
                for r_ in range(3):
                    ps2, pk2 = nps()
                    MM(ps2[:, 0:256], cst["sel"][0:4, r_ * P:(r_ + 1) * P], mr[0:4, :], r=["m_r%d" % b, ("c", "sel")], w=[pk2])
                    TT("dve", Gb[:, r_, j * 256:(j + 1) * 256], ps2[:, 0:256], gp[:, j * 256:(j + 1) * 256], ALU.mult,
                       r=[pk2, "m_gp"], w=["Gb"])
        CP("dve", shT[:].rearrange("p k r -> p (k r)"), psT[:, 0:32], r=[kT], w=["shT"])
        STT(AT[:], psT[:, 32:64].rearrange("p (k r) -> p k r", r=4), 1.0,
            ppp[:, PP_GPRE:PP_GPRE + 8].unsqueeze(2).to_broadcast([P, KC, 4]), ALU.add, ALU.mult,
            r=[kT, "ppp"], w=["AT"])
        for c in range(2):
            for hh in range(2):
                h = 2 * c + hh
                CP("dve", bsT[64 * hh:64 * hh + 64, c, :], rowp[64 * hh:64 * hh + 64, RP_SGB + h * P:RP_SGB + (h + 1) * P],
                   r=["rowp"], w=["bsT"])
        ACT(negA[:], rowp[:, RP_ALOG:RP_ALOG + 8], AF.Exp, r=["rowp"], w=["negA"])
        TS("dve", negA[:], negA[:], -1.0, None, ALU.mult, r=["negA"], w=["negA"])
        tap("AT%d" % l, AT[:].rearrange("p k r -> p (k r)"), [P, 32], ["AT"])
        tap("shT%d" % l, shT[:].rearrange("p k r -> p (k r)"), [P, 32], ["shT"])
        tap("Gb%d" % l, Gb[:].rearrange("p a n -> p (a n)"), [P, 3 * D], ["Gb"])

    def src_tile(li, s, t):
        if t < TC:
            base = ctx_d if li == 0 else cs_d
            return base[s, t * P:(t + 1) * P, :], ("dx", "c", s, t)
        tt = t - TC
        base = x_d if li == 0 else xs_d
        return base[s, tt * P:(tt + 1) * P, :], ("dx", "x", s, tt)

    def dst_tile(li, s, t):
        if t < TC:
            return cs_d[s, t * P:(t + 1) * P, :], ("dx", "c", s, t)
        tt = t - TC
        base = y_d if li == L - 1 else xs_d
        return base[s, tt * P:(tt + 1) * P, :], ("dx", "x", s, tt)

    xt_rr = [0]

    def load_x(li, s, t):
        b = xt_rr[0]
        xt_rr[0] ^= 1
        src, dk = src_tile(li, s, t)
        DMA("sp", xt[b][:], src, r=[dk], w=[("xt", b)])
        return xt[b], ("xt", b), st8[b], ("st8", b)

    def rstd_from(s8, sk, col_in, scale):
        ACT(s8[:, col_in + 1:col_in + 2], s8[:, col_in:col_in + 1], AF.Ln, bias=EPS, scale=scale, r=[sk], w=[sk])
        ACT(s8[:, col_in + 2:col_in + 3], s8[:, col_in + 1:col_in + 2], AF.Exp, scale=-0.5, r=[sk], w=[sk])

    def stage_h(li, s):
        for t in range(NT):
            r_ = 2 if t < TC else s
            xa, xk, s8, sk = load_x(li, s, t)
            ACT(wk("h_junk", (P, D))[:], xa[:], AF.Square, accum=s8[:, 0:1], r=[xk], w=["h_junk", sk])
            rstd_from(s8, sk, 0, 1.0 / D)
            xn = wk("h_xn", (P, D), BF16)
            TS("dve", xn[:], xa[:], s8[:, 2:3], None, ALU.mult, r=[xk, sk], w=["h_xn"])
            psx, kx = nps()
            pv = psx[:].bitcast(BF16)
            for kc in range(KC):
                TR(pv[:, kc * P:(kc + 1) * P], xn[:, kc * P:(kc + 1) * P], ident_b[:], r=["h_xn", "ident_b"], w=[kx])
            for kc in range(KC):
                o_ = hT[:, kc, 1 + t * P:1 + (t + 1) * P]
                i_ = pv[:, kc * P:(kc + 1) * P]
                if kc % 2:
                    ACT(o_, i_, AF.Identity, bias=shT[:, kc, r_:r_ + 1], scale=AT[:, kc, r_:r_ + 1],
                        r=[kx, "AT", "shT"], w=[("hT", t)])
                else:
                    TS("dve", o_, i_, AT[:, kc, r_:r_ + 1], shT[:, kc, r_:r_ + 1], ALU.mult, ALU.add,
                       r=[kx, "AT", "shT"], w=[("hT", t)])

    def proj_tok(t, c0, n, ps_ap, pk):
        for kc in range(KC):
            MM(ps_ap, hT[:, kc, 1 + t * P:1 + (t + 1) * P], wbuf[:, kc, c0:c0 + n],
               start=(kc == 0), stop=(kc == KC - 1), r=[("hT", t), "wbuf"], w=[pk])

    def hkeys(t):
        return [("hT", t), ("hT", t - 1) if t > 0 else "hTpadL", ("hT", t + 1) if t < NT - 1 else "hTpadR"]

    def proj_feat(t, c0, ps_ap, pk, halo):
        lo, n = (t * P, P + 2) if halo else (t * P + 1, P)
        for kc in range(KC):
            MM(ps_ap, wbuf[:, kc, c0:c0 + P], hT[:, kc, lo:lo + n],
               start=(kc == 0), stop=(kc == KC - 1), r=(hkeys(t) if halo else [("hT", t)]) + ["wbuf"], w=[pk])

    def silu_of(src, src_keys, name, shape, tile=None):
        e = tile if tile is not None else wk(name, shape)
        ACT(e[:], src, AF.Exp, scale=-1.0, r=src_keys, w=[name])
        ACT(e[:], e[:], AF.Ln, bias=1.0, r=[name], w=[name])
        ACT(e[:], e[:], AF.Exp, scale=-1.0, r=[name], w=[name])
        return e

    def gelu_of(src, src_keys, name, shape):
        xs = wk(name + "_x", shape)
        CP("act", xs[:], src, r=src_keys, w=[name + "_x"])
        a = wk(name + "_a", shape)
        ACT(a[:], xs[:], AF.Square, r=[name + "_x"], w=[name + "_a"])
        TS("dve", a[:], a[:], 0.044715, 1.0, ALU.mult, ALU.add, r=[name + "_a"], w=[name + "_a"])
        TT("pool", a[:], a[:], xs[:], ALU.mult, r=[name + "_a", name + "_x"], w=[name + "_a"])
        ACT(a[:], a[:], AF.Exp, scale=-1.5957691216057308, r=[name + "_a"], w=[name + "_a"])
        ACT(a[:], a[:], AF.Ln, bias=1.0, r=[name + "_a"], w=[name + "_a"])
        ACT(a[:], a[:], AF.Exp, scale=-1.0, r=[name + "_a"], w=[name + "_a"])
        TT("dve", xs[:], xs[:], a[:], ALU.mult, r=[name + "_a", name + "_x"], w=[name + "_x"])
        return xs, name + "_x"

    def ret_dir(d, t, first, pass_b, cur, need_out):
        S_c, S_n = Sret[cur], Sret[cur ^ 1]
        kS_c, kS_n = ("Sret", cur), ("Sret", cur ^ 1)
        ps1, k1 = nps()
        ps2, k2 = nps()
        proj_tok(t, 0, 512, ps1[:], k1)
        nv = 512 if pass_b else 256
        proj_tok(t, 512, nv, ps2[:, 0:nv], k2)
        C = wk("r_rc", (P, 64))
        Sg_ = wk("r_rs", (P, 64))
        DMA("sp", C[:], cds["rot_c"][:, t * 64:(t + 1) * 64], w=["r_rc"])
        DMA("sp", Sg_[:], cds["rot_s"][:, t * 64:(t + 1) * 64], w=["r_rs"])
        Sg = Sg_.rearrange("p (a s f) -> p a s f", a=2, s=2, f=16)
        t1 = wk("r_t1")
        t2 = wk("r_t2")
        qk = wk("r_qk")
        TT("dve", t1[:].rearrange("p (h d) -> p h d", d=64), ps1[:].rearrange("p (h d) -> p h d", d=64),
           C.unsqueeze(1).to_broadcast([P, 8, 64]), ALU.mult, r=[k1, "r_rc"], w=["r_t1"])
        x5 = ps1[:].rearrange("p (h a s f) -> p h a s f", a=2, s=2, f=16)
        t5 = t2[:].rearrange("p (h a s f) -> p h a s f", a=2, s=2, f=16)
        for s_ in range(2):
            TT("dve", t5[:, :, :, s_, :], x5[:, :, :, 1 - s_, :],
               Sg[:, :, s_, :].unsqueeze(1).to_broadcast([P, 8, 2, 16]), ALU.mult,
               r=[k1, "r_rs"], w=["r_t2"])
        TT("pool", t1[:], t1[:], t2[:], ALU.add, r=["r_t1", "r_t2"], w=["r_t1"])
        TT("dve", qk[:].rearrange("p (h d) -> p h d", d=64), t1[:].rearrange("p (h d) -> p h d", d=64),
           cst["ret_dec"][:, d, :].unsqueeze(2).to_broadcast([P, 8, 64]), ALU.mult,
           r=["r_t1", ("c", "ret_dec")], w=["r_qk"])
        v_sb = wk("r_v", (P, 256))
        CP("act", v_sb[:], ps2[:, 0:256], r=[k2], w=["r_v"])
        if "rstop1" in KDBG:
            return
        psq, kq = nps()
        for c in range(4):
            TR(psq[:, c * P:(c + 1) * P], qk[:, c * P:(c + 1) * P], ident[:], r=["r_qk", KI], w=[kq])
        if "rstopA" in KDBG:
            return
        kT = wk("r_kT", (P, 2, P))
        CP("act", kT[:], psq[:, 256:512].rearrange("p (c n) -> p c n", c=2), r=[kq], w=["r_kT"])
        if "rstopB" in KDBG:
            return
        for hh in range(2):
            pb = 64 * hh
            if "qz2d" in KDBG:
                for c in range(2):
                    CP("dve", r_qz[pb:pb + 64, 2 * c + hh, :], psq[pb:pb + 64, c * P:(c + 1) * P], r=[kq], w=["r_qz"])
            else:
                CP("act" if "qzact" in KDBG else "dve", r_qz[pb:pb + 64, hh::2, :],
                   psq[pb:pb + 64, 0:256].rearrange("p (c n) -> p c n", c=2), r=[kq], w=["r_qz"])
        if "rstop2" in KDBG:
            return
        pss, ks = nps()
        for h in range(NH):
            c = h // 2
            MM(pss[:, h * P:(h + 1) * P], kT[:, c, :], r_qz[:, h, :], r=["r_kT", "r_qz"], w=[ks])
        SmT = wk("r_sm")
        TT("dve", SmT[:], pss[:], cst["ret_mask"][:, d, :], ALU.mult, r=[ks, ("c", "ret_mask")], w=["r_sm"])
        if "rstop3" in KDBG:
            return
        if need_out:
            pso, ko = nps()
            for h in range(NH):
                c, pb = h // 2, 64 * (h % 2)
                MM(pso[:, h * 64:(h + 1) * 64], SmT[:, h * P:(h + 1) * P], v_sb[:, h * 64:(h + 1) * 64],
                   start=True, stop=first, r=["r_sm", "r_v"], w=[ko])
                if not first:
                    MM(pso[:, h * 64:(h + 1) * 64], r_qz[:, h, :], S_c[:, c, :],
                       start=False, stop=True, r=["r_qz", kS_c], w=[ko])
        if "rstop4" in KDBG:
            return
        psu, ku = nps()
        for c in range(2):
            MM(psu[:, c * P:(c + 1) * P], qk[:, 256 + c * P:256 + (c + 1) * P], v_sb[:, c * P:(c + 1) * P],
               r=["r_qk", "r_v"], w=[ku])
        for hh in range(2):
            pb = 64 * hh
            src = psu[pb:pb + 64, 0:256].rearrange("p (c x) -> p c x", c=2)[:, :, pb:pb + 64]
            if first:
                CP("dve", S_n[pb:pb + 64, :, :], src, r=[ku], w=[kS_n])
            else:
                tmp = wk("r_stmp", (P, 2, 64))
                TT("pool", tmp[pb:pb + 64, :, :], S_c[pb:pb + 64, :, :],
                   cst["ret_cdec_p"][pb:pb + 64, :].rearrange("p (c e) -> p c e", e=64), ALU.mult,
                   r=[kS_c, ("c", "ret_cdec_p")], w=["r_stmp"])
                TT("dve", S_n[pb:pb + 64, :, :], tmp[pb:pb + 64, :, :], src, ALU.add, r=["r_stmp", ku], w=[kS_n])
        if not need_out or "rstop5" in KDBG:
            return None
        if not pass_b:
            o_sb = wk("r_of", (P, 256))
            CP("act", o_sb[:], pso[:, 0:256], r=[ko], w=["r_of"])
            if "noof" not in KDBG:
                DMA("sp", of_d[t, :, 0:256], o_sb[:], r=["r_of"], w=[("of", t, 0)])
            tap("ret_of_%d" % t, o_sb[:], [P, 256], ["r_of"])
            return None
        ofl = wk("r_ofl", (P, 256))
        if "noof" not in KDBG:
            DMA("sp", ofl[:], of_d[t, :, 0:256], r=[("of", t, 0)], w=["r_ofl"])
        else:
            MSET("dve", ofl[:], 0.0, w=["r_ofl"])
        o = wk("r_o", (P, 256))
        TT("dve", o[:], pso[:, 0:256], ofl[:], ALU.add, r=[ko, "r_ofl"], w=["r_o"])
        tap("ret_o_%d" % t, o[:], [P, 256], ["r_o"])
        o3 = o[:].rearrange("p (h e) -> p h e", e=64)
        s8 = wk("r_s8", (P, 16))
        RSUM(s8[:, 0:4], o3, r=["r_o"], w=["r_s8"])
        TS("dve", s8[:, 0:4], s8[:, 0:4], 1.0 / 64, None, ALU.mult, r=["r_s8"], w=["r_s8"])
        TT("dve", o3, o3, s8[:, 0:4].unsqueeze(2).to_broadcast([P, 4, 64]), ALU.subtract, r=["r_o", "r_s8"], w=["r_o"])
        sq = wk("r_sq", (P, 256))
        ACT(sq[:], o[:], AF.Square, r=["r_o"], w=["r_sq"])
        RSUM(s8[:, 4:8], sq[:].rearrange("p (h e) -> p h e", e=64), r=["r_sq"], w=["r_s8"])
        ACT(s8[:, 8:12], s8[:, 4:8], AF.Ln, bias=EPS, scale=1.0 / 64, r=["r_s8"], w=["r_s8"])
        ACT(s8[:, 12:16], s8[:, 8:12], AF.Exp, scale=-0.5, r=["r_s8"], w=["r_s8"])
        TT("dve", o3, o3, s8[:, 12:16].unsqueeze(2).to_broadcast([P, 4, 64]), ALU.mult, r=["r_o", "r_s8"], w=["r_o"])
        TT("pool", o[:], o[:], rowp[:, RP_RETG:RP_RETG + 256], ALU.mult, r=["r_o", "rowp"], w=["r_o"])
        sg_ = silu_of(ps2[:, 256:512], [k2], "r_sz", (P, 256))
        TT("dve", sg_[:], sg_[:], ps2[:, 256:512], ALU.mult, r=["r_sz", k2], w=["r_sz"])
        yb = wk("r_yb", (P, 256), BF16)
        TT("dve", yb[:], o[:], sg_[:], ALU.mult, r=["r_o", "r_sz"], w=["r_yb"])
        for c in range(2):
            TR(psb[:, c * P:(c + 1) * P], yb[:, c * P:(c + 1) * P], ident_b[:], r=["r_yb", "ident_b"], w=["psb"])
        CP("act", yT[:, 0:2, :], psb[:, 0:256].rearrange("p (c n) -> p c n", n=P), r=["psb"], w=[("yT", 0)])
        return None

    def sg_tile(t):
        psu, ku = nps()
        for c in range(2):
            proj_feat(t, 1024 + c * P, psu[:, c * P:(c + 1) * P], ku, False)
        for c in range(2):
            proj_feat(t, 1536 + c * P, psu[:, (2 + c) * P:(3 + c) * P], ku, False)
        psv, kv = nps()
        proj_tok(t, 1280, 256, psv[:, 0:256], kv)
        gv, kgv = gelu_of(psv[:, 0:256], [kv], "s_gv", (P, 256))
        s8 = wk("s_s8", (P, 8))
        RSUM(s8[:, 0:1], gv[:], r=[kgv], w=["s_s8"])
        TS("dve", s8[:, 0:1], s8[:, 0:1], 1.0 / 256, None, ALU.mult, r=["s_s8"], w=["s_s8"])
        TS("dve", gv[:], gv[:], s8[:, 0:1], None, ALU.subtract, r=[kgv, "s_s8"], w=[kgv])
        ACT(wk("s_j", (P, 256))[:], gv[:], AF.Square, accum=s8[:, 1:2], r=[kgv], w=["s_j", "s_s8"])
        ACT(s8[:, 2:3], s8[:, 1:2], AF.Ln, bias=EPS, scale=1.0 / 256, r=["s_s8"], w=["s_s8"])
        ACT(s8[:, 3:4], s8[:, 2:3], AF.Exp, scale=-0.5, r=["s_s8"], w=["s_s8"])
        TS("dve", gv[:], gv[:], s8[:, 3:4], None, ALU.mult, r=[kgv, "s_s8"], w=[kgv])
        tap("sg_vn_%d" % t, gv[:], [P, 256], [kgv])
        pss, ks = nps()
        for h in range(NH):
            c = h // 2
            MM(pss[:, h * P:(h + 1) * P], gv[:, c * P:(c + 1) * P], wsT[:, h, :], r=[kgv, "wsT"], w=[ks])
        sT = wk("s_sT", (P, 2, P))
        for hh in range(2):
            pb = 64 * hh
            src = pss[pb:pb + 64, :].rearrange("p (c x i) -> p c x i", c=2, x=2)[:, :, hh, :]
            TT("dve", sT[pb:pb + 64, :, :], src, bsT[pb:pb + 64, :, :], ALU.add, r=[ks, "bsT"], w=["s_sT"])
        gu, kgu = gelu_of(psu[:, 0:256], [ku], "s_gu", (P, 256))
        sz = silu_of(psu[:, 256:512], [ku], "s_sz", (P, 256))
        TT("dve", sz[:], sz[:], psu[:, 256:512], ALU.mult, r=["s_sz", ku], w=["s_sz"])
        TT("pool", gu[:], gu[:], sT[:].rearrange("p c i -> p (c i)"), ALU.mult, r=[kgu, "s_sT"], w=[kgu])
        TT("dve", yT[:, 2:4, :], gu[:].rearrange("p (c i) -> p c i", c=2), sz[:].rearrange("p (c i) -> p c i", c=2),
           ALU.mult, r=[kgu, "s_sz"], w=[("yT", 1)])

    def sc_tile(t, seg_first, seg_last):
        W = P + 2
        psA, kA = nps()
        psB, kB = nps()
        psC, kC = nps()
        for c in range(2):
            proj_feat(t, 2048 + c * P, psA[:, c * W:(c + 1) * W], kA, True)
        proj_feat(t, 2304, psA[:, 2 * W:3 * W], kA, True)
        proj_feat(t, 2304 + P, psB[:, 0:W], kB, True)
        for c in range(2):
            proj_feat(t, 1792 + c * P, psB[:, W + c * P:W + (c + 1) * P], kB, False)
            proj_feat(t, 2560 + c * P, psC[:, c * P:(c + 1) * P], kC, False)
        hs = wk("c_h", (P, 2, W))
        CP("act", hs[:, 0, :], psA[:, 2 * W:3 * W], r=[kA], w=["c_h"])
        CP("act", hs[:, 1, :], psB[:, 0:W], r=[kB], w=["c_h"])
        ch = wk("c_ch", (P, 2, W))
        TT("dve", ch[:], psA[:, 0:2 * W].rearrange("p (c n) -> p c n", c=2), hs[:], ALU.mult, r=[kA, "c_h"], w=["c_ch"])
        if seg_first:
            MSET("dve", ch[:, :, 0:1], 0.0, w=["c_ch"])
        if seg_last:
            MSET("dve", ch[:, :, W - 1:W], 0.0, w=["c_ch"])
        cv = wk("c_cv", (P, 2, P))
        for c in range(2):
            w_ = lambda k: ppp[:, PP_SCW + c * 3 + k:PP_SCW + c * 3 + k + 1]
            TS("dve", cv[:, c, :], ch[:, c, 0:P], w_(0), None, ALU.mult, r=["c_ch", "ppp"], w=["c_cv"])
            STT(cv[:, c, :], ch[:, c, 1:P + 1], w_(1), cv[:, c, :], ALU.mult, ALU.add, r=["c_ch", "ppp", "c_cv"], w=["c_cv"])
            STT(cv[:, c, :], ch[:, c, 2:P + 2], w_(2), cv[:, c, :], ALU.mult, ALU.add, r=["c_ch", "ppp", "c_cv"], w=["c_cv"])
        tap("sc_cv_%d" % t, cv[:].rearrange("p c n -> p (c n)"), [P, 256], ["c_cv"])
        sz = silu_of(psC[:, 0:256], [kC], "c_sz", (P, 256))
        TT("dve", sz[:], sz[:], psC[:, 0:256], ALU.mult, r=["c_sz", kC], w=["c_sz"])
        TT("dve", cv[:].rearrange("p c n -> p (c n)"), cv[:].rearrange("p c n -> p (c n)"), psB[:, W:W + 256], ALU.mult,
           r=["c_cv", kB], w=["c_cv"])
        TT("dve", yT[:, 4:6, :], cv[:], sz[:].rearrange("p (c i) -> p c i", c=2), ALU.mult, r=["c_cv", "c_sz"],
           w=[("yT", 2)])

    def gdn_dir(d, t, first, pass_b, cur, need_out, seg_first, seg_last):
        W = P + 2
        S_c, S_n = Sgdn[cur], Sgdn[cur ^ 1]
        kS_c, kS_n = ("Sgdn", cur), ("Sgdn", cur ^ 1)
        psg, kg = nps()
        proj_tok(t, 3840, 16, psg[:, 0:16], kg)
        if not pass_b:
            qkv = wk("g_qkv", (P, 6, W))
            for b in range(2):
                ps, pk = nps()
                for c in range(3):
                    proj_feat(t, 2816 + (3 * b + c) * P, ps[:, c * W:(c + 1) * W], pk, True)
                CP("act", qkv[:, 3 * b:3 * b + 3, :], ps[:, 0:3 * W].rearrange("p (c n) -> p c n", c=3), r=[pk], w=["g_qkv"])
            if seg_first:
                MSET("dve", qkv[:, :, 0:1], 0.0, w=["g_qkv"])
            if seg_last:
                MSET("dve", qkv[:, :, W - 1:W], 0.0, w=["g_qkv"])
        else:
            cv = wk("g_cv", (P, 6, P))
            qkn = wk("g_qkn", (P, 4, P))
            DMA("sp", qkn[:].rearrange("p c n -> p (c n)"), gq_d[t, :, 0:512], r=[("gq", t, 0)], w=["g_qkn"])
            DMA("sp", cv[:, 4:6, :], gq_d[t, :, 512:768].rearrange("p (c n) -> p c n", c=2), r=[("gq", t, 1)], w=["g_cv"])
        g8 = wk("g_g8", (P, 48))
        K8 = "g_g8"
        a_ps = psg[:, 4 * d:4 * d + 4]
        b_ps = psg[:, 8 + 4 * d:12 + 4 * d]
        TT("dve", g8[:, 0:4], a_ps, rowp[:, RP_DTB + 4 * d:RP_DTB + 4 * d + 4], ALU.add, r=[kg, "rowp"], w=[K8])
        ACT(g8[:, 0:4], g8[:, 0:4], AF.Exp, r=[K8], w=[K8])
        ACT(g8[:, 0:4], g8[:, 0:4], AF.Ln, bias=1.0, r=[K8], w=[K8])
        TT("dve", g8[:, 0:4], g8[:, 0:4], negA[:, 4 * d:4 * d + 4], ALU.mult, r=[K8, "negA"], w=[K8])
        ACT(g8[:, 4:8], b_ps, AF.Exp, scale=-1.0, r=[kg], w=[K8])
        ACT(g8[:, 4:8], g8[:, 4:8], AF.Ln, bias=1.0, r=[K8], w=[K8])
        TS("dve", g8[:, 4:8], g8[:, 4:8], -1.0, None, ALU.mult, r=[K8], w=[K8])
        ACT(g8[:, 8:12], g8[:, 4:8], AF.Exp, r=[K8], w=[K8])
        psc, kc_ = nps()
        MM(psc[:, 0:4], cst["tri_b" if d else "tri_f"][:], g8[:, 0:4], r=[K8, ("c", "tri_b" if d else "tri_f")], w=[kc_])
        MM(psc[:, 4:8], ones[:], g8[:, 0:4], r=[K8, ("c", "ones")], w=[kc_])
        CP("dve", g8[:, 12:16], psc[:, 0:4], r=[kc_], w=[K8])
        TT("dve", g8[:, 16:20], g8[:, 4:8], g8[:, 12:16], ALU.add, r=[K8], w=[K8])
        ACT(g8[:, 20:24], g8[:, 16:20], AF.Exp, r=[K8], w=[K8])
        TT("dve", g8[:, 24:28], psc[:, 4:8], g8[:, 12:16], ALU.subtract, r=[kc_, K8], w=[K8])
        ACT(g8[:, 24:28], g8[:, 24:28], AF.Exp, r=[K8], w=[K8])
        TS("dve", g8[:, 32:36], g8[:, 12:16], -1.0, None, ALU.mult, r=[K8], w=[K8])
        cdP = wk("g_cdP", (P, 2))
        for hh in range(2):
            pb = 64 * hh
            ACT(cdP[pb:pb + 64, 0:2], psc[pb:pb + 64, 4 + hh:8:2], AF.Exp, r=[kc_], w=["g_cdP"])
        tap("gdn_g8_%d_%d" % (d, t), g8[:], [P, 48], [K8])
        dg = wk("g_dg")
        TT("dve", dg[:].rearrange("p (h i) -> p h i", h=4), ident[:].unsqueeze(1).to_broadcast([P, 4, P]),
           g8[:, 12:16].unsqueeze(2).to_broadcast([P, 4, P]), ALU.mult, r=[KI, K8], w=["g_dg"])
        psA, kA = nps()
        MM(psA[:], ones[:], dg[:], start=True, stop=False, r=["g_dg", ("c", "ones")], w=[kA])
        MM(psA[:], ident[:], cst["gdn_mask"][:, 2 * d, :], start=False, stop=True, r=[KI, ("c", "gdn_mask")], w=[kA])
        Nm = wk("g_N")
        NmT = wk("g_NT")
        X = wk("g_X")
        kX = "g_X"
        for h in range(NH):
            ACT(Nm[:, h * P:(h + 1) * P], psA[:, h * P:(h + 1) * P], AF.Exp, bias=g8[:, 16 + h:17 + h], scale=-1.0,
                r=[kA, K8], w=["g_N"])
        if not pass_b:
            cv = wk("g_cv", (P, 6, P))
            for c in range(6):
                w_ = lambda k: ppp[:, PP_GDW + c * 3 + k:PP_GDW + c * 3 + k + 1]
                TS("dve", cv[:, c, :], qkv[:, c, 0:P], w_(0), None, ALU.mult, r=["g_qkv", "ppp"], w=["g_cv"])
                STT(cv[:, c, :], qkv[:, c, 1:P + 1], w_(1), cv[:, c, :], ALU.mult, ALU.add, r=["g_qkv", "ppp", "g_cv"], w=["g_cv"])
                STT(cv[:, c, :], qkv[:, c, 2:P + 2], w_(2), cv[:, c, :], ALU.mult, ALU.add, r=["g_qkv", "ppp", "g_cv"], w=["g_cv"])
            cvf = cv[:].rearrange("p c n -> p (c n)")
            sg_ = silu_of(cvf, ["g_cv"], "g_qkv", None, tile=qkv.rearrange("p c n -> p (c n)")[:, 0:6 * P])
            TT("dve", cvf, cvf, sg_[:], ALU.mult, r=["g_cv", "g_qkv"], w=["g_cv"])
            sq = wk("g_sq")
            ACT(sq[:], cv[:, 0:4, :].rearrange("p c n -> p (c n)"), AF.Square, r=["g_cv"], w=["g_sq"])
            psn, kn_ = nps()
            for c in range(4):
                MM(psn[:, c * P:(c + 1) * P], cst["bones"][:], sq[:, c * P:(c + 1) * P], r=["g_sq", ("c", "bones")], w=[kn_])
            rn = sq
            ACT(rn[:], psn[:], AF.Ln, bias=EPS, r=[kn_], w=["g_sq"])
            ACT(rn[:], rn[:], AF.Exp, scale=-0.5, r=["g_sq"], w=["g_sq"])
            qkn = wk("g_qkn", (P, 4, P))
            TT("dve", qkn[:].rearrange("p c n -> p (c n)"), cv[:, 0:4, :].rearrange("p c n -> p (c n)"), rn[:], ALU.mult,
               r=["g_cv", "g_sq"], w=["g_qkn"])
            DMA("sp", gq_d[t, :, 0:512], qkn[:].rearrange("p c n -> p (c n)"), r=["g_qkn"], w=[("gq", t, 0)])
            DMA("sp", gq_d[t, :, 512:768].rearrange("p (c n) -> p c n", c=2), cv[:, 4:6, :], r=["g_cv"], w=[("gq", t, 1)])
        tap("gdn_qkn_%d_%d" % (d, t), qkn[:].rearrange("p c n -> p (c n)"), [P, 512], ["g_qkn"])
        for hh in range(2):
            pb = 64 * hh
            CP("pool", g_knz[pb:pb + 64, hh::2, :], qkn[pb:pb + 64, 2:4, :], r=["g_qkn"], w=["g_knz"])
        pst, kt = nps()
        for c in range(2):
            TR(pst[:, c * P:(c + 1) * P], qkn[:, 2 + c, :], ident[:], r=["g_qkn", KI], w=[kt])
            TR(pst[:, 256 + c * P:256 + (c + 1) * P], cv[:, 4 + c, :], ident[:], r=["g_cv", KI], w=[kt])
        vb = wk("g_vb", (P, 256))
        kbg = wk("g_kbg", (P, 256))
        ktl = wk("g_ktl", (P, 256))
        kn3 = pst[:, 0:256].rearrange("p (h e) -> p h e", e=64)
        TT("dve", vb[:].rearrange("p (h e) -> p h e", e=64), pst[:, 256:512].rearrange("p (h e) -> p h e", e=64),
           g8[:, 8:12].unsqueeze(2).to_broadcast([P, 4, 64]), ALU.mult, r=[kt, K8], w=["g_vb"])
        TT("dve", kbg[:].rearrange("p (h e) -> p h e", e=64), kn3, g8[:, 20:24].unsqueeze(2).to_broadcast([P, 4, 64]),
           ALU.mult, r=[kt, K8], w=["g_kbg"])
        TT("dve", ktl[:].rearrange("p (h e) -> p h e", e=64), kn3, g8[:, 24:28].unsqueeze(2).to_broadcast([P, 4, 64]),
           ALU.mult, r=[kt, K8], w=["g_ktl"])
        psG, kG = nps()
        for h in range(NH):
            c, pb = h // 2, 64 * (h % 2)
            MM(psG[:, h * P:(h + 1) * P], g_knz[:, h, :], qkn[:, 2 + c, :], r=["g_qkn", "g_knz"], w=[kG])
        STT(Nm[:], psG[:], -1.0, Nm[:], ALU.mult, ALU.mult, r=[kG, "g_N"], w=["g_N"])
        tap("gdn_N_%d_%d" % (d, t), Nm[:], [P, 512], ["g_N"])
        psT, kT_ = nps()
        for h in range(NH):
            TR(psT[:, h * P:(h + 1) * P], Nm[:, h * P:(h + 1) * P], ident[:], r=["g_N", KI], w=[kT_])
        CP("act", NmT[:], psT[:], r=[kT_], w=["g_NT"])
        TT("dve", X[:].rearrange("p (h i) -> p h i", h=4), psT[:].rearrange("p (h i) -> p h i", h=4),
           ident[:].unsqueeze(1).to_broadcast([P, 4, P]), ALU.add, r=[kT_, KI], w=["g_X"])
        side = []
        if need_out:
            EG = wk("g_EG", (P, 2, P))
            PT = wk("g_PT")

            def side1():
                psR, kR = nps()
                MM(psR[:], ones[:], dg[:], r=["g_dg", ("c", "ones")], w=[kR])
                for h in range(NH):
                    c, pb = h // 2, 64 * (h % 2)
                    ACT(EG[pb:pb + 64, c, :], psR[pb:pb + 64, h * P:(h + 1) * P], AF.Exp, r=[kR], w=["g_EG"])
                for hh in range(2):
                    pb = 64 * hh
                    STT(g_qdz[pb:pb + 64, hh::2, :], qkn[pb:pb + 64, 0:2, :], 0.125, EG[pb:pb + 64, :, :], ALU.mult, ALU.mult,
                        r=["g_qkn", "g_EG"], w=["g_qdz"])

            def side2():
                psP, kP = nps()
                MM(psP[:], ones[:], dg[:], start=True, stop=False, r=["g_dg", ("c", "ones")], w=[kP])
                MM(psP[:], ident[:], cst["gdn_mask"][:, 2 * d + 1, :], start=False, stop=True, r=[KI, ("c", "gdn_mask")], w=[kP])
                for h in range(NH):
                    ACT(PT[:, h * P:(h + 1) * P], psP[:, h * P:(h + 1) * P], AF.Exp, bias=g8[:, 32 + h:33 + h],
                        r=[kP, K8], w=["g_PT"])

            def side3():
                psq, kq = nps()
                for h in range(NH):
                    c, pb = h // 2, 64 * (h % 2)
                    MM(psq[:, h * P:(h + 1) * P], g_knz[:, h, :], qkn[:, c, :], r=["g_qkn", "g_knz"], w=[kq])
                STT(PT[:], psq[:], 0.125, PT[:], ALU.mult, ALU.mult, r=[kq, "g_PT"], w=["g_PT"])
            side = [side1, side2, side3]
        for k in range(1, 7):
            ps1_, k1_ = nps()
            for h in range(NH):
                sl = slice(h * P, (h + 1) * P)
                MM(ps1_[:, sl], NmT[:, sl], Nm[:, sl], r=["g_N", "g_NT"], w=[k1_])
            if k < 6:
                ps2_, k2_ = nps()
                for h in range(NH):
                    sl = slice(h * P, (h + 1) * P)
                    MM(ps2_[:, sl], Nm[:, sl], NmT[:, sl], r=["g_N", "g_NT"], w=[k2_])
            CP("act", Nm[:], ps1_[:], r=[k1_], w=["g_N"])
            if k < 6:
                CP("dve", NmT[:], ps2_[:], r=[k2_], w=["g_NT"])
            if side and k >= 2:
                side.pop(0)()
            ps3_, k3_ = nps()
            for h in range(NH):
                sl = slice(h * P, (h + 1) * P)
                MM(ps3_[:, sl], Nm[:, sl], X[:, sl], r=["g_N", "g_X"], w=[k3_])
            TT("dve", X[:], X[:], ps3_[:], ALU.add, r=["g_X", k3_], w=["g_X"])
        while side:
            side.pop(0)()
        tap("gdn_XT_%d_%d" % (d, t), X[:], [P, 512], [kX])
        psu, ku = nps()
        for h in range(NH):
            MM(psu[:, h * 64:(h + 1) * 64], X[:, h * P:(h + 1) * P], vb[:, h * 64:(h + 1) * 64], r=[kX, "g_vb"], w=[ku])
        u_sb = wk("g_u", (P, 256))
        CP("act", u_sb[:], psu[:, 0:256], r=[ku], w=["g_u"])
        vnew = u_sb
        if not first:
            psw, kw = nps()
            for h in range(NH):
                c = h // 2
                MM(psw[:, h * P:(h + 1) * P], kbg[:, c * P:(c + 1) * P], X[:, h * P:(h + 1) * P], r=[kX, "g_kbg"], w=[kw])
            for hh in range(2):
                pb = 64 * hh
                src = psw[pb:pb + 64, :].rearrange("p (c x i) -> p c x i", c=2, x=2)[:, :, hh, :]
                CP("act", g_wz[pb:pb + 64, hh::2, :], src, r=[kw], w=["g_wz"])
            pws, kws = nps()
            for h in range(NH):
                c = h // 2
                MM(pws[:, h * 64:(h + 1) * 64], g_wz[:, h, :], S_c[:, c, :], r=["g_wz", kS_c], w=[kws])
            TT("dve", u_sb[:], u_sb[:], pws[:, 0:256], ALU.subtract, r=["g_u", kws], w=["g_u"])
        tap("gdn_vnew_%d_%d" % (d, t), vnew[:], [P, 256], ["g_u"])
        if need_out:
            pso, ko = nps()
            for h in range(NH):
                c, pb = h // 2, 64 * (h % 2)
                MM(pso[:, h * 64:(h + 1) * 64], PT[:, h * P:(h + 1) * P], vnew[:, h * 64:(h + 1) * 64],
                   start=True, stop=first, r=["g_PT", "g_u"], w=[ko])
                if not first:
                    MM(pso[:, h * 64:(h + 1) * 64], g_qdz[:, h, :], S_c[:, c, :],
                       start=False, stop=True, r=["g_qdz", kS_c], w=[ko])
        pss, ks = nps()
        for c in range(2):
            MM(pss[:, c * P:(c + 1) * P], ktl[:, c * P:(c + 1) * P], vnew[:, c * P:(c + 1) * P], r=["g_ktl", "g_u"], w=[ks])
        for hh in range(2):
            pb = 64 * hh
            src = pss[pb:pb + 64, 0:256].rearrange("p (c x) -> p c x", c=2)[:, :, pb:pb + 64]
            if first:
                CP("dve", S_n[pb:pb + 64, :, :], src, r=[ks], w=[kS_n])
            else:
                tmp = wk("g_stmp", (P, 2, 64))
                TT("dve", tmp[pb:pb + 64, :, :], S_c[pb:pb + 64, :, :],
                   cdP[pb:pb + 64, 0:2].unsqueeze(2).to_broadcast([64, 2, 64]), ALU.mult, r=[kS_c, "g_cdP"], w=["g_stmp"])
                TT("dve", S_n[pb:pb + 64, :, :], tmp[pb:pb + 64, :, :], src, ALU.add, r=["g_stmp", ks], w=[kS_n])
        if not need_out:
            return
        if not pass_b:
            o_sb = wk("g_of", (P, 256))
            CP("act", o_sb[:], pso[:, 0:256], r=[ko], w=["g_of"])
            DMA("sp", of_d[t, :, 256:512], o_sb[:], r=["g_of"], w=[("of", t, 1)])
            tap("gdn_of_%d" % t, o_sb[:], [P, 256], ["g_of"])
            return
        psz, kz = nps()
        proj_tok(t, 3584, 256, psz[:, 0:256], kz)
        ofl = wk("g_ofl", (P, 256))
        DMA("sp", ofl[:], of_d[t, :, 256:512], r=[("of", t, 1)], w=["g_ofl"])
        o = wk("g_o", (P, 256))
        TT("dve", o[:], pso[:, 0:256], ofl[:], ALU.add, r=[ko, "g_ofl"], w=["g_o"])
        tap("gdn_o_%d" % t, o[:], [P, 256], ["g_o"])
        o3 = o[:].rearrange("p (h e) -> p h e", e=64)
        s8 = wk("g_s8", (P, 16))
        sq2 = wk("g_sq2", (P, 256))
        ACT(sq2[:], o[:], AF.Square, r=["g_o"], w=["g_sq2"])
        RSUM(s8[:, 0:4], sq2[:].rearrange("p (h e) -> p h e", e=64), r=["g_sq2"], w=["g_s8"])
        ACT(s8[:, 4:8], s8[:, 0:4], AF.Ln, bias=EPS, scale=1.0 / 64, r=["g_s8"], w=["g_s8"])
        ACT(s8[:, 8:12], s8[:, 4:8], AF.Exp, scale=-0.5, r=["g_s8"], w=["g_s8"])
        TT("dve", o3, o3, s8[:, 8:12].unsqueeze(2).to_broadcast([P, 4, 64]), ALU.mult, r=["g_o", "g_s8"], w=["g_o"])
        TT("pool", o3, o3, rowp[:, RP_GDNG:RP_GDNG + 64].unsqueeze(1).to_broadcast([P, 4, 64]), ALU.mult,
           r=["g_o", "rowp"], w=["g_o"])
        sz = silu_of(psz[:, 0:256], [kz], "g_sz", (P, 256))
        TT("dve", sz[:], sz[:], psz[:, 0:256], ALU.mult, r=["g_sz", kz], w=["g_sz"])
        yb = wk("g_yb", (P, 256), BF16)
        TT("dve", yb[:], o[:], sz[:], ALU.mult, r=["g_o", "g_sz"], w=["g_yb"])
        for c in range(2):
            TR(psb[:, 512 + c * P:512 + (c + 1) * P], yb[:, c * P:(c + 1) * P], ident_b[:], r=["g_yb", "ident_b"], w=["psb"])
        CP("act", yT[:, 6:8, :], psb[:, 512:768].rearrange("p (c n) -> p c n", n=P), r=["psb"], w=[("yT", 3)])

    def outproj(li, s, t):
        r_ = 2 if t < TC else s
        xa, xk, s8, sk = load_x(li, s, t)
        pso = []
        for nt in range(2):
            ps, pk = nps()
            for kc in range(KC):
                MM(ps[:], yT[:, kc, :], wout[:, kc, nt * 512:(nt + 1) * 512], start=(kc == 0), stop=(kc == KC - 1),
                   r=[("yT", 0), ("yT", 1), ("yT", 2), ("yT", 3), "wout"], w=[pk])
            pso.append((ps, pk))
            ACT(wk("o_junk", (P, D))[:, nt * 512:(nt + 1) * 512], ps[:], AF.Square, accum=s8[:, 4 + nt:5 + nt], r=[pk], w=["o_junk", sk])
        TT("dve", s8[:, 0:1], s8[:, 4:5], s8[:, 5:6], ALU.add, r=[sk], w=[sk])
        rstd_from(s8, sk, 0, 1.0 / D)
        on = wk("o_on", (P, D))
        for nt in range(2):
            ps, pk = pso[nt]
            STT(on[:, nt * 512:(nt + 1) * 512], ps[:], s8[:, 2:3], Gb[:, r_, nt * 512:(nt + 1) * 512], ALU.mult, ALU.mult,
                r=[pk, sk, "Gb"], w=["o_on"])
        TT("pool", on[:], on[:], xa[:], ALU.add, r=["o_on", xk], w=["o_on"])
        dst, dk = dst_tile(li, s, t)
        DMA("sp", dst, on[:], r=["o_on"], w=[dk])

    for li, l in enumerate(layers):
        last = (l == DEPTH - 1)
        layer_setup(l)
        for s in range(n_seq):
            stage_h(li, s)
            orderF = list(range(NT))
            orderB = list(range(TC - 1, -1, -1)) + list(range(NT - 1, TC - 1, -1))
            segf = lambda t: t == 0 or t == TC
            segl = lambda t: t == TC - 1 or t == NT - 1
            if s == 0:
                tap("hT_%d" % li, hT[:, :, 1:NT * P + 1], [P, KC, NT * P], [("hT", t) for t in range(NT)], BF16)
            if "F" in passes:
                curR = curG = 0
                for n, t in enumerate(orderF):
                    need = not (last and t < TC)
                    if "ret" in branches:
                        ret_dir(0, t, n == 0, False, curR, need)
                        curR ^= 1
                    if "gdn" in branches:
                        gdn_dir(0, t, n == 0, False, curG, need, segf(t), segl(t))
                        curG ^= 1
            if "B" in passes:
                curR = curG = 0
                for n, t in enumerate(orderB):
                    need = not (last and t < TC)
                    if "ret" in branches:
                        ret_dir(1, t, n == 0, True, curR, need)
                        curR ^= 1
                    if "gdn" in branches:
                        gdn_dir(1, t, n == 0, True, curG, need, segf(t), segl(t))
                        curG ^= 1
                    if need:
                        if "sg" in branches:
                            sg_tile(t)
                        if "sc" in branches:
                            sc_tile(t, segf(t), segl(t))
                        tap("yT_%d_%d_%d" % (li, s, t), yT[:].rearrange("p k n -> p (k n)"), [P, KC * P],
                            [("yT", 0), ("yT", 1), ("yT", 2), ("yT", 3)], BF16)
                        if do_out:
                            outproj(li, s, t)
    if ps_last is None:
        st.close()
        return dict(pg.key_last)
    pg.emit()
    st.close()
    build.last_counts = {e: len(v) for e, v in pg.ops.items()}
    return nc, tap_out


def make_in_maps(inp, TC, TL, n_seq, n_cores):
    consts = _host_consts(TC, TL)
    rowp, pp, wsT = _host_layer_params(inp)
    shared = {
        "w_mod": np.ascontiguousarray(inp["w_mod"], np.float32),
        "b_mod": np.ascontiguousarray(inp["b_mod"], np.float32),
        "w_in": np.ascontiguousarray(inp["w_in"], np.float32),
        "w_out": np.ascontiguousarray(inp["w_out"], np.float32),
        "rowp": rowp, "pp": pp, "wsT": wsT,
    }
    for k, v in consts.items():
        shared["c_" + k] = np.ascontiguousarray(v, np.float32)
    maps = []
    for c in range(n_cores):
        b0 = c * n_seq
        cc = np.zeros((4, D), np.float32)
        cc[0:n_seq] = inp["c"][b0:b0 + n_seq]
        cc[2] = inp["c_ctx"]
        cT = np.ascontiguousarray(cc.reshape(4, KC, P).transpose(2, 1, 0)).reshape(P, KC * 4)
        m = dict(shared)
        m["x"] = np.ascontiguousarray(inp["x"][b0:b0 + n_seq], np.float32)
        m["ctx"] = np.ascontiguousarray(inp["ctx"][b0:b0 + n_seq], np.float32)
        m["cT"] = cT
        maps.append(m)
    return maps


_NC_CACHE = {}


N_LAUNCH = 1


def kernel(**inputs):
    inp = {k: np.asarray(v) for k, v in inputs.items()}
    B, T, _ = inp["x"].shape
    TL = T // P
    TC = inp["ctx"].shape[1] // P
    n_seq = B // N_CORES
    maps = make_in_maps(inp, TC, TL, n_seq, N_CORES)
    per = DEPTH // N_LAUNCH
    for g in range(N_LAUNCH):
        layers = list(range(g * per, (g + 1) * per))
        key = (TC, TL, n_seq, tuple(layers))
        if key not in _NC_CACHE:
            _NC_CACHE[key] = build(TC, TL, layers, n_seq=n_seq, ctx_ext=(N_LAUNCH > 1))[0]
        res = run_bass_kernel_spmd(_NC_CACHE[key], maps, core_ids=list(range(N_CORES)))
        if g < N_LAUNCH - 1:
            for c in range(N_CORES):
                maps[c]["x"] = np.asarray(res.results[c]["y"])
                maps[c]["ctx"] = np.asarray(res.results[c]["cs_scr"])
    out = np.concatenate([np.asarray(r["y"]) for r in res.results], axis=0)
    return out.astype(np.float32)
```
